# Optimizing a Trainium2 kernel written in Bass

```python
import math
import jax
import jax.numpy as jnp
from jax import lax
import numpy as np

D_MODEL = 1024
BATCH = 4
SEQ = 4096
DEPTH = 4

N_MIXERS = 4
GRID_W = 64
EPS = 1e-6
NEG_BIG = -1e30

SSD_DI = 2 * D_MODEL
SSD_HEADDIM = 64
SSD_HEADS = SSD_DI // SSD_HEADDIM
SSD_GROUPS = 4
SSD_HPG = SSD_HEADS // SSD_GROUPS
SSD_STATE = 128
SSD_CONV = 7
SSD_CHUNK = 128
SSD_CONV_CH = SSD_DI + 2 * SSD_GROUPS * SSD_STATE
SSD_IN = SSD_DI + SSD_CONV_CH + 2 * SSD_HEADS

HG_EXPAND = 128
HG_HEADS = D_MODEL // HG_EXPAND
HG_W = HG_HEADS * HG_EXPAND
HG_DV = HG_W // HG_HEADS
HG_CHUNK = 32
HG_IN = 5 * HG_W

AT_HEADS = 16
AT_KV = 8
AT_GRP = AT_HEADS // AT_KV
AT_HD = 128
AT_QBLK = 128
ROPE_THETA = 10000.0
ROPE_AXIS = AT_HD // 2
AT_QW = AT_HEADS * AT_HD
AT_KW = AT_KV * AT_HD
AT_IN = 2 * AT_QW + 2 * AT_KW

DL_PAIRS = ((128, 1), (512, 4), (2048, 16))
DL_HEADS = 16
DL_HD = 64
DL_W = DL_HEADS * DL_HD
DL_IN = 3 * len(DL_PAIRS) * DL_W + DL_W

REL_BUCKETS = 32
REL_MAX_DIST = 1024

N_SSD_LAYERS = (DEPTH - 0 + N_MIXERS - 1) // N_MIXERS
N_HG_LAYERS = (DEPTH - 1 + N_MIXERS - 1) // N_MIXERS
N_AT_LAYERS = (DEPTH - 2 + N_MIXERS - 1) // N_MIXERS
N_DL_LAYERS = (DEPTH - 3 + N_MIXERS - 1) // N_MIXERS

kernel_name = "hybrid_bidir_ssd_hgrn2_axialgqa_dilated"

F32 = jnp.float32


def rms_norm(x, g):
    xf = x.astype(F32)
    y = xf * lax.rsqrt(jnp.mean(xf * xf, axis=-1, keepdims=True) + EPS)
    return (y * g.astype(F32)).astype(x.dtype)


def centred_depthwise_conv(x, w, b):
    C = x.shape[-1]
    pad = w.shape[0] // 2
    y = lax.conv_general_dilated(x, w[:, None, :].astype(x.dtype), window_strides=(1,),
                                 padding=((pad, pad),), dimension_numbers=('NWC', 'WIO', 'NWC'),
                                 feature_group_count=C)
    return y + b.astype(x.dtype)


def exp_segsum(a):
    T = a.shape[-1]
    cs = jnp.cumsum(a, axis=-1)
    diff = cs[..., :, None] - cs[..., None, :]
    mask = jnp.tril(jnp.ones((T, T), dtype=bool))
    return jnp.exp(jnp.where(mask, diff, -jnp.inf))


def ssd_chunked(xdt, dA, Bm, Cm):
    b, S, G, J, P = xdt.shape
    N = Bm.shape[-1]
    Q = SSD_CHUNK
    nc = S // Q
    x = xdt.reshape(b, nc, Q, G, J, P)
    a = dA.reshape(b, nc, Q, G, J).transpose(0, 3, 4, 1, 2)
    Bc = Bm.reshape(b, nc, Q, G, N)
    Cc = Cm.reshape(b, nc, Q, G, N)
    a_cum = jnp.cumsum(a, axis=-1)
    L = exp_segsum(a)
    CB = jnp.einsum('bclgn,bcsgn->bcgls', Cc, Bc)
    y_diag = jnp.einsum('bcgls,bgjcls,bcsgjp->bclgjp', CB, L, x)
    decay_states = jnp.exp(a_cum[..., -1:] - a_cum)
    states = jnp.einsum('bclgn,bgjcl,bclgjp->bcgjpn', Bc, decay_states, x)
    states = jnp.concatenate([jnp.zeros_like(states[:, :1]), states], axis=1)
    chunk_decay = exp_segsum(jnp.pad(a_cum[..., -1], ((0, 0), (0, 0), (0, 0), (1, 0))))
    start_states = jnp.einsum('bgjzc,bcgjpn->bzgjpn', chunk_decay, states)[:, :-1]
    y_off = jnp.einsum('bclgn,bcgjpn,bgjcl->bclgjp', Cc, start_states, jnp.exp(a_cum))
    return (y_diag + y_off).reshape(b, S, G, J, P)


def ssd_mixer(h, w_in, conv_w, conv_b, dt_bias, a_log, d_skip, norm_g, w_out):
    b, S, _ = h.shape
    GN = SSD_GROUPS * SSD_STATE
    u = h @ w_in
    z = u[..., :SSD_DI]
    xbc = u[..., SSD_DI:SSD_DI + SSD_CONV_CH]
    dt_raw = u[..., SSD_DI + SSD_CONV_CH:].reshape(b, S, 2, SSD_HEADS)
    xbc = jax.nn.silu(centred_depthwise_conv(xbc, conv_w, conv_b))
    xs = xbc[..., :SSD_DI].reshape(b, S, SSD_GROUPS, SSD_HPG, SSD_HEADDIM)
    Bm = xbc[..., SSD_DI:SSD_DI + GN].reshape(b, S, SSD_GROUPS, SSD_STATE)
    Cm = xbc[..., SSD_DI + GN:].reshape(b, S, SSD_GROUPS, SSD_STATE)
    dt = jax.nn.softplus(dt_raw.astype(F32) + dt_bias.astype(F32))
    A = -jnp.exp(a_log.astype(F32))
    dA = (dt * A).reshape(b, S, 2, SSD_GROUPS, SSD_HPG)
    dtg = dt.reshape(b, S, 2, SSD_GROUPS, SSD_HPG)
    x_f = xs * dtg[:, :, 0, :, :, None]
    x_b = xs * dtg[:, :, 1, :, :, None]
    y_fwd = ssd_chunked(x_f, dA[:, :, 0], Bm, Cm)
    y_bwd = jnp.flip(ssd_chunked(jnp.flip(x_b, 1), jnp.flip(dA[:, :, 1], 1),
                                 jnp.flip(Bm, 1), jnp.flip(Cm, 1)), 1)
    y = y_fwd + y_bwd + xs * d_skip.reshape(SSD_GROUPS, SSD_HPG)[..., None]
    y = y.reshape(b, S, SSD_DI) * jax.nn.silu(z)
    gs = SSD_DI // SSD_GROUPS
    y = rms_norm(y.reshape(b, S, SSD_GROUPS, gs), norm_g.reshape(SSD_GROUPS, gs))
    return y.reshape(b, S, SSD_DI).astype(h.dtype) @ w_out


def hgrn2_chunked(q, k, v, g):
    b, S, h, dk = q.shape
    dv = v.shape[-1]
    C = HG_CHUNK
    nc = S // C

    def to_chunks(t):
        return t.reshape(b, nc, C, h, t.shape[-1]).transpose(1, 0, 3, 2, 4)

    mask = jnp.tril(jnp.ones((C, C), dtype=bool))

    def step(state, inp):
        qi, ki, vi, gi = inp
        G = jnp.cumsum(gi, axis=-2)
        Gr = G[..., C // 2:C // 2 + 1, :]
        q_t = qi * jnp.exp(G - Gr)
        k_t = ki * jnp.exp(Gr - G)
        att = jnp.where(mask, jnp.einsum('bhid,bhjd->bhij', q_t, k_t), 0.0)
        o = (jnp.einsum('bhij,bhjv->bhiv', att, vi)
             + jnp.einsum('bhid,bhdv->bhiv', qi * jnp.exp(G), state))
        G_last = G[..., -1:, :]
        new_state = (jnp.exp(G_last)[..., 0, :, None] * state
                     + jnp.einsum('bhjd,bhjv->bhdv', ki * jnp.exp(G_last - G), vi))
        return new_state, o

    s0 = jnp.zeros((b, h, dk, dv), F32)
    _, o = lax.scan(step, s0, (to_chunks(q), to_chunks(k), to_chunks(v), to_chunks(g)))
    return o.transpose(1, 0, 3, 2, 4).reshape(b, S, h, dv)


def hgrn2_mixer(h, lb, w_in, norm_g, w_out):
    b, S, _ = h.shape
    u = h @ w_in
    q, f_fwd, f_bwd, inp, gate = jnp.split(u, 5, axis=-1)
    shp = (b, S, HG_HEADS, HG_EXPAND)
    q = jax.nn.silu(q).astype(F32).reshape(shp)
    v = inp.astype(F32).reshape(b, S, HG_HEADS, HG_DV)
    lbh = lb.reshape(HG_HEADS, HG_EXPAND)

    def direction(fpre, reverse):
        f = lbh + (1.0 - lbh) * jax.nn.sigmoid(fpre.astype(F32).reshape(shp))
        args = (q, 1.0 - f, v, jnp.log(f))
        if reverse:
            return jnp.flip(hgrn2_chunked(*[jnp.flip(a, 1) for a in args]), 1)
        return hgrn2_chunked(*args)

    o = direction(f_fwd, False) + direction(f_bwd, True)
    o = rms_norm(o, norm_g.reshape(HG_HEADS, HG_DV)).reshape(b, S, HG_W).astype(h.dtype)
    return (o * jax.nn.silu(gate)) @ w_out


def axial_rope_tables(S):
    rows = S // GRID_W
    row = jnp.repeat(jnp.arange(rows), GRID_W).astype(F32)
    col = (jnp.arange(S) % GRID_W).astype(F32)
    inv = ROPE_THETA ** (-jnp.arange(0, ROPE_AXIS, 2, dtype=F32) / ROPE_AXIS)
    ang = jnp.stack([row[:, None] * inv, col[:, None] * inv], axis=1)
    return jnp.cos(ang), jnp.sin(ang)


def apply_axial_rope(x, cos, sin):
    shp = x.shape
    xf = x.astype(F32).reshape(*shp[:-1], 2, 2, ROPE_AXIS // 2)
    x1, x2 = xf[..., 0, :], xf[..., 1, :]
    c, s = cos[:, None], sin[:, None]
    out = jnp.stack([x1 * c - x2 * s, x2 * c + x1 * s], axis=-2)
    return out.reshape(shp).astype(x.dtype)


def gqa_mixer(h, w_in, q_g, k_g, w_out):
    b, S, _ = h.shape
    u = h @ w_in
    q = u[..., :AT_QW].reshape(b, S, AT_HEADS, AT_HD)
    k = u[..., AT_QW:AT_QW + AT_KW].reshape(b, S, AT_KV, AT_HD)
    v = u[..., AT_QW + AT_KW:AT_QW + 2 * AT_KW].reshape(b, S, AT_KV, AT_HD)
    gate = u[..., AT_QW + 2 * AT_KW:]
    cos, sin = axial_rope_tables(S)
    q = apply_axial_rope(rms_norm(q, q_g), cos, sin) * (AT_HD ** -0.5)
    k = apply_axial_rope(rms_norm(k, k_g), cos, sin)
    nq = S // AT_QBLK
    qb = q.reshape(b, nq, AT_QBLK, AT_KV, AT_GRP, AT_HD).transpose(1, 0, 2, 3, 4, 5)

    def block(qi):
        s = jnp.einsum('bqkgd,bskd->bkgqs', qi, k).astype(F32)
        p = jax.nn.softmax(s, axis=-1).astype(v.dtype)
        return jnp.einsum('bkgqs,bskd->bqkgd', p, v)

    o = lax.map(block, qb)
    o = o.transpose(1, 0, 2, 3, 4, 5).reshape(b, S, AT_QW)
    return (o * jax.nn.silu(gate)) @ w_out


def t5_bucket(rel):
    half = REL_BUCKETS // 2
    exact = half // 2
    n = jnp.abs(rel)
    large = exact + (jnp.log(jnp.maximum(n, 1).astype(F32) / exact)
                     / math.log(REL_MAX_DIST / exact) * (half - exact)).astype(jnp.int32)
    large = jnp.minimum(large, half - 1)
    return jnp.where(rel > 0, half, 0) + jnp.where(n < exact, n, large)


def dilated_group(q, k, v, dil, steps, rel_bias):
    b, S, h, e = q.shape
    Ls = S // dil
    blk = steps
    nb = -(-Ls // blk)
    Lp = nb * blk

    def sub(t):
        return t.reshape(b, Ls, dil, h, e).transpose(0, 2, 1, 3, 4)

    qb = jnp.pad(sub(q), ((0, 0), (0, 0), (0, Lp - Ls), (0, 0), (0, 0))).reshape(b, dil, nb, blk, h, e)
    kpad = ((0, 0), (0, 0), (blk, Lp - Ls + blk), (0, 0), (0, 0))
    kp = jnp.pad(sub(k), kpad).reshape(b, dil, nb + 2, blk, h, e)
    vp = jnp.pad(sub(v), kpad).reshape(b, dil, nb + 2, blk, h, e)
    kb = jnp.concatenate([kp[:, :, 0:nb], kp[:, :, 1:nb + 1], kp[:, :, 2:nb + 2]], axis=3)
    vb = jnp.concatenate([vp[:, :, 0:nb], vp[:, :, 1:nb + 1], vp[:, :, 2:nb + 2]], axis=3)
    i = jnp.arange(blk)[:, None]
    j = jnp.arange(3 * blk)[None, :]
    dm = j - blk - i
    m_k = jnp.arange(nb)[:, None, None] * blk + j[None] - blk
    mask = (jnp.abs(dm) <= steps)[None] & (m_k >= 0) & (m_k < Ls)
    bias = rel_bias[t5_bucket(dm * dil)].transpose(2, 0, 1).astype(F32)
    s = jnp.einsum('bdnqhe,bdnkhe->bdnhqk', qb, kb).astype(F32) + bias[None, None, None]
    s = jnp.where(mask[None, None, :, None], s, NEG_BIG)
    lse = jax.nn.logsumexp(s, axis=-1)
    p = jnp.exp(s - lse[..., None]).astype(v.dtype)
    o = jnp.einsum('bdnhqk,bdnkhe->bdnqhe', p, vb)
    o = o.reshape(b, dil, Lp, h, e)[:, :, :Ls].transpose(0, 2, 1, 3, 4).reshape(b, S, h, e)
    lse = lse.transpose(0, 1, 2, 4, 3).reshape(b, dil, Lp, h)[:, :, :Ls]
    lse = lse.transpose(0, 2, 1, 3).reshape(b, S, h)
    return o, lse


def dilated_mixer(h, rel_bias, w_in, w_out):
    b, S, _ = h.shape
    u = h @ w_in
    outs, lses = [], []
    for gi, (window, dil) in enumerate(DL_PAIRS):
        base = gi * 3 * DL_W
        q, k, v = [u[..., base + c * DL_W: base + (c + 1) * DL_W].reshape(b, S, DL_HEADS, DL_HD)
                   for c in range(3)]
        o, l = dilated_group(q * (DL_HD ** -0.5), k, v, dil, (window // 2) // dil, rel_bias)
        outs.append(o)
        lses.append(l)
    w = jax.nn.softmax(jnp.stack(lses), axis=0)
    o = jnp.sum(w[..., None] * jnp.stack(outs).astype(F32), axis=0)
    o = o.astype(h.dtype).reshape(b, S, DL_W)
    gate = u[..., 3 * len(DL_PAIRS) * DL_W:]
    return (o * jax.nn.silu(gate)) @ w_out


def _dense(key, shape, fan_in):
    return jax.random.normal(key, shape, F32) * (fan_in ** -0.5)


def _gain(key, shape):
    return 1.0 + 0.05 * jax.random.normal(key, shape, F32)


def setup_inputs(seed: int = 0) -> dict:
    key = jax.random.key(seed)
    k = jax.random.split(key, 24)
    nA, nB, nC, nD = N_SSD_LAYERS, N_HG_LAYERS, N_AT_LAYERS, N_DL_LAYERS
    dt0 = jnp.exp(jax.random.uniform(k[8], (nA, 2, SSD_HEADS), F32,
                                     minval=math.log(1e-3), maxval=math.log(1e-1)))
    return {
        "x": jax.random.normal(k[0], (BATCH, SEQ, D_MODEL), F32),
        "norm_g": _gain(k[1], (DEPTH, D_MODEL)),
        "final_g": _gain(k[2], (D_MODEL,)),
        "rel_bias": 0.5 * jax.random.normal(k[3], (REL_BUCKETS, DL_HEADS), F32),
        "hgrn_lb": 0.5 * jax.random.normal(k[4], (DEPTH, HG_W), F32),
        "ssd_w_in": _dense(k[5], (nA, D_MODEL, SSD_IN), D_MODEL),
        "ssd_conv_w": _dense(k[6], (nA, SSD_CONV, SSD_CONV_CH), SSD_CONV),
        "ssd_conv_b": 0.02 * jax.random.normal(k[7], (nA, SSD_CONV_CH), F32),
        "ssd_dt_bias": dt0 + jnp.log(-jnp.expm1(-dt0)),
        "ssd_a_log": jnp.log(jax.random.uniform(k[9], (nA, 2, SSD_HEADS), F32, minval=1.0, maxval=16.0)),
        "ssd_d": 1.0 + 0.1 * jax.random.normal(k[10], (nA, SSD_HEADS), F32),
        "ssd_norm_g": _gain(k[11], (nA, SSD_DI)),
        "ssd_w_out": _dense(k[12], (nA, SSD_DI, D_MODEL), SSD_DI),
        "hg_w_in": _dense(k[13], (nB, D_MODEL, HG_IN), D_MODEL),
        "hg_norm_g": _gain(k[14], (nB, HG_W)),
        "hg_w_out": _dense(k[15], (nB, HG_W, D_MODEL), HG_W),
        "at_w_in": _dense(k[16], (nC, D_MODEL, AT_IN), D_MODEL),
        "at_q_norm_g": _gain(k[17], (nC, AT_HD)),
        "at_k_norm_g": _gain(k[18], (nC, AT_HD)),
        "at_w_out": _dense(k[19], (nC, AT_QW, D_MODEL), AT_QW),
        "dl_w_in": _dense(k[20], (nD, D_MODEL, DL_IN), D_MODEL),
        "dl_w_out": _dense(k[21], (nD, DL_W, D_MODEL), DL_W),
    }


def reference(x, norm_g, final_g, rel_bias, hgrn_lb,
              ssd_w_in, ssd_conv_w, ssd_conv_b, ssd_dt_bias, ssd_a_log, ssd_d, ssd_norm_g, ssd_w_out,
              hg_w_in, hg_norm_g, hg_w_out,
              at_w_in, at_q_norm_g, at_k_norm_g, at_w_out,
              dl_w_in, dl_w_out):
    lb_sm = jax.nn.softmax(hgrn_lb.astype(F32), axis=0)
    lb_all = jnp.cumsum(lb_sm, axis=0) - lb_sm[0:1]
    for layer in range(DEPTH):
        kind = layer % N_MIXERS
        slot = layer // N_MIXERS
        hn = rms_norm(x, norm_g[layer])
        if kind == 0:
            y = ssd_mixer(hn, ssd_w_in[slot], ssd_conv_w[slot], ssd_conv_b[slot], ssd_dt_bias[slot],
                          ssd_a_log[slot], ssd_d[slot], ssd_norm_g[slot], ssd_w_out[slot])
        elif kind == 1:
            y = hgrn2_mixer(hn, lb_all[layer], hg_w_in[slot], hg_norm_g[slot], hg_w_out[slot])
        elif kind == 2:
            y = gqa_mixer(hn, at_w_in[slot], at_q_norm_g[slot], at_k_norm_g[slot], at_w_out[slot])
        else:
            y = dilated_mixer(hn, rel_bias, dl_w_in[slot], dl_w_out[slot])
        x = x + y.astype(x.dtype)
    return rms_norm(x, final_g)
```

```python
from contextlib import ExitStack
import numpy as np
import ml_dtypes
import concourse.bass as bass
import concourse.mybir as mybir
from concourse.bass_utils import run_bass_kernel_spmd

F32 = mybir.dt.float32
BF16 = mybir.dt.bfloat16
I32 = mybir.dt.int32
AF = mybir.ActivationFunctionType
ALU = mybir.AluOpType
AX = mybir.AxisListType

ENGS = ("pe", "act", "dve", "pool", "sp")
DMA_SLOTS = 6


class Buf:
    __slots__ = ("name", "w", "r", "excl")

    def __init__(self, name, excl=False):
        self.name = name
        self.excl = excl
        self.w = None
        self.r = []


class Prog:
    def __init__(self):
        self.nc = bass.Bass("TRN2", target_bir_lowering=False)
        self.es = ExitStack()
        self.q = {e: [] for e in ENGS}
        self.sem = {}
        self.seen = {e: {} for e in ENGS}
        self.ecnt = {e: 0 for e in ENGS}
        self.dma_i = {qe: 0 for qe in ("sp", "act", "pool")}
        self.dcnt = {}
        self.n_ops = 0

    EPOCH = 20000

    def _sem(self, k):
        if k not in self.sem:
            self.sem[k] = self.es.enter_context(self.nc.semaphore("s%d" % len(self.sem)))
        return self.sem[k]

    def dram(self, name, shape, dtype=F32, kind="ExternalInput"):
        return self.nc.dram_tensor(name, list(shape), dtype, kind=kind).ap()

    def sbuf(self, name, shape, dtype=F32):
        return self.es.enter_context(self.nc.sbuf_tensor(name, list(shape), dtype))

    def psum(self, name, shape, dtype=F32):
        return self.es.enter_context(self.nc.psum_tensor(name, list(shape), dtype))

    def _deps(self, reads, writes):
        need = {}
        def add(ev):
            if ev is None:
                return
            k, v = ev
            if need.get(k, 0) < v:
                need[k] = v
        for b in reads:
            add(b.w)
            if b.excl:
                for ev in b.r:
                    add(ev)
        for b in writes:
            add(b.w)
            for ev in b.r:
                add(ev)
        return need

    def _commit(self, ev, reads, writes):
        for b in reads:
            if b.excl:
                b.r = [ev]
            else:
                b.r.append(ev)
        for b in writes:
            b.w = ev
            b.r = []

    def _waits(self, eng, need):
        ws = []
        seen = self.seen[eng]
        for k, v in need.items():
            if eng == "pe" and k[0] == "pe":
                continue
            if seen.get(k, 0) < v:
                seen[k] = v
                ws.append((k, v))
        return ws

    def op(self, eng, meth, *args, reads=(), writes=(), **kw):
        fn = (meth, args, kw)
        need = self._deps(reads, writes)
        ws = self._waits(eng, need)
        n = self.ecnt[eng]
        self.ecnt[eng] = n + 1
        k = (eng, n // self.EPOCH)
        ev = (k, n % self.EPOCH + 1)
        self.q[eng].append((ws, fn, k, 1))
        self._commit(ev, reads, writes)
        self.n_ops += 1
        return ev

    def dma(self, qe, out, in_, reads=(), writes=(), **kw):
        need = self._deps(reads, writes)
        i = self.dma_i[qe]
        self.dma_i[qe] = i + 1
        slot = i % DMA_SLOTS
        ep = (i // DMA_SLOTS) // 1000
        k = ("dma", qe, slot, ep)
        c = self.dcnt.get(k, 0)
        if c > 0:
            need[k] = max(need.get(k, 0), c)
        elif ep > 0:
            kp = ("dma", qe, slot, ep - 1)
            need[kp] = max(need.get(kp, 0), self.dcnt[kp])
        ws = self._waits(qe, need)
        self.dcnt[k] = c + 16
        ev = (k, c + 16)
        self.q[qe].append((ws, ("dma_start", (), dict(out=out, in_=in_, **kw)), k, 16))
        self._commit(ev, reads, writes)
        self.n_ops += 1
        return ev

    def wait_all_dma(self):
        for qe in ("sp", "act", "pool"):
            need = {k: c for k, c in self.dcnt.items() if k[1] == qe}
            ws = self._waits(qe, need)
            if ws:
                self.q[qe].append((ws, None, None, 0))


    def fence(self):
        need = {}
        for e in ENGS:
            n = self.ecnt[e]
            if n > 0:
                need[(e, (n - 1) // self.EPOCH)] = (n - 1) % self.EPOCH + 1
        for k, c in self.dcnt.items():
            need[k] = c
        for e in ENGS:
            ws = self._waits(e, dict(need))
            if ws:
                self.q[e].append((ws, None, None, 0))

    def emit(self):
        self.wait_all_dma()
        nc = self.nc
        engobj = {"pe": "tensor", "act": "scalar", "dve": "vector", "pool": "gpsimd", "sp": "sync"}
        with nc.Block() as block:
            for e in ENGS:
                items = self.q[e]

                def body(eng, items=items):
                    for ws, fn, sk, inc in items:
                        for k, v in ws:
                            eng.wait_ge(self._sem(k), v)
                        if fn is not None:
                            ins = getattr(eng, fn[0])(*fn[1], **fn[2])
                            ins.then_inc(self._sem(sk), inc)
                getattr(block, engobj[e])(body)
        self.es.close()
        return nc


class T:
    def __init__(self, P, name, shape, dtype=F32, psum=False):
        if psum:
            esz = 4 if dtype == F32 else 2
            n = int(np.prod(shape[1:]))
            assert n * esz <= 2048
            h = P.psum("t_" + name, [128, 2048 // esz], dtype)
            v = h[0:shape[0], 0:n]
            if len(shape) == 3:
                v = v.rearrange("p (a b) -> p a b", b=shape[2])
            elif len(shape) == 4:
                v = v.rearrange("p (a b c) -> p a b c", b=shape[2], c=shape[3])
            self.t = v
        else:
            self.t = P.sbuf("t_" + name, shape, dtype)
        self.b = Buf(name, excl=psum)

    def __getitem__(self, k):
        return self.t[k]


def bf16_np(a):
    return np.asarray(a, dtype=np.float32).astype(ml_dtypes.bfloat16)


class TV:
    def __init__(self, ap, name, excl):
        self.t = ap
        self.b = Buf(name, excl=excl)

    def __getitem__(self, k):
        return self.t[k]


class Arena:
    def __init__(self, P, kib=188):
        self.P = P
        self.n32 = kib * 256
        self.h = P.sbuf("arena", [128, self.n32], F32)
        self.banks = [P.psum("bank%d" % i, [128, 512], F32) for i in range(8)]
        self.off = 0
        self.nb = 0

    def reset(self):
        self.P.fence()
        self.off = 0
        self.nb = 0

    def T(self, name, shape, dtype=F32, psum=False):
        esz = 4 if dtype == F32 else 2
        n = int(np.prod(shape[1:]))
        n32 = (n * esz + 3) // 4
        if psum:
            assert n32 <= 512 and self.nb < 8, name
            v = self.banks[self.nb][0:shape[0], 0:n32]
            self.nb += 1
        else:
            n32 = (n32 + 7) // 8 * 8
            assert self.off + n32 <= self.n32, (name, self.off, n32)
            v = self.h[0:shape[0], self.off:self.off + n32]
            self.off += n32
        if dtype != F32:
            v = v.bitcast(dtype)
        v = v[:, 0:n]
        if len(shape) == 3:
            v = v.rearrange("p (a b) -> p a b", b=shape[2])
        elif len(shape) == 4:
            v = v.rearrange("p (a b c) -> p a b c", b=shape[2], c=shape[3])
        return TV(v, name, psum)


TT = 2048
NTT = TT // 128
D = 1024
EPS = 1e-6
S = 4096
NST = S // 128


def emit_lin(P, A, t0, ident_d, x_d, c=None, a=None, fin=None):
    A.reset()
    T = A.T
    has_c, has_a, final = c is not None, a is not None, fin is not None
    ident = T("ident", [128, 128], BF16)
    P.dma("sp", ident[:], ident_d, writes=[ident.b])
    wstage = [T("wstage%d" % i, [128, 4, 512]) for i in range(2)]
    ws_i = [0]

    def load_w_bf(dst, dst_k0, src_ap, ncols):
        st = wstage[ws_i[0] % 2]
        ws_i[0] += 1
        P.dma("sp", st[:, :, 0:ncols], src_ap, writes=[st.b])
        P.op("pool", "tensor_copy", dst[:, dst_k0:dst_k0 + 4, 0:ncols], st[:, :, 0:ncols], reads=[st.b], writes=[dst.b])

    xt = [T("xt%d" % i, [128, D]) for i in range(3)]
    tps = T("tps", [128, 8, 128], BF16, psum=True)
    if has_c:
        W, mode = c["W"], c["mode"]
        WC = W // 128
        use_gate = c["gate"] is not None
        wout = T("wout", [128, WC, D], BF16)
        wv = c["w_out"].rearrange("(kc p) n -> p kc n", p=128)
        for k0 in range(0, WC, 4):
            for nb in range(2):
                st = wstage[ws_i[0] % 2]
                ws_i[0] += 1
                P.dma("sp", st[:, :, :], wv[:, k0:k0 + 4, nb * 512:(nb + 1) * 512], writes=[st.b])
                P.op("pool", "tensor_copy", wout[:, k0:k0 + 4, nb * 512:(nb + 1) * 512], st[:, :, :],
                     reads=[st.b], writes=[wout.b])
        gTb = T("gTb", [128, WC, 512], BF16)
        yps = [[T("yps%d_%d" % (i, nb), [128, 512], F32, psum=True) for nb in range(2)] for i in range(2)]
        xout_b = Buf("xout")
        if mode == "fm":
            o_sb = [T("o_sb%d" % i, [128, 4, 512]) for i in range(2)]
            g_sb = [T("g_sb%d" % i, [128, 4, 512]) for i in range(2)] if use_gate else None
        else:
            o_tm = [T("o_tm%d" % i, [128, W]) for i in range(2)]
            g_tm = [T("g_tm%d" % i, [128, W]) for i in range(2)] if use_gate else None
            g_bf = [T("g_bf%d" % i, [128, W], BF16) for i in range(2)]
    if has_a or final:
        grep = T("grep", [128, D])
        P.dma("sp", grep[:], (a or fin)["g_rep"], writes=[grep.b])
        ssq = T("ssq", [128, 1])
        rstd = T("rstd", [128, 1])
        junk = T("junk", [128, D])
    if has_a:
        n_fm, n_tm = a["n_fm"], a["n_tm"]
        hnb = [T("hnb%d" % i, [128, D], BF16) for i in range(2)]
        hnT = T("hnT", [128, 8, TT], BF16)
        hnT_tb = [Buf("hnT%d" % i) for i in range(NTT)]
        winb = [T("winb%d" % i, [128, 8, 512], BF16) for i in range(2)]
        ups = [T("ups%d" % i, [128, 512], F32, psum=True) for i in range(3)]
        ust = [T("ust%d" % i, [128, 512]) for i in range(4)]
        u_b = Buf("u_scratch")
    if final:
        out_b = Buf("out_d")
        ost = [T("ost%d" % i, [128, D]) for i in range(2)]

    xi = 0
    ci = 0
    for blk in range(TT // 512):
        tb0 = t0 + blk * 512
        if has_c and mode == "fm":
            ov = c["o"].rearrange("(kc p) t -> p kc t", p=128)
            gv = c["gate"].rearrange("(kc p) t -> p kc t", p=128) if use_gate else None
            for k0 in range(0, WC, 4):
                i = ci % 2
                ci += 1
                ot = o_sb[i]
                P.dma("sp", ot[:], ov[:, k0:k0 + 4, tb0:tb0 + 512], reads=[c["o_b"]], writes=[ot.b])
                if use_gate:
                    gt = g_sb[i]
                    P.dma("sp", gt[:], gv[:, k0:k0 + 4, tb0:tb0 + 512], reads=[c["gate_b"]], writes=[gt.b])
                    P.op("act", "activation", gt[:], gt[:], AF.Silu, reads=[gt.b], writes=[gt.b])
                    P.op("dve", "tensor_tensor", gTb[:, k0:k0 + 4, :], ot[:], gt[:], ALU.mult, reads=[ot.b, gt.b], writes=[gTb.b])
                else:
                    P.op("dve", "tensor_copy", gTb[:, k0:k0 + 4, :], ot[:], reads=[ot.b], writes=[gTb.b])
        if has_c and mode == "tm":
            for ti in range(4):
                tsl = slice(tb0 + ti * 128, tb0 + (ti + 1) * 128)
                i = ci % 2
                ci += 1
                ot, gb = o_tm[i], g_bf[i]
                P.dma("sp", ot[:], c["o"][tsl, 0:W], reads=[c["o_b"]], writes=[ot.b])
                if use_gate:
                    gt = g_tm[i]
                    P.dma("sp", gt[:], c["gate"][tsl, :], reads=[c["gate_b"]], writes=[gt.b])
                    P.op("act", "activation", gt[:], gt[:], AF.Silu, reads=[gt.b], writes=[gt.b])
                    P.op("dve", "tensor_tensor", gb[:], ot[:], gt[:], ALU.mult, reads=[ot.b, gt.b], writes=[gb.b])
                else:
                    P.op("dve", "tensor_copy", gb[:], ot[:], reads=[ot.b], writes=[gb.b])
                for w0 in range(0, WC, 8):
                    nw = min(8, WC - w0)
                    for k in range(nw):
                        P.op("pe", "transpose", tps[:, k, :], gb[:, (w0 + k) * 128:(w0 + k + 1) * 128], ident[:],
                             reads=[gb.b, ident.b], writes=[tps.b])
                    P.op("act", "copy", gTb[:, w0:w0 + nw, ti * 128:(ti + 1) * 128], tps[:, 0:nw, :],
                         reads=[tps.b], writes=[gTb.b])
        for ti in range(4):
            tt = blk * 4 + ti
            tsl = slice(tb0 + ti * 128, tb0 + (ti + 1) * 128)
            xb = xt[xi % 3]
            xi += 1
            P.dma("sp", xb[:], x_d["ap"][tsl, :], reads=[x_d["b"]], writes=[xb.b])
            if has_c:
                yp = yps[tt % 2]
                for nb in range(2):
                    for wc in range(WC):
                        P.op("pe", "matmul", yp[nb][:], gTb[:, wc, ti * 128:(ti + 1) * 128],
                             wout[:, wc, nb * 512:(nb + 1) * 512], start=(wc == 0), stop=(wc == WC - 1),
                             reads=[gTb.b, wout.b], writes=[yp[nb].b])
                for nb in range(2):
                    P.op("dve", "tensor_tensor", xb[:, nb * 512:(nb + 1) * 512], xb[:, nb * 512:(nb + 1) * 512], yp[nb][:],
                         ALU.add, reads=[xb.b, yp[nb].b], writes=[xb.b])
                if not final:
                    P.dma("act", c["xout"]["ap"][tsl, :], xb[:], reads=[xb.b], writes=[c["xout"]["b"]])
            if has_a or final:
                P.op("act", "activation", junk[:], xb[:], AF.Square, accum_out=ssq[:], reads=[xb.b], writes=[junk.b, ssq.b])
                P.op("act", "activation", rstd[:], ssq[:], AF.Sqrt, bias=EPS, scale=1.0 / D, reads=[ssq.b], writes=[rstd.b])
                P.op("dve", "reciprocal", rstd[:], rstd[:], reads=[rstd.b], writes=[rstd.b])
            if final:
                ot_ = ost[tt % 2]
                P.op("dve", "scalar_tensor_tensor", ot_[:], xb[:], rstd[:, 0:1], grep[:], ALU.mult, ALU.mult,
                     reads=[xb.b, rstd.b, grep.b], writes=[ot_.b])
                P.dma("act", fin["out"][tsl, :], ot_[:], reads=[ot_.b], writes=[out_b])
            if has_a:
                hb = hnb[tt % 2]
                P.op("dve", "scalar_tensor_tensor", hb[:], xb[:], rstd[:, 0:1], grep[:], ALU.mult, ALU.mult,
                     reads=[xb.b, rstd.b, grep.b], writes=[hb.b])
                for kc in range(8):
                    P.op("pe", "transpose", tps[:, kc, :], hb[:, kc * 128:(kc + 1) * 128], ident[:],
                         reads=[hb.b, ident.b], writes=[tps.b])
                P.op("act", "copy", hnT[:, :, tt * 128:(tt + 1) * 128], tps[:], reads=[tps.b], writes=[hnT_tb[tt]])
    if has_a:
        N = n_fm + n_tm
        wv = a["w_in"].rearrange("(kc p) n -> p kc n", p=128)
        nblk = (N + 511) // 512
        assert n_fm % 512 == 0
        ui = 0
        for nb in range(nblk):
            c0 = nb * 512
            cw = min(512, N - c0)
            wb = winb[nb % 2]
            for k0 in (0, 4):
                load_w_bf(wb, k0, wv[:, k0:k0 + 4, c0:c0 + cw], cw)
            if c0 < n_fm:
                for cc in range(4):
                    for tb in range(TT // 512):
                        up, us = ups[ui % 3], ust[ui % 4]
                        for kc in range(8):
                            P.op("pe", "matmul", up[:], wb[:, kc, cc * 128:(cc + 1) * 128], hnT[:, kc, tb * 512:(tb + 1) * 512],
                                 start=(kc == 0), stop=(kc == 7),
                                 reads=[wb.b] + hnT_tb[tb * 4:tb * 4 + 4], writes=[up.b])
                        P.op("act", "copy", us[:], up[:], reads=[up.b], writes=[us.b])
                        r0 = c0 + cc * 128
                        P.dma("act", a["ufm"][r0:r0 + 128, t0 + tb * 512:t0 + (tb + 1) * 512], us[:], reads=[us.b], writes=[u_b])
                        ui += 1
            else:
                for tt in range(NTT):
                    up, us = ups[ui % 3], ust[ui % 4]
                    for kc in range(8):
                        P.op("pe", "matmul", up[:, 0:cw], hnT[:, kc, tt * 128:(tt + 1) * 128], wb[:, kc, 0:cw],
                             start=(kc == 0), stop=(kc == 7), reads=[hnT_tb[tt], wb.b], writes=[up.b])
                    P.op("act", "copy", us[:, 0:cw], up[:, 0:cw], reads=[up.b], writes=[us.b])
                    P.dma("act", a["utm"][t0 + tt * 128:t0 + (tt + 1) * 128, c0 - n_fm:c0 - n_fm + cw], us[:, 0:cw],
                          reads=[us.b], writes=[u_b])
                    ui += 1


SSD_NEG = -30000.0


def emit_ssd(P, A, hh, io):
    A.reset()
    T = A.T
    NCK = S // 128
    ufm, utm, o_d = io["ufm"], io["utm"], io["o"]
    o_b = Buf("o_d")

    def const(name, d_ap, shape, dtype=F32):
        t = T(name, shape, dtype)
        P.dma("sp", t[:], d_ap, writes=[t.b])
        return t
    cw = const("cw", io["convw"][hh], [128, 2, 6, 7])
    cbias = const("cbias", io["convb"][hh], [128, 2, 6])
    dtb = const("dtb", io["dtb_rep"][hh], [128, 2, 2, 8])
    alog = const("alog", io["alog_rep"][hh], [128, 2, 2, 8])
    dsk = const("dsk", io["d_rep"][hh], [128, 2, 8])
    ngr = const("ngr", io["ssd_ng_rep"][hh], [128, 2, 512])
    tmat = const("tmatc", io["tmat"], [128, 2, 128])
    negm = const("negmc", io["negm"], [128, 2, 4, 128])
    identf = const("identf", io["identf"], [128, 128])
    ident = const("ident", io["ident"], [128, 128], BF16)
    onesf = T("onesf", [128, 128])
    P.op("pool", "memset", onesf[:], 1.0, writes=[onesf.b])
    P.op("act", "activation", alog[:], alog[:], AF.Exp, reads=[alog.b], writes=[alog.b])
    P.op("dve", "tensor_scalar", alog[:], alog[:], -1.0, None, ALU.mult, reads=[alog.b], writes=[alog.b])

    xcT = T("xcT", [128, 6, S], BF16)
    yb = T("yb", [128, NCK, 512])
    ybf = yb[:].rearrange("p c e -> p (c e)")
    raw = ybf[:, 0:S + 6]
    acc = ybf[:, 8192:8192 + S]
    dtall = T("dtall", [128, NCK, 2, 8])
    dA = T("dA", [128, NCK, 2, 8])
    ncum = T("ncum", [128, NCK, 2, 8])
    ecum = T("ecum", [128, NCK, 2, 8])
    dtd = T("dtd", [128, NCK, 2, 8])
    etot = T("etot", [128, NCK, 2, 8])
    fr = T("fr", [128, 512], F32, psum=True)
    tpx = fr[:, 0:256].bitcast(BF16).rearrange("p (k e) -> p k e", e=128)
    tpb = fr[:, 256:320].bitcast(BF16)
    cbp = fr[:, 320:448]
    bcp = [T("bcp%d" % i, [128, 4, 128], F32, psum=True) for i in range(4)]
    yps = T("yps", [128, 512], F32, psum=True)
    yop = T("yop", [128, 512], F32, psum=True)
    stp = T("stp", [128, 512], F32, psum=True)
    xst = [T("xst%d" % i, [128, 8, 64], BF16) for i in range(2)]
    bst = [T("bst%d" % i, [128, 128], BF16) for i in range(2)]
    cbT = [T("cbT%d" % i, [128, 128]) for i in range(2)]
    dAb = [T("dAb%d" % i, [128, 8, 128]) for i in range(2)]
    xdt = [T("xdt%d" % i, [128, 8, 64], BF16) for i in range(2)]
    xdd = [T("xdd%d" % i, [128, 8, 64], BF16) for i in range(2)]
    NLT = 6
    LT = [T("LT%d" % i, [128, 128]) for i in range(NLT)]
    MT = [T("MT%d" % i, [128, 128], BF16) for i in range(NLT)]
    ytm = [T("ytm%d" % i, [128, 8, 64]) for i in range(2)]
    yt = [T("yt%d" % i, [128, 8, 64]) for i in range(2)]
    zt = [T("zt%d" % i, [128, 512]) for i in range(2)]
    Sst = T("Sst", [128, 8, 64])
    Sb = [T("Sb%d" % i, [128, 512], BF16) for i in range(2)]
    ssq = T("ssq", [128, 1])
    junk = T("junk", [128, 512])
    li = 0

    for gl in range(2):
        g = 2 * hh + gl
        rows = [g * 512 + k * 128 for k in range(4)] + [2048 + g * 128, 2048 + 512 + g * 128]
        for ci in range(6):
            P.op("pool", "memset", raw[:, 0:3], 0.0, writes=[yb.b])
            P.op("pool", "memset", raw[:, S + 3:S + 6], 0.0, writes=[yb.b])
            P.dma("sp", raw[:, 3:3 + S], ufm[rows[ci]:rows[ci] + 128, :], writes=[yb.b])
            P.op("dve", "tensor_scalar", acc, raw[:, 0:S], cw[:, gl, ci, 0:1], None, ALU.mult, reads=[yb.b, cw.b], writes=[yb.b])
            for k in range(1, 7):
                P.op("dve", "scalar_tensor_tensor", acc, raw[:, k:k + S], cw[:, gl, ci, k:k + 1], acc, ALU.mult, ALU.add,
                     reads=[yb.b, cw.b], writes=[yb.b])
            P.op("act", "activation", xcT[:, ci, :], acc, AF.Silu, bias=cbias[:, gl, ci:ci + 1],
                 reads=[yb.b, cbias.b], writes=[xcT.b])
        for dr in range(2):
            c0 = 2048 + dr * 32 + g * 8
            P.dma("sp", dtall[:, :, dr, :], utm[:, c0:c0 + 8].rearrange("(c p) j -> p c j", p=128), writes=[dtall.b])
        P.op("dve", "tensor_tensor", dtall[:], dtall[:], dtb[:, gl].unsqueeze(1).to_broadcast([128, NCK, 2, 8]), ALU.add,
             reads=[dtall.b, dtb.b], writes=[dtall.b])
        P.op("act", "activation", dtall[:], dtall[:], AF.Exp, reads=[dtall.b], writes=[dtall.b])
        P.op("act", "activation", dtall[:], dtall[:], AF.Ln, bias=1.0, reads=[dtall.b], writes=[dtall.b])
        P.op("dve", "tensor_tensor", dA[:], dtall[:], alog[:, gl].unsqueeze(1).to_broadcast([128, NCK, 2, 8]), ALU.mult,
             reads=[dtall.b, alog.b], writes=[dA.b])
        cum_ps = yps[:, 0:256].rearrange("p (c j) -> p c j", j=8)
        tot_ps = yop[:, 0:256].rearrange("p (c j) -> p c j", j=8)
        for dr in range(2):
            P.op("pe", "matmul", cum_ps, tmat[:, dr, :], dA[:, :, dr, :], start=True, stop=True, reads=[tmat.b, dA.b], writes=[yps.b])
            P.op("pe", "matmul", tot_ps, onesf[:], dA[:, :, dr, :], start=True, stop=True, reads=[onesf.b, dA.b], writes=[yop.b])
            P.op("dve", "tensor_scalar", ncum[:, :, dr, :], cum_ps, -1.0, None, ALU.mult, reads=[yps.b], writes=[ncum.b])
            P.op("act", "activation", ecum[:, :, dr, :], cum_ps, AF.Exp, reads=[yps.b], writes=[ecum.b])
            P.op("act", "activation", etot[:, :, dr, :], tot_ps, AF.Exp, reads=[yop.b], writes=[etot.b])
            P.op("dve", "tensor_tensor", dtd[:, :, dr, :], ncum[:, :, dr, :], tot_ps, ALU.add, reads=[ncum.b, yop.b], writes=[dtd.b])
        P.op("act", "activation", dtd[:], dtd[:], AF.Exp, reads=[dtd.b], writes=[dtd.b])
        P.op("dve", "tensor_tensor", dtd[:], dtd[:], dtall[:], ALU.mult, reads=[dtd.b, dtall.b], writes=[dtd.b])

        for dr in (1, 0):
            P.op("dve", "memset", Sst[:], 0.0, writes=[Sst.b])
            P.op("pool", "memset", Sb[0][:], 0.0, writes=[Sb[0].b])
            order = list(range(NCK)) if dr == 0 else list(range(NCK - 1, -1, -1))
            def front(n, c):
                cs = slice(c * 128, (c + 1) * 128)
                i2 = n % 2
                xs_, bs_, cb_, dab_, xd_, xq_ = xst[i2], bst[i2], cbT[i2], dAb[i2], xdt[i2], xdd[i2]
                for k in range(4):
                    P.op("pe", "transpose", tpx[:, k, :], xcT[:, k, cs], ident[:], reads=[xcT.b, ident.b], writes=[fr.b])
                P.op("pe", "transpose", tpb, xcT[:, 4, cs], ident[:], reads=[xcT.b, ident.b], writes=[fr.b])
                P.op("pe", "matmul", cbp, xcT[:, 4, cs], xcT[:, 5, cs], start=True, stop=True, reads=[xcT.b], writes=[fr.b])
                P.op("act", "copy", xs_[:].rearrange("p j e -> p (j e)"), tpx.rearrange("p k e -> p (k e)"),
                     reads=[fr.b], writes=[xs_.b])
                P.op("act", "copy", bs_[:], tpb, reads=[fr.b], writes=[bs_.b])
                P.op("act", "copy", cb_[:], cbp, reads=[fr.b], writes=[cb_.b])
                P.op("dve", "tensor_tensor", xd_[:], xs_[:], dtall[:, c, dr, :].unsqueeze(2).to_broadcast([128, 8, 64]), ALU.mult,
                     reads=[xs_.b, dtall.b], writes=[xd_.b])
                P.op("pool", "tensor_tensor", xq_[:], xs_[:], dtd[:, c, dr, :].unsqueeze(2).to_broadcast([128, 8, 64]), ALU.mult,
                     reads=[xs_.b, dtd.b], writes=[xq_.b])
                P.op("dve", "tensor_tensor", dab_[:], tmat[:, dr, :].unsqueeze(1).to_broadcast([128, 8, 128]),
                     dA[:, c, dr, :].unsqueeze(2).to_broadcast([128, 8, 128]), ALU.mult,
                     reads=[tmat.b, dA.b], writes=[dab_.b])
                for h2 in range(2):
                    bp = bcp[i2 * 2 + h2]
                    P.op("pe", "matmul", bp[:].rearrange("p r l -> p (r l)"), identf[:],
                         negm[:, dr].rearrange("p r l -> p (r l)"), start=True, stop=False,
                         reads=[identf.b, negm.b], writes=[bp.b])
                    P.op("pe", "matmul", bp[:].rearrange("p r l -> p (r l)"), onesf[:],
                         dab_[:, h2 * 4:h2 * 4 + 4, :].rearrange("p r l -> p (r l)"), start=False, stop=True,
                         reads=[onesf.b, dab_.b], writes=[bp.b])
            front(0, order[0])
            for n, c in enumerate(order):
                cs = slice(c * 128, (c + 1) * 128)
                i2 = n % 2
                xs_, bs_, cb_, dab_, xd_, xq_ = xst[i2], bst[i2], cbT[i2], dAb[i2], xdt[i2], xdd[i2]
                sb_cur, sb_nxt = Sb[n % 2], Sb[(n + 1) % 2]
                if n + 1 < NCK:
                    front(n + 1, order[n + 1])
                for h2 in range(2):
                    bp = bcp[i2 * 2 + h2]
                    for jj in range(4):
                        j = h2 * 4 + jj
                        lt, mt = LT[li % NLT], MT[li % NLT]
                        li += 1
                        P.op("act", "activation", lt[:], bp[:, jj, :], AF.Exp, bias=ncum[:, c, dr, j:j + 1],
                             reads=[bp.b, ncum.b], writes=[lt.b])
                        P.op("pool" if li % 2 == 0 else "dve", "tensor_tensor", mt[:], lt[:], cb_[:], ALU.mult,
                             reads=[lt.b, cb_.b], writes=[mt.b])
                        P.op("pe", "matmul", yps[:, j * 64:(j + 1) * 64], mt[:], xd_[:, j, :], start=True, stop=True,
                             reads=[mt.b, xd_.b], writes=[yps.b])
                P.op("pe", "matmul", yop[:], xcT[:, 5, cs], sb_cur[:], start=True, stop=True, reads=[xcT.b, sb_cur.b], writes=[yop.b])
                ym, y_ = ytm[i2], yt[i2]
                P.op("dve", "tensor_tensor", ym[:], yop[:].rearrange("p (j e) -> p j e", e=64),
                     ecum[:, c, dr, :].unsqueeze(2).to_broadcast([128, 8, 64]), ALU.mult, reads=[yop.b, ecum.b], writes=[ym.b])
                if dr == 1:
                    P.op("dve", "tensor_tensor", yb[:, c, :], ym[:].rearrange("p j e -> p (j e)"), yps[:], ALU.add,
                         reads=[ym.b, yps.b], writes=[yb.b])
                else:
                    P.op("dve", "tensor_tensor", y_[:].rearrange("p j e -> p (j e)"), ym[:].rearrange("p j e -> p (j e)"), yps[:],
                         ALU.add, reads=[ym.b, yps.b], writes=[y_.b])
                P.op("pe", "matmul", stp[:], bs_[:], xq_[:].rearrange("p j e -> p (j e)"), start=True, stop=True,
                     reads=[bs_.b, xq_.b], writes=[stp.b])
                P.op("dve", "tensor_tensor", Sst[:], Sst[:], etot[:, c, dr, :].unsqueeze(2).to_broadcast([128, 8, 64]), ALU.mult,
                     reads=[Sst.b, etot.b], writes=[Sst.b])
                P.op("dve", "tensor_tensor", Sst[:].rearrange("p j e -> p (j e)"), Sst[:].rearrange("p j e -> p (j e)"), stp[:],
                     ALU.add, reads=[Sst.b, stp.b], writes=[Sst.b])
                P.op("act", "copy", sb_nxt[:], Sst[:].rearrange("p j e -> p (j e)"), reads=[Sst.b], writes=[sb_nxt.b])
                if dr == 0:
                    z_ = zt[i2]
                    P.dma("sp", z_[:], utm[cs, g * 512:(g + 1) * 512], writes=[z_.b])
                    P.op("act", "activation", z_[:], z_[:], AF.Silu, reads=[z_.b], writes=[z_.b])
                    yf = y_[:].rearrange("p j e -> p (j e)")
                    P.op("dve", "tensor_tensor", yf, yf, yb[:, c, :], ALU.add, reads=[y_.b, yb.b], writes=[y_.b])
                    P.op("pool", "tensor_tensor", ym[:], xs_[:], dsk[:, gl, :].unsqueeze(2).to_broadcast([128, 8, 64]), ALU.mult,
                         reads=[xs_.b, dsk.b], writes=[ym.b])
                    P.op("dve", "tensor_tensor", y_[:], y_[:], ym[:], ALU.add, reads=[y_.b, ym.b], writes=[y_.b])
                    P.op("dve", "tensor_tensor", yf, yf, z_[:], ALU.mult, reads=[y_.b, z_.b], writes=[y_.b])
                    P.op("act", "activation", junk[:], yf, AF.Square, accum_out=ssq[:], reads=[y_.b], writes=[junk.b, ssq.b])
                    P.op("act", "activation", ssq[:], ssq[:], AF.Sqrt, bias=EPS, scale=1.0 / 512, reads=[ssq.b], writes=[ssq.b])
                    P.op("dve", "reciprocal", ssq[:], ssq[:], reads=[ssq.b], writes=[ssq.b])
                    P.op("dve", "scalar_tensor_tensor", z_[:], yf, ssq[:, 0:1], ngr[:, gl, :], ALU.mult, ALU.mult,
                         reads=[y_.b, ssq.b, ngr.b], writes=[z_.b])
                    P.dma("sp", o_d[cs, g * 512:(g + 1) * 512], z_[:], reads=[z_.b], writes=[o_b])


def ssd_consts():
    s = np.arange(128)[:, None]
    l = np.arange(128)[None, :]
    tm = np.stack([(s <= l), (s >= l)], 1).astype(np.float32)
    nm = np.stack([np.where(l < s, SSD_NEG, 0.0), np.where(l > s, SSD_NEG, 0.0)], 1).astype(np.float32)
    nm = np.ascontiguousarray(np.repeat(nm[:, :, None, :], 4, axis=2))
    return tm, nm


def ssd_small(hh, conv_w, conv_b, dt_bias, a_log, d_skip, norm_g):
    cwt = np.empty((128, 2, 6, 7), np.float32)
    cbt = np.empty((128, 2, 6), np.float32)
    dtb = np.empty((128, 2, 2, 8), np.float32)
    alog = np.empty((128, 2, 2, 8), np.float32)
    dsk = np.empty((128, 2, 8), np.float32)
    ng = np.empty((128, 2, 512), np.float32)
    for gl in range(2):
        g = 2 * hh + gl
        chans = [np.arange(g * 512 + k * 128, g * 512 + (k + 1) * 128) for k in range(4)]
        chans.append(np.arange(2048 + g * 128, 2048 + (g + 1) * 128))
        chans.append(np.arange(2048 + 512 + g * 128, 2048 + 512 + (g + 1) * 128))
        for ci, ch in enumerate(chans):
            cwt[:, gl, ci, :] = conv_w[:, ch].T
            cbt[:, gl, ci] = conv_b[ch]
        hs = slice(g * 8, (g + 1) * 8)
        dtb[:, gl] = dt_bias[:, hs][None]
        alog[:, gl] = a_log[:, hs][None]
        dsk[:, gl] = d_skip[hs][None]
        ng[:, gl] = norm_g[g * 512:(g + 1) * 512][None]
    return cwt, cbt, dtb, alog, dsk, ng


HG_LAYER = 1


def emit_hg(P, A, hh, io):
    A.reset()
    T = A.T
    NCH = S // 32
    ufm, utm, od_d, o_d = io["ufm"], io["utm"], io["od"], io["o"]
    od_b = Buf("od_d")
    o_b = Buf("o_d")

    def const(name, d_ap, shape, dtype=F32):
        t = T(name, shape, dtype)
        P.dma("sp", t[:], d_ap, writes=[t.b])
        return t
    ident = const("ident", io["ident"], [128, 128], BF16)
    smask = const("smask", io["smask"], [128, 1024])
    masks = const("masks", io["masks"], [32, 2, 32])
    ng = const("ng", io["hg_ng_rep"][hh], [128, 512])
    lbt = const("lbt", io["lbT"][hh], [128, 4, 4])
    lsum = T("lsum", [128, 4])
    lb = T("lb", [128, 4])
    oml = T("oml", [128, 4])
    P.op("act", "activation", lbt[:], lbt[:], AF.Exp, reads=[lbt.b], writes=[lbt.b])
    P.op("dve", "tensor_reduce", lsum[:], lbt[:].rearrange("p l h -> p h l"), AX.X, ALU.add, reads=[lbt.b], writes=[lsum.b])
    P.op("dve", "reciprocal", lsum[:], lsum[:], reads=[lsum.b], writes=[lsum.b])
    P.op("dve", "tensor_tensor", lb[:], lbt[:, HG_LAYER, :], lsum[:], ALU.mult, reads=[lbt.b, lsum.b], writes=[lb.b])
    P.op("dve", "tensor_scalar", oml[:], lb[:], -1.0, 1.0, ALU.mult, ALU.add, reads=[lb.b], writes=[oml.b])

    PW = 1024
    NP_ = S // PW
    qs = T("qs", [128, S])
    vstg = T("vstg", [32, 16, 128])
    v32 = T("v32b", [32, NCH, 128], BF16)
    qtl = T("qtl", [128, S], BF16)
    ktl = T("ktl", [128, S], BF16)
    qht = T("qht", [128, S], BF16)
    kh32 = T("kh32", [32, NCH, 128], BF16)
    ec = T("ec", [128, NCH])
    fr = [T("fr%d" % i, [128, PW]) for i in range(2)]
    tf2 = [T("tf0", [128, PW])] * 2
    tg2 = [T("tg0", [128, PW])] * 2
    tk2 = [T("tk%d" % i, [128, PW]) for i in range(2)]
    tX2 = [T("tX%d" % i, [128, PW]) for i in range(2)]
    tD2 = [T("tD%d" % i, [128, PW]) for i in range(2)]
    tE4 = [T("tE%d" % i, [128, PW]) for i in range(4)]
    khT2 = [T("khT%d" % i, [128, PW], BF16) for i in range(2)]
    tpk = [T("tpk%d" % i, [32, 4, 128], BF16, psum=True) for i in range(2)]
    aps = [T("aps%d" % i, [32, 32], F32, psum=True) for i in range(2)]
    ops_ = [T("ops%d" % i, [32, 128], F32, psum=True) for i in range(2)]
    ups = [T("ups%d" % i, [128, 128], F32, psum=True) for i in range(2)]
    asb = [T("asb%d" % i, [32, 32], BF16) for i in range(2)]
    st2 = [T("st%d" % i, [128, 128]) for i in range(2)]
    stb = [T("stb%d" % i, [128, 128], BF16) for i in range(2)]
    ost = [T("ost%d" % i, [32, 8, 128]) for i in range(2)]

    def v3(t):
        return t[:, 0:PW].rearrange("p (c j) -> p c j", j=32)

    fi = 0
    oi = 0
    for h in range(4):
        hg = 4 * hh + h
        for p in range(NP_):
            sl = slice(p * PW, (p + 1) * PW)
            P.dma("sp", qs[:, sl], ufm[hg * 128:(hg + 1) * 128, sl], writes=[qs.b])
        P.op("act", "activation", qs[:], qs[:], AF.Silu, reads=[qs.b], writes=[qs.b])
        for p in range(8):
            src = utm[p * 512:(p + 1) * 512, hg * 128:(hg + 1) * 128].rearrange("(c j) e -> j c e", j=32)
            P.dma("sp", vstg[:], src, writes=[vstg.b])
            P.op("pool", "tensor_copy", v32[:, p * 16:(p + 1) * 16, :], vstg[:], reads=[vstg.b], writes=[v32.b])
        for dr in range(2):
            frow = (1 + dr) * 1024 + hg * 128
            for p in range(NP_):
                sl = slice(p * PW, (p + 1) * PW)
                frt = fr[fi % 2]
                tf, tg, tk, tX, tD, khT = tf2[fi % 2], tg2[fi % 2], tk2[fi % 2], tX2[fi % 2], tD2[fi % 2], khT2[fi % 2]
                tE = tE4[(fi % 2) * 2:(fi % 2) * 2 + 2]
                fi += 1
                P.dma("sp", frt[:], ufm[frow:frow + 128, sl], writes=[frt.b])
                P.op("act", "activation", frt[:], frt[:], AF.Sigmoid, reads=[frt.b], writes=[frt.b])
                P.op("dve", "tensor_scalar", tf[:], frt[:], oml[:, h:h + 1], lb[:, h:h + 1], ALU.mult, ALU.add,
                     reads=[frt.b, oml.b, lb.b], writes=[tf.b])
                P.op("act", "activation", tg[:], tf[:], AF.Ln, reads=[tf.b], writes=[tg.b])
                P.op("pool", "tensor_scalar", tk[:], tf[:], -1.0, 1.0, ALU.mult, ALU.add, reads=[tf.b], writes=[tk.b])
                P.op("dve", "tensor_tensor_scan", tX[:], smask[:], tg[:], 0.0, ALU.mult, ALU.add,
                     reads=[smask.b, tg.b], writes=[tX.b])
                X3 = v3(tX)
                if dr == 1:
                    P.op("dve", "tensor_tensor", v3(tD), X3[:, :, 31:32].to_broadcast([128, 32, 32]), X3, ALU.subtract,
                         reads=[tX.b], writes=[tD.b])
                    P.op("dve", "tensor_tensor", tX[:], tD[:], tg[:], ALU.add, reads=[tD.b, tg.b], writes=[tX.b])
                edge = 31 if dr == 0 else 0
                P.op("act", "activation", ec[:, p * 32:(p + 1) * 32], X3[:, :, edge], AF.Exp, reads=[tX.b], writes=[ec.b])
                P.op("dve", "tensor_tensor", v3(tD), X3, X3[:, :, 16:17].to_broadcast([128, 32, 32]), ALU.subtract,
                     reads=[tX.b], writes=[tD.b])
                P.op("act", "activation", tE[0][:], tD[:], AF.Exp, reads=[tD.b], writes=[tE[0].b])
                P.op("act", "activation", tE[1][:], tD[:], AF.Exp, scale=-1.0, reads=[tD.b], writes=[tE[1].b])
                P.op("pool", "tensor_tensor", qtl[:, sl], qs[:, sl], tE[0][:], ALU.mult, reads=[qs.b, tE[0].b], writes=[qtl.b])
                P.op("dve", "tensor_tensor", ktl[:, sl], tk[:], tE[1][:], ALU.mult, reads=[tk.b, tE[1].b], writes=[ktl.b])
                P.op("act", "activation", tE[0][:], tX[:], AF.Exp, reads=[tX.b], writes=[tE[0].b])
                P.op("pool", "tensor_tensor", qht[:, sl], qs[:, sl], tE[0][:], ALU.mult, reads=[qs.b, tE[0].b], writes=[qht.b])
                P.op("dve", "tensor_tensor", v3(tD), X3[:, :, edge:edge + 1].to_broadcast([128, 32, 32]), X3, ALU.subtract,
                     reads=[tX.b], writes=[tD.b])
                P.op("act", "activation", tE[1][:], tD[:], AF.Exp, reads=[tD.b], writes=[tE[1].b])
                P.op("dve", "tensor_tensor", khT[:], tk[:], tE[1][:], ALU.mult, reads=[tk.b, tE[1].b], writes=[khT.b])
                for g8 in range(PW // 128):
                    tp = tpk[g8 % 2]
                    for q4 in range(4):
                        c = g8 * 4 + q4
                        P.op("pe", "transpose", tp[:, q4, :], khT[:, c * 32:(c + 1) * 32], ident[:],
                             reads=[khT.b, ident.b], writes=[tp.b])
                    c0 = p * 32 + g8 * 4
                    P.op("act", "copy", kh32[:, c0:c0 + 4, :], tp[:], reads=[tp.b], writes=[kh32.b])
            P.op("dve", "memset", st2[1][:], 0.0, writes=[st2[1].b])
            P.op("pool", "memset", stb[0][:], 0.0, writes=[stb[0].b])
            order = list(range(NCH)) if dr == 0 else list(range(NCH - 1, -1, -1))
            def front(n, c):
                cs = slice(c * 32, (c + 1) * 32)
                ap_, up_, as_ = aps[n % 2], ups[n % 2], asb[n % 2]
                P.op("pe", "matmul", ap_[:], ktl[:, cs], qtl[:, cs], start=True, stop=True, reads=[ktl.b, qtl.b], writes=[ap_.b])
                P.op("pe", "matmul", up_[:], kh32[:, c, :], v32[:, c, :], start=True, stop=True, reads=[kh32.b, v32.b], writes=[up_.b])
                P.op("dve", "tensor_tensor", as_[:], ap_[:], masks[:, dr, :], ALU.mult, reads=[ap_.b, masks.b], writes=[as_.b])
            front(0, order[0])
            for n, c in enumerate(order):
                cs = slice(c * 32, (c + 1) * 32)
                ap_, op_, up_, as_ = aps[n % 2], ops_[n % 2], ups[n % 2], asb[n % 2]
                sb_cur, sb_nxt = stb[n % 2], stb[(n + 1) % 2]
                if n + 1 < NCH:
                    front(n + 1, order[n + 1])
                P.op("pe", "matmul", op_[:], as_[:], v32[:, c, :], start=True, stop=False, reads=[as_.b, v32.b], writes=[op_.b])
                P.op("pe", "matmul", op_[:], qht[:, cs], sb_cur[:], start=False, stop=True, reads=[qht.b, sb_cur.b], writes=[op_.b])
                os_ = ost[oi % 2]
                slot = c % 8
                s_old, s_new = st2[(n + 1) % 2], st2[n % 2]
                P.op("dve", "scalar_tensor_tensor", s_new[:], s_old[:], ec[:, c:c + 1], up_[:], ALU.mult, ALU.add,
                     reads=[s_old.b, ec.b, up_.b], writes=[s_new.b])
                P.op("act", "copy", sb_nxt[:], s_new[:], reads=[s_new.b], writes=[sb_nxt.b])
                P.op("act", "copy", os_[:, slot, :], op_[:], reads=[op_.b], writes=[os_.b])
                if n % 8 == 7:
                    cb = (c // 8) * 8
                    dst = od_d[dr][cb * 32:(cb + 8) * 32, h * 128:(h + 1) * 128].rearrange("(c i) e -> i c e", i=32)
                    P.dma("sp", dst, os_[:], reads=[os_.b], writes=[od_b])
                    oi += 1
    fa = [T("fa%d" % i, [128, 512]) for i in range(2)]
    fb = [T("fb%d" % i, [128, 512]) for i in range(2)]
    fo = [T("fo%d" % i, [128, 512]) for i in range(2)]
    fss = [T("fss%d" % i, [128, 4]) for i in range(2)]
    fj = T("fj", [128, 128])
    for tt in range(NST):
        i = tt % 2
        sl = slice(tt * 128, (tt + 1) * 128)
        P.dma("sp", fa[i][:], od_d[0][sl], reads=[od_b], writes=[fa[i].b])
        P.dma("sp", fb[i][:], od_d[1][sl], reads=[od_b], writes=[fb[i].b])
        P.op("dve", "tensor_tensor", fa[i][:], fa[i][:], fb[i][:], ALU.add, reads=[fa[i].b, fb[i].b], writes=[fa[i].b])
        for h in range(4):
            P.op("act", "activation", fj[:], fa[i][:, h * 128:(h + 1) * 128], AF.Square, accum_out=fss[i][:, h:h + 1],
                 reads=[fa[i].b], writes=[fj.b, fss[i].b])
        P.op("act", "activation", fss[i][:], fss[i][:], AF.Sqrt, bias=EPS, scale=1.0 / 128, reads=[fss[i].b], writes=[fss[i].b])
        P.op("dve", "reciprocal", fss[i][:], fss[i][:], reads=[fss[i].b], writes=[fss[i].b])
        for h in range(4):
            hs = slice(h * 128, (h + 1) * 128)
            P.op("dve", "scalar_tensor_tensor", fo[i][:, hs], fa[i][:, hs], fss[i][:, h:h + 1], ng[:, hs], ALU.mult, ALU.mult,
                 reads=[fa[i].b, fss[i].b, ng.b], writes=[fo[i].b])
        P.dma("sp", o_d[sl, hh * 512:(hh + 1) * 512], fo[i][:], reads=[fo[i].b], writes=[o_b])


def hg_consts():
    sm = np.ones((128, 1024), np.float32)
    sm[:, ::32] = 0.0
    j = np.arange(32)[:, None]
    i = np.arange(32)[None, :]
    masks = np.stack([(j <= i), (j >= i)], 1).astype(np.float32)
    return sm, masks


def emit_at(P, A, hh, io):
    A.reset()
    T = A.T
    utm, oT_d = io["utm"], io["oT"]
    oT_b = Buf("oT_d")
    ident = T("ident", [128, 128], BF16)
    gain = T("gain", [128, 12, 128])
    ones = T("ones", [128, 128], BF16)
    P.dma("sp", ident[:], io["ident"], writes=[ident.b])
    P.dma("sp", gain[:], io["gain"], writes=[gain.b])
    P.op("pool", "memset", ones[:], 1.0, writes=[ones.b])
    qT = T("qT", [128, 8, S], BF16)
    kT = T("kT", [128, 4, S], BF16)
    vb = T("vb", [128, NST, 4, 128], BF16)
    qT_b = [Buf("qT%d" % i) for i in range(NST)]
    kT_b = [Buf("kT%d" % i) for i in range(NST)]
    vb_b = [Buf("vb%d" % i) for i in range(NST)]
    qk = [T("qk%d" % i, [128, 12, 128]) for i in range(2)]
    vt = [T("vt%d" % i, [128, 4, 128]) for i in range(2)]
    cs = [T("cs%d" % i, [128, 2, 2, 32]) for i in range(2)]
    junk = T("junk", [128, 128])
    ssq = [T("ssq%d" % i, [128, 12]) for i in range(2)]
    qn = [T("qn0", [128, 12, 128])] * 2
    ta = [T("ta0", [128, 12, 2, 32])] * 2
    tb = [T("tb0", [128, 12, 2, 32])] * 2
    tc = [T("tc0", [128, 12, 2, 32])] * 2
    td = [T("td0", [128, 12, 2, 32])] * 2
    qr = [T("qr%d" % i, [128, 12, 128], BF16) for i in range(2)]
    tpq = T("tpq", [128, 8, 128], BF16, psum=True)
    tpk = T("tpk", [128, 4, 128], BF16, psum=True)
    for tt in range(NST):
        i = tt % 2
        sl = slice(tt * 128, (tt + 1) * 128)
        qkt, vtt, cst, sst, qnt, qrt = qk[i], vt[i], cs[i], ssq[i], qn[i], qr[i]
        P.dma("sp", qkt[:, 0:8, :], utm[sl, hh * 1024:(hh + 1) * 1024].rearrange("t (h e) -> t h e", h=8), writes=[qkt.b])
        P.dma("sp", qkt[:, 8:12, :], utm[sl, 2048 + hh * 512:2048 + (hh + 1) * 512].rearrange("t (h e) -> t h e", h=4), writes=[qkt.b])
        P.dma("sp", vtt[:], utm[sl, 3072 + hh * 512:3072 + (hh + 1) * 512].rearrange("t (h e) -> t h e", h=4), writes=[vtt.b])
        P.dma("sp", cst[:, 0], io["cos"][sl], writes=[cst.b])
        P.dma("sp", cst[:, 1], io["sin"][sl], writes=[cst.b])
        P.op("pool", "tensor_copy", vb[:, tt], vtt[:], reads=[vtt.b], writes=[vb_b[tt]])
        for h in range(12):
            P.op("act", "activation", junk[:], qkt[:, h, :], AF.Square, accum_out=sst[:, h:h + 1], reads=[qkt.b], writes=[junk.b, sst.b])
        P.op("act", "activation", sst[:], sst[:], AF.Sqrt, bias=EPS, scale=1.0 / 128, reads=[sst.b], writes=[sst.b])
        P.op("dve", "reciprocal", sst[:], sst[:], reads=[sst.b], writes=[sst.b])
        for h in range(12):
            P.op("dve", "scalar_tensor_tensor", qnt[:, h, :], qkt[:, h, :], sst[:, h:h + 1], gain[:, h, :], ALU.mult, ALU.mult,
                 reads=[qkt.b, sst.b, gain.b], writes=[qnt.b])
        xv = qnt[:].rearrange("p h (a two f) -> p h a two f", a=2, two=2)
        ov = qrt[:].rearrange("p h (a two f) -> p h a two f", a=2, two=2)
        x1, x2 = xv[:, :, :, 0, :], xv[:, :, :, 1, :]
        cb = cst[:, 0].unsqueeze(1).to_broadcast([128, 12, 2, 32])
        sb = cst[:, 1].unsqueeze(1).to_broadcast([128, 12, 2, 32])
        a_, b_, c_, d_ = ta[i], tb[i], tc[i], td[i]
        P.op("dve", "tensor_tensor", a_[:], x1, cb, ALU.mult, reads=[qnt.b, cst.b], writes=[a_.b])
        P.op("pool", "tensor_tensor", b_[:], x2, sb, ALU.mult, reads=[qnt.b, cst.b], writes=[b_.b])
        P.op("dve", "tensor_tensor", c_[:], x2, cb, ALU.mult, reads=[qnt.b, cst.b], writes=[c_.b])
        P.op("pool", "tensor_tensor", d_[:], x1, sb, ALU.mult, reads=[qnt.b, cst.b], writes=[d_.b])
        P.op("dve", "tensor_tensor", ov[:, :, :, 0, :], a_[:], b_[:], ALU.subtract, reads=[a_.b, b_.b], writes=[qrt.b])
        P.op("pool", "tensor_tensor", ov[:, :, :, 1, :], c_[:], d_[:], ALU.add, reads=[c_.b, d_.b], writes=[qrt.b])
        for h in range(8):
            P.op("pe", "transpose", tpq[:, h, :], qrt[:, h, :], ident[:], reads=[qrt.b, ident.b], writes=[tpq.b])
        for h in range(4):
            P.op("pe", "transpose", tpk[:, h, :], qrt[:, 8 + h, :], ident[:], reads=[qrt.b, ident.b], writes=[tpk.b])
        P.op("act", "copy", qT[:, :, sl], tpq[:], reads=[tpq.b], writes=[qT_b[tt]])
        P.op("act", "copy", kT[:, :, sl], tpk[:], reads=[tpk.b], writes=[kT_b[tt]])
    sps = [T("sps%d" % i, [128, 512], F32, psum=True) for i in range(2)]
    ops_ = [T("ops%d" % i, [128, 512], F32, psum=True) for i in range(2)]
    dps = [T("dps%d" % i, [128, 512], F32, psum=True) for i in range(2)]
    NPT = 6
    pT = [T("pT%d" % i, [128, 512], BF16) for i in range(NPT)]
    psum4 = [T("psum4_%d" % i, [128, 512], BF16) for i in range(2)]
    rec = [T("rec%d" % i, [128, 512]) for i in range(2)]
    osb = [T("osb%d" % i, [128, 512]) for i in range(2)]
    scale = 128.0 ** -0.5
    it = 0
    for h in range(8):
        kv = h // 2
        for qb in range(S // 512):
            qsl = slice(qb * 512, (qb + 1) * 512)
            q_reads = [qT_b[qb * 4 + j] for j in range(4)]
            op_, dp_ = ops_[it % 2], dps[it % 2]

            def mm1(kt):
                sp_ = sps[kt % 2]
                P.op("pe", "matmul", sp_[:], kT[:, kv, kt * 128:(kt + 1) * 128], qT[:, h, qsl], start=True, stop=True,
                     reads=[kT_b[kt]] + q_reads, writes=[sp_.b])

            def rest(kt):
                sp_ = sps[kt % 2]
                p_ = pT[kt % NPT]
                P.op("act", "activation", p_[:], sp_[:], AF.Exp, scale=scale, reads=[sp_.b], writes=[p_.b])
                P.op("pe", "matmul", op_[:], vb[:, kt, kv, :], p_[:], start=(kt == 0), stop=(kt == NST - 1),
                     reads=[vb_b[kt], p_.b], writes=[op_.b])
                if kt % 4 == 1:
                    ps_ = psum4[(kt // 4) % 2]
                    P.op("dve", "tensor_tensor", ps_[:], pT[(kt - 1) % NPT][:], p_[:], ALU.add,
                         reads=[pT[(kt - 1) % NPT].b, p_.b], writes=[ps_.b])
                elif kt % 4 in (2, 3):
                    ps_ = psum4[(kt // 4) % 2]
                    P.op("dve", "tensor_tensor", ps_[:], ps_[:], p_[:], ALU.add, reads=[ps_.b, p_.b], writes=[ps_.b])
                    if kt % 4 == 3:
                        P.op("pe", "matmul", dp_[:], ones[:], ps_[:], start=(kt == 3), stop=(kt == NST - 1),
                             reads=[ones.b, ps_.b], writes=[dp_.b])
            mm1(0)
            for kt in range(NST):
                if kt + 1 < NST:
                    mm1(kt + 1)
                rest(kt)
            r_, o_ = rec[it % 2], osb[it % 2]
            P.op("dve", "reciprocal", r_[:], dp_[:], reads=[dp_.b], writes=[r_.b])
            P.op("dve", "tensor_tensor", o_[:], op_[:], r_[:], ALU.mult, reads=[op_.b, r_.b], writes=[o_.b])
            hgl = hh * 8 + h
            P.dma("sp", oT_d[hgl * 128:(hgl + 1) * 128, qsl], o_[:], reads=[o_.b], writes=[oT_b])
            it += 1


def rope_tables():
    row = np.repeat(np.arange(S // 64), 64).astype(np.float32)
    col = (np.arange(S) % 64).astype(np.float32)
    inv = (np.float32(10000.0) ** (-np.arange(0, 64, 2, dtype=np.float32) / np.float32(64))).astype(np.float32)
    ang = np.stack([row[:, None] * inv, col[:, None] * inv], 1).astype(np.float32)
    return np.cos(ang).astype(np.float32), np.sin(ang).astype(np.float32)


DL_PAIRS = ((128, 1), (512, 4), (2048, 16))
NEG = -30000.0


def dl_geom(d):
    Ls = S // d
    nt = Ls // 128
    return Ls, nt, d * (Ls + 128), d * (nt + 1)


def emit_dl(P, A, hh, io):
    A.reset()
    T = A.T
    ufm, utm, nd_d, o_d = io["ufm"], io["utm"], io["nd"], io["o"]
    nd_b = Buf("nd_d")
    o_b = Buf("o_d")
    bias = T("bias", [128, 24, 256])
    P.dma("sp", bias[:], io["dl_bias"][hh], writes=[bias.b])
    KMAX = 6144
    VMAX = 48
    qst = T("qst", [64, S])
    kst = T("kst", [64, S])
    vst = T("vst", [128, VMAX, 65])
    qb_ = [T("qb%d" % i, [64, S], BF16) for i in range(2)]
    kb_ = [T("kb%d" % i, [64, KMAX], BF16) for i in range(2)]
    vb_ = [T("vb%d" % i, [128, VMAX, 65], BF16) for i in range(2)]
    nds = [T("nds%d" % i, [128, 32, 65]) for i in range(2)]
    NPB = 4
    sps = [T("sps%d" % i, [128, 2, 128], F32, psum=True) for i in range(NPB)]
    ops_ = [T("ops%d" % i, [128, 65], F32, psum=True) for i in range(NPB)]
    tmp = [T("tmp%d" % i, [128, 256]) for i in range(NPB)]
    pb = [T("pb%d" % i, [128, 2, 128], BF16) for i in range(NPB)]
    items = [(g, d, h) for g, (_, d) in enumerate(DL_PAIRS) for h in range(8)]

    def prep(it):
        g, d, h = items[it]
        Ls, nt, klen, vt_n = dl_geom(d)
        i = it % 2
        hgl = 8 * hh + h
        qrow = (g * 2) * 1024 + hgl * 64
        krow = (g * 2 + 1) * 1024 + hgl * 64
        P.dma("sp", qst[:], ufm[qrow:qrow + 64, :], writes=[qst.b])
        P.dma("sp", kst[:], ufm[krow:krow + 64, :], writes=[kst.b])
        P.op("act", "copy", qb_[i][:].rearrange("p (r m) -> p r m", r=d), qst[:].rearrange("p (m r) -> p r m", r=d),
             reads=[qst.b], writes=[qb_[i].b])
        P.op("pool", "memset", kb_[i][:, 0:klen], 0.0, writes=[kb_[i].b])
        P.op("act", "copy", kb_[i][:, 0:klen].rearrange("p (r m) -> p r m", r=d)[:, :, 64:64 + Ls],
             kst[:].rearrange("p (m r) -> p r m", r=d), reads=[kst.b], writes=[kb_[i].b])
        v4 = vst[:, 0:vt_n, :].rearrange("p (r i) c -> p r i c", r=d)
        vsrc = utm[:, g * 1024 + hgl * 64:g * 1024 + (hgl + 1) * 64].rearrange("(m r) e -> r m e", r=d)
        P.op("pool", "memset", vst[:, 0:vt_n, :], 0.0, writes=[vst.b])
        if nt > 1:
            P.op("pool", "memset", v4[:, :, 1:nt, 64:65], 1.0, writes=[vst.b])
        P.op("pool", "memset", v4[64:128, :, 0, 64:65], 1.0, writes=[vst.b])
        P.op("pool", "memset", v4[0:64, :, nt, 64:65], 1.0, writes=[vst.b])
        if nt - 1 == 1:
            P.dma("sp", v4[:, :, 1, 0:64], vsrc[:, 64:192, :].rearrange("r k e -> k r e"), writes=[vst.b])
        elif nt > 1:
            for r in range(d):
                P.dma("sp", v4[:, r, 1:nt, 0:64], vsrc[r][64:64 + 128 * (nt - 1), :].rearrange("(i k) e -> k i e", k=128),
                      writes=[vst.b])
        P.dma("sp", v4[64:128, :, 0, 0:64], vsrc[:, 0:64, :].rearrange("r k e -> k r e"), writes=[vst.b])
        P.dma("sp", v4[0:64, :, nt, 0:64], vsrc[:, Ls - 64:Ls, :].rearrange("r k e -> k r e"), writes=[vst.b])
        P.op("dve", "tensor_copy", vb_[i][:, 0:vt_n, :], vst[:, 0:vt_n, :], reads=[vst.b], writes=[vb_[i].b])

    ucnt = [0]

    def compute(it):
        g, d, h = items[it]
        Ls, nt, klen, vt_n = dl_geom(d)
        i = it % 2
        nd_ = nds[i]
        u0 = ucnt[0]
        ucnt[0] += 32

        def geo(j):
            r, jj = divmod(j, nt)
            return r * Ls + jj * 128, r * (Ls + 128) + jj * 128, r * (nt + 1) + jj

        def qk(j):
            n0, kbase, _ = geo(j)
            sp_ = sps[(u0 + j) % NPB]
            for c in range(2):
                P.op("pe", "matmul", sp_[:, c, :], kb_[i][:, kbase + c * 128:kbase + (c + 1) * 128],
                     qb_[i][:, n0:n0 + 128], start=True, stop=True, reads=[kb_[i].b, qb_[i].b], writes=[sp_.b])
            tm_, p_ = tmp[(u0 + j) % NPB], pb[(u0 + j) % NPB]
            P.op("dve", "scalar_tensor_tensor", tm_[:], sp_[:].rearrange("p c q -> p (c q)"), 0.125,
                 bias[:, g * 8 + h, :], ALU.mult, ALU.add, reads=[sp_.b, bias.b], writes=[tm_.b])
            P.op("act", "activation", p_[:].rearrange("p c q -> p (c q)"), tm_[:], AF.Exp, reads=[tm_.b], writes=[p_.b])

        def pv(j):
            _, _, vbase = geo(j)
            op_, p_ = ops_[(u0 + j) % NPB], pb[(u0 + j) % NPB]
            for c in range(2):
                P.op("pe", "matmul", op_[:], p_[:, c, :], vb_[i][:, vbase + c, :], start=(c == 0), stop=(c == 1),
                     reads=[p_.b, vb_[i].b], writes=[op_.b])
            P.op("dve", "tensor_copy", nd_[:, j, :], op_[:], reads=[op_.b], writes=[nd_.b])
        AHEAD = 2
        for j in range(min(AHEAD, 32)):
            qk(j)
        for j in range(32):
            if j + AHEAD < 32:
                qk(j + AHEAD)
            pv(j)

    def store(it):
        g, d, h = items[it]
        Ls, nt, klen, vt_n = dl_geom(d)
        nd_ = nds[it % 2]
        ndv = nd_d.rearrange("(m r) g h c -> r m g h c", r=d)
        for r in range(d):
            dst = ndv[r][:, g, h, :].rearrange("(jj q) c -> q jj c", q=128)
            for j0 in range(0, nt, 8):
                j1 = min(nt, j0 + 8)
                P.dma("sp", dst[:, j0:j1, :], nd_[:, r * nt + j0:r * nt + j1, :], reads=[nd_.b], writes=[nd_b])

    prep(0)
    for it in range(len(items)):
        if it + 1 < len(items):
            prep(it + 1)
        compute(it)
        store(it)
    mt = [T("mt%d" % i, [128, 3, 8, 65]) for i in range(2)]
    ms = [T("ms%d" % i, [128, 8, 65]) for i in range(2)]
    mr = [T("mr%d" % i, [128, 8]) for i in range(2)]
    mo = [T("mo%d" % i, [128, 8, 64]) for i in range(2)]
    for tt in range(NST):
        i = tt % 2
        sl = slice(tt * 128, (tt + 1) * 128)
        P.dma("sp", mt[i][:], nd_d[sl], reads=[nd_b], writes=[mt[i].b])
        P.op("dve", "tensor_tensor", ms[i][:], mt[i][:, 0], mt[i][:, 1], ALU.add, reads=[mt[i].b], writes=[ms[i].b])
        P.op("dve", "tensor_tensor", ms[i][:], ms[i][:], mt[i][:, 2], ALU.add, reads=[mt[i].b, ms[i].b], writes=[ms[i].b])
        P.op("dve", "reciprocal", mr[i][:], ms[i][:, :, 64], reads=[ms[i].b], writes=[mr[i].b])
        P.op("dve", "tensor_tensor", mo[i][:], ms[i][:, :, 0:64], mr[i][:].unsqueeze(2).to_broadcast([128, 8, 64]),
             ALU.mult, reads=[ms[i].b, mr[i].b], writes=[mo[i].b])
        P.dma("sp", o_d[sl, hh * 512:(hh + 1) * 512].rearrange("t (h e) -> t h e", h=8), mo[i][:], reads=[mo[i].b], writes=[o_b])


def t5_bucket_np(rel):
    half, exact = 16, 8
    n = np.abs(rel)
    large = exact + (np.log(np.maximum(n, 1).astype(np.float32) / np.float32(exact))
                     / np.float32(np.log(1024 / exact)) * np.float32(half - exact)).astype(np.int32)
    large = np.minimum(large, half - 1)
    return np.where(rel > 0, half, 0) + np.where(n < exact, n, large)


def dl_bias_tables(rel_bias, heads):
    kk = np.arange(128)[:, None]
    qq = np.arange(128)[None, :]
    out = np.empty((128, 24, 256), np.float32)
    for g, (_, d) in enumerate(DL_PAIRS):
        bA = t5_bucket_np((kk - 64 - qq) * d)
        bB = t5_bucket_np((64 + kk - qq) * d)
        for hi, h in enumerate(heads):
            out[:, g * 8 + hi, 0:128] = np.where(kk >= qq, rel_bias[bA, h], NEG)
            out[:, g * 8 + hi, 128:256] = np.where(kk <= qq, rel_bias[bB, h], NEG)
    return out


def build_fused():
    P = Prog()
    nc = P.nc
    A = Arena(P, kib=207)
    dr = P.dram

    def scratch(name, shape):
        return nc.dram_tensor(name, list(shape), F32, kind="Internal").ap()

    def db(ap):
        return {"ap": ap, "b": Buf("d")}
    x_in = dr("x", [S, D])
    out_d = dr("out", [S, D], kind="ExternalOutput")
    ident = dr("ident", [128, 128], BF16)
    io = {
        "ident": ident, "identf": dr("identf", [128, 128]),
        "convw": dr("convw", [2, 128, 2, 6, 7]), "convb": dr("convb", [2, 128, 2, 6]),
        "dtb_rep": dr("dtb_rep", [2, 128, 2, 2, 8]), "alog_rep": dr("alog_rep", [2, 128, 2, 2, 8]),
        "d_rep": dr("d_rep", [2, 128, 2, 8]), "ssd_ng_rep": dr("ssd_ng_rep", [2, 128, 2, 512]),
        "tmat": dr("tmat", [128, 2, 128]), "negm": dr("negm", [128, 2, 4, 128]),
        "lbT": dr("lbT", [2, 128, 4, 4]), "hg_ng_rep": dr("hg_ng_rep", [2, 128, 512]),
        "smask": dr("smask", [128, 1024]), "masks": dr("masks", [32, 2, 32]),
        "cos": dr("cos", [S, 2, 32]), "sin": dr("sin", [S, 2, 32]), "gain": dr("gain", [128, 12, 128]),
        "dl_bias": dr("dl_bias", [2, 128, 24, 256]),
    }
    NIN = (5184, 5120, 6144, 10240)
    NFM = (3072, 3072, 2048, 6144)
    WOUT = (2048, 1024, 2048, 1024)
    g_rep = [dr("g_rep%d" % l, [128, D]) for l in range(4)]
    g_fin = dr("g_fin", [128, D])
    w_in = [dr("w_in%d" % l, [D, NIN[l]]) for l in range(4)]
    w_out = [dr("w_out%d" % l, [WOUT[l], D]) for l in range(4)]
    ufm = [scratch("ufm%d" % i, [6144, S]) for i in range(2)]
    utm = [scratch("utm%d" % i, [S, 4096]) for i in range(2)]
    o_tm = scratch("o_tm", [S, 2048])
    oT_fm = scratch("oT_fm", [2048, S])
    xs = [scratch("xs%d" % i, [S, D]) for i in range(2)]
    od = scratch("od", [2, S, 512])
    nd = scratch("nd", [S, 3, 8, 65])

    def lin(layer, x_src, c, x_dst, final=False):
        for t0 in (0, TT):
            a = None
            if layer < 4:
                a = {"g_rep": g_rep[layer], "w_in": w_in[layer], "n_fm": NFM[layer], "n_tm": NIN[layer] - NFM[layer],
                     "ufm": ufm[layer % 2], "utm": utm[layer % 2]}
            cc = None
            if c is not None:
                cc = dict(c)
                cc["o_b"] = Buf("o")
                cc["gate_b"] = Buf("g")
                cc["xout"] = db(x_dst) if x_dst is not None else None
            fin = {"g_rep": g_fin, "out": out_d} if final else None
            emit_lin(P, A, t0, ident, db(x_src), c=cc, a=a, fin=fin)

    lin(0, x_in, None, None)
    for hh in range(2):
        emit_ssd(P, A, hh, dict(io, ufm=ufm[0], utm=utm[0], o=o_tm))
    lin(1, x_in, {"mode": "tm", "W": 2048, "o": o_tm, "gate": None, "w_out": w_out[0]}, xs[0])
    for hh in range(2):
        emit_hg(P, A, hh, dict(io, ufm=ufm[1], utm=utm[1], od=od, o=o_tm))
    lin(2, xs[0], {"mode": "tm", "W": 1024, "o": o_tm, "gate": utm[1][:, 1024:2048], "w_out": w_out[1]}, xs[1])
    for hh in range(2):
        emit_at(P, A, hh, dict(io, utm=utm[0], oT=oT_fm))
    lin(3, xs[1], {"mode": "fm", "W": 2048, "o": oT_fm, "gate": ufm[0][0:2048, :], "w_out": w_out[2]}, xs[0])
    for hh in range(2):
        emit_dl(P, A, hh, dict(io, ufm=ufm[1], utm=utm[1], nd=nd, o=o_tm))
    lin(4, xs[0], {"mode": "tm", "W": 1024, "o": o_tm, "gate": utm[1][:, 3072:4096], "w_out": w_out[3]}, None, final=True)
    print("fused program ops:", P.n_ops, dict(P.ecnt))
    return P.emit()


def fused_inputs(x, norm_g, final_g, rel_bias, hgrn_lb,
                 ssd_w_in, ssd_conv_w, ssd_conv_b, ssd_dt_bias, ssd_a_log, ssd_d, ssd_norm_g, ssd_w_out,
                 hg_w_in, hg_norm_g, hg_w_out, at_w_in, at_q_norm_g, at_k_norm_g, at_w_out, dl_w_in, dl_w_out):
    f = lambda a: np.ascontiguousarray(np.asarray(a, dtype=np.float32))
    rep = lambda v: np.ascontiguousarray(np.broadcast_to(f(v)[None], (128,) + tuple(np.shape(v))))
    tm, nm = ssd_consts()
    sm, masks = hg_consts()
    cos, sin = rope_tables()
    w0 = f(ssd_w_in)[0]
    w2 = f(at_w_in)[0]
    w3 = f(dl_w_in)[0]
    fm3 = np.concatenate([np.arange(g * 3072 + s * 1024, g * 3072 + (s + 1) * 1024) for g in range(3) for s in range(2)])
    tm3 = np.concatenate([np.arange(g * 3072 + 2048, g * 3072 + 3072) for g in range(3)] + [np.arange(9216, 10240)])
    small = [ssd_small(hh, f(ssd_conv_w)[0], f(ssd_conv_b)[0], f(ssd_dt_bias)[0], f(ssd_a_log)[0], f(ssd_d)[0], f(ssd_norm_g)[0])
             for hh in range(2)]
    lb4 = f(hgrn_lb).reshape(4, 8, 128)
    gain = np.concatenate([np.broadcast_to(f(at_q_norm_g)[0][None, None], (128, 8, 128)),
                           np.broadcast_to(f(at_k_norm_g)[0][None, None], (128, 4, 128))], 1)
    common = {
        "ident": bf16_np(np.eye(128)), "identf": np.eye(128, dtype=np.float32),
        "convw": np.stack([s_[0] for s_ in small]), "convb": np.stack([s_[1] for s_ in small]),
        "dtb_rep": np.stack([s_[2] for s_ in small]), "alog_rep": np.stack([s_[3] for s_ in small]),
        "d_rep": np.stack([s_[4] for s_ in small]), "ssd_ng_rep": np.stack([s_[5] for s_ in small]),
        "tmat": tm, "negm": nm,
        "lbT": np.stack([np.ascontiguousarray(lb4[:, 4 * hh:4 * hh + 4].transpose(2, 0, 1)) for hh in range(2)]),
        "hg_ng_rep": np.stack([rep(f(hg_norm_g)[0][hh * 512:(hh + 1) * 512]) for hh in range(2)]),
        "smask": sm, "masks": masks, "cos": cos, "sin": sin, "gain": np.ascontiguousarray(gain, dtype=np.float32),
        "dl_bias": np.stack([dl_bias_tables(f(rel_bias), list(range(8 * hh, 8 * hh + 8))) for hh in range(2)]),
        "g_rep0": rep(f(norm_g)[0]), "g_rep1": rep(f(norm_g)[1]), "g_rep2": rep(f(norm_g)[2]), "g_rep3": rep(f(norm_g)[3]),
        "g_fin": rep(f(final_g)),
        "w_in0": np.ascontiguousarray(np.concatenate([w0[:, 2048:5120], w0[:, 0:2048], w0[:, 5120:5184]], 1)),
        "w_in1": f(hg_w_in)[0],
        "w_in2": np.ascontiguousarray(np.concatenate([w2[:, 4096:6144], w2[:, 0:4096]], 1)),
        "w_in3": np.ascontiguousarray(np.concatenate([w3[:, fm3], w3[:, tm3]], 1)),
        "w_out0": f(ssd_w_out)[0], "w_out1": f(hg_w_out)[0], "w_out2": f(at_w_out)[0], "w_out3": f(dl_w_out)[0],
    }
    x = f(x)
    return [dict(common, x=x[core % x.shape[0]]) for core in range(8)]


def kernel(**inputs):
    B = np.asarray(inputs["x"]).shape[0]
    maps = fused_inputs(**inputs)
    nc = build_fused()
    res = run_bass_kernel_spmd(nc, maps, core_ids=list(range(8)))
    return np.stack([res.results[b]["out"] for b in range(B)], 0)
```

```python
from contextlib import ExitStack
import numpy as np
import ml_dtypes
import concourse.bass as bass
import concourse.mybir as mybir
from concourse.bass_utils import run_bass_kernel_spmd

F32 = mybir.dt.float32
BF16 = mybir.dt.bfloat16
I32 = mybir.dt.int32
AF = mybir.ActivationFunctionType
ALU = mybir.AluOpType
AX = mybir.AxisListType

ENGS = ("pe", "act", "dve", "pool", "sp")
DMA_SLOTS = 6


class Buf:
    __slots__ = ("name", "w", "r", "excl")

    def __init__(self, name, excl=False):
        self.name = name
        self.excl = excl
        self.w = None
        self.r = []


class Prog:
    def __init__(self):
        self.nc = bass.Bass("TRN2", target_bir_lowering=False)
        self.es = ExitStack()
        self.q = {e: [] for e in ENGS}
        self.sem = {}
        self.seen = {e: {} for e in ENGS}
        self.ecnt = {e: 0 for e in ENGS}
        self.dma_i = {qe: 0 for qe in ("sp", "act", "pool")}
        self.dcnt = {}
        self.n_ops = 0

    EPOCH = 20000

    def _sem(self, k):
        if k not in self.sem:
            self.sem[k] = self.es.enter_context(self.nc.semaphore("s%d" % len(self.sem)))
        return self.sem[k]

    def dram(self, name, shape, dtype=F32, kind="ExternalInput"):
        return self.nc.dram_tensor(name, list(shape), dtype, kind=kind).ap()

    def sbuf(self, name, shape, dtype=F32):
        return self.es.enter_context(self.nc.sbuf_tensor(name, list(shape), dtype))

    def psum(self, name, shape, dtype=F32):
        return self.es.enter_context(self.nc.psum_tensor(name, list(shape), dtype))

    def _deps(self, reads, writes):
        need = {}
        def add(ev):
            if ev is None:
                return
            k, v = ev
            if need.get(k, 0) < v:
                need[k] = v
        for b in reads:
            add(b.w)
            if b.excl:
                for ev in b.r:
                    add(ev)
        for b in writes:
            add(b.w)
            for ev in b.r:
                add(ev)
        return need

    def _commit(self, ev, reads, writes):
        for b in reads:
            if b.excl:
                b.r = [ev]
            else:
                b.r.append(ev)
        for b in writes:
            b.w = ev
            b.r = []

    def _waits(self, eng, need):
        ws = []
        seen = self.seen[eng]
        for k, v in need.items():
            if eng == "pe" and k[0] == "pe":
                continue
            if seen.get(k, 0) < v:
                seen[k] = v
                ws.append((k, v))
        return ws

    def op(self, eng, meth, *args, reads=(), writes=(), **kw):
        fn = (meth, args, kw)
        need = self._deps(reads, writes)
        ws = self._waits(eng, need)
        n = self.ecnt[eng]
        self.ecnt[eng] = n + 1
        k = (eng, n // self.EPOCH)
        ev = (k, n % self.EPOCH + 1)
        self.q[eng].append((ws, fn, k, 1))
        self._commit(ev, reads, writes)
        self.n_ops += 1
        return ev

    def dma(self, qe, out, in_, reads=(), writes=(), **kw):
        need = self._deps(reads, writes)
        i = self.dma_i[qe]
        self.dma_i[qe] = i + 1
        slot = i % DMA_SLOTS
        ep = (i // DMA_SLOTS) // 1000
        k = ("dma", qe, slot, ep)
        c = self.dcnt.get(k, 0)
        if c > 0:
            need[k] = max(need.get(k, 0), c)
        elif ep > 0:
            kp = ("dma", qe, slot, ep - 1)
            need[kp] = max(need.get(kp, 0), self.dcnt[kp])
        ws = self._waits(qe, need)
        self.dcnt[k] = c + 16
        ev = (k, c + 16)
        self.q[qe].append((ws, ("dma_start", (), dict(out=out, in_=in_, **kw)), k, 16))
        self._commit(ev, reads, writes)
        self.n_ops += 1
        return ev

    def wait_all_dma(self):
        for qe in ("sp", "act", "pool"):
            need = {k: c for k, c in self.dcnt.items() if k[1] == qe}
            ws = self._waits(qe, need)
            if ws:
                self.q[qe].append((ws, None, None, 0))


    def fence(self):
        need = {}
        for e in ENGS:
            n = self.ecnt[e]
            if n > 0:
                need[(e, (n - 1) // self.EPOCH)] = (n - 1) % self.EPOCH + 1
        for k, c in self.dcnt.items():
            need[k] = c
        for e in ENGS:
            ws = self._waits(e, dict(need))
            if ws:
                self.q[e].append((ws, None, None, 0))

    def emit(self):
        self.wait_all_dma()
        nc = self.nc
        engobj = {"pe": "tensor", "act": "scalar", "dve": "vector", "pool": "gpsimd", "sp": "sync"}
        with nc.Block() as block:
            for e in ENGS:
                items = self.q[e]

                def body(eng, items=items):
                    for ws, fn, sk, inc in items:
                        for k, v in ws:
                            eng.wait_ge(self._sem(k), v)
                        if fn is not None:
                            ins = getattr(eng, fn[0])(*fn[1], **fn[2])
                            ins.then_inc(self._sem(sk), inc)
                getattr(block, engobj[e])(body)
        self.es.close()
        return nc


class T:
    def __init__(self, P, name, shape, dtype=F32, psum=False):
        if psum:
            esz = 4 if dtype == F32 else 2
            n = int(np.prod(shape[1:]))
            assert n * esz <= 2048
            h = P.psum("t_" + name, [128, 2048 // esz], dtype)
            v = h[0:shape[0], 0:n]
            if len(shape) == 3:
                v = v.rearrange("p (a b) -> p a b", b=shape[2])
            elif len(shape) == 4:
                v = v.rearrange("p (a b c) -> p a b c", b=shape[2], c=shape[3])
            self.t = v
        else:
            self.t = P.sbuf("t_" + name, shape, dtype)
        self.b = Buf(name, excl=psum)

    def __getitem__(self, k):
        return self.t[k]


def bf16_np(a):
    return np.asarray(a, dtype=np.float32).astype(ml_dtypes.bfloat16)


class TV:
    def __init__(self, ap, name, excl):
        self.t = ap
        self.b = Buf(name, excl=excl)

    def __getitem__(self, k):
        return self.t[k]


class Arena:
    def __init__(self, P, kib=188):
        self.P = P
        self.n32 = kib * 256
        self.h = P.sbuf("arena", [128, self.n32], F32)
        self.banks = [P.psum("bank%d" % i, [128, 512], F32) for i in range(8)]
        self.off = 0
        self.nb = 0

    def reset(self):
        self.P.fence()
        self.off = 0
        self.nb = 0

    def T(self, name, shape, dtype=F32, psum=False):
        esz = 4 if dtype == F32 else 2
        n = int(np.prod(shape[1:]))
        n32 = (n * esz + 3) // 4
        if psum:
            assert n32 <= 512 and self.nb < 8, name
            v = self.banks[self.nb][0:shape[0], 0:n32]
            self.nb += 1
        else:
            n32 = (n32 + 7) // 8 * 8
            assert self.off + n32 <= self.n32, (name, self.off, n32)
            v = self.h[0:shape[0], self.off:self.off + n32]
            self.off += n32
        if dtype != F32:
            v = v.bitcast(dtype)
        v = v[:, 0:n]
        if len(shape) == 3:
            v = v.rearrange("p (a b) -> p a b", b=shape[2])
        elif len(shape) == 4:
            v = v.rearrange("p (a b c) -> p a b c", b=shape[2], c=shape[3])
        return TV(v, name, psum)


TT = 2048
NTT = TT // 128
D = 1024
EPS = 1e-6
S = 4096
NST = S // 128


def emit_lin(P, A, t0, ident_d, x_d, c=None, a=None, fin=None):
    A.reset()
    T = A.T
    has_c, has_a, final = c is not None, a is not None, fin is not None
    ident = T("ident", [128, 128], BF16)
    P.dma("sp", ident[:], ident_d, writes=[ident.b])
    wstage = [T("wstage%d" % i, [128, 4, 512]) for i in range(2)]
    ws_i = [0]

    def load_w_bf(dst, dst_k0, src_ap, ncols):
        st = wstage[ws_i[0] % 2]
        ws_i[0] += 1
        P.dma("sp", st[:, :, 0:ncols], src_ap, writes=[st.b])
        P.op("pool", "tensor_copy", dst[:, dst_k0:dst_k0 + 4, 0:ncols], st[:, :, 0:ncols], reads=[st.b], writes=[dst.b])

    xt = [T("xt%d" % i, [128, D]) for i in range(3)]
    tps = T("tps", [128, 8, 128], BF16, psum=True)
    if has_c:
        W, mode = c["W"], c["mode"]
        WC = W // 128
        use_gate = c["gate"] is not None
        wout = T("wout", [128, WC, D], BF16)
        wv = c["w_out"].rearrange("(kc p) n -> p kc n", p=128)
        for k0 in range(0, WC, 4):
            for nb in range(2):
                st = wstage[ws_i[0] % 2]
                ws_i[0] += 1
                P.dma("sp", st[:, :, :], wv[:, k0:k0 + 4, nb * 512:(nb + 1) * 512], writes=[st.b])
                P.op("pool", "tensor_copy", wout[:, k0:k0 + 4, nb * 512:(nb + 1) * 512], st[:, :, :],
                     reads=[st.b], writes=[wout.b])
        gTb = T("gTb", [128, WC, 512], BF16)
        yps = [[T("yps%d_%d" % (i, nb), [128, 512], F32, psum=True) for nb in range(2)] for i in range(2)]
        xout_b = Buf("xout")
        if mode == "fm":
            o_sb = [T("o_sb%d" % i, [128, 4, 512]) for i in range(2)]
            g_sb = [T("g_sb%d" % i, [128, 4, 512]) for i in range(2)] if use_gate else None
        else:
            o_tm = [T("o_tm%d" % i, [128, W]) for i in range(2)]
            g_tm = [T("g_tm%d" % i, [128, W]) for i in range(2)] if use_gate else None
            g_bf = [T("g_bf%d" % i, [128, W], BF16) for i in range(2)]
    if has_a or final:
        grep = T("grep", [128, D])
        P.dma("sp", grep[:], (a or fin)["g_rep"], writes=[grep.b])
        ssq = T("ssq", [128, 1])
        rstd = T("rstd", [128, 1])
        junk = T("junk", [128, D])
    if has_a:
        n_fm, n_tm = a["n_fm"], a["n_tm"]
        hnb = [T("hnb%d" % i, [128, D], BF16) for i in range(2)]
        hnT = T("hnT", [128, 8, TT], BF16)
        hnT_tb = [Buf("hnT%d" % i) for i in range(NTT)]
        winb = [T("winb%d" % i, [128, 8, 512], BF16) for i in range(2)]
        ups = [T("ups%d" % i, [128, 512], F32, psum=True) for i in range(3)]
        ust = [T("ust%d" % i, [128, 512]) for i in range(4)]
        u_b = Buf("u_scratch")
    if final:
        out_b = Buf("out_d")
        ost = [T("ost%d" % i, [128, D]) for i in range(2)]

    xi = 0
    ci = 0
    for blk in range(TT // 512):
        tb0 = t0 + blk * 512
        if has_c and mode == "fm":
            ov = c["o"].rearrange("(kc p) t -> p kc t", p=128)
            gv = c["gate"].rearrange("(kc p) t -> p kc t", p=128) if use_gate else None
            for k0 in range(0, WC, 4):
                i = ci % 2
                ci += 1
                ot = o_sb[i]
                P.dma("sp", ot[:], ov[:, k0:k0 + 4, tb0:tb0 + 512], reads=[c["o_b"]], writes=[ot.b])
                if use_gate:
                    gt = g_sb[i]
                    P.dma("sp", gt[:], gv[:, k0:k0 + 4, tb0:tb0 + 512], reads=[c["gate_b"]], writes=[gt.b])
                    P.op("act", "activation", gt[:], gt[:], AF.Silu, reads=[gt.b], writes=[gt.b])
                    P.op("dve", "tensor_tensor", gTb[:, k0:k0 + 4, :], ot[:], gt[:], ALU.mult, reads=[ot.b, gt.b], writes=[gTb.b])
                else:
                    P.op("dve", "tensor_copy", gTb[:, k0:k0 + 4, :], ot[:], reads=[ot.b], writes=[gTb.b])
        if has_c and mode == "tm":
            for ti in range(4):
                tsl = slice(tb0 + ti * 128, tb0 + (ti + 1) * 128)
                i = ci % 2
                ci += 1
                ot, gb = o_tm[i], g_bf[i]
                P.dma("sp", ot[:], c["o"][tsl, 0:W], reads=[c["o_b"]], writes=[ot.b])
                if use_gate:
                    gt = g_tm[i]
                    P.dma("sp", gt[:], c["gate"][tsl, :], reads=[c["gate_b"]], writes=[gt.b])
                    P.op("act", "activation", gt[:], gt[:], AF.Silu, reads=[gt.b], writes=[gt.b])
                    P.op("dve", "tensor_tensor", gb[:], ot[:], gt[:], ALU.mult, reads=[ot.b, gt.b], writes=[gb.b])
                else:
                    P.op("dve", "tensor_copy", gb[:], ot[:], reads=[ot.b], writes=[gb.b])
                for w0 in range(0, WC, 8):
                    nw = min(8, WC - w0)
                    for k in range(nw):
                        P.op("pe", "transpose", tps[:, k, :], gb[:, (w0 + k) * 128:(w0 + k + 1) * 128], ident[:],
                             reads=[gb.b, ident.b], writes=[tps.b])
                    P.op("act", "copy", gTb[:, w0:w0 + nw, ti * 128:(ti + 1) * 128], tps[:, 0:nw, :],
                         reads=[tps.b], writes=[gTb.b])
        for ti in range(4):
            tt = blk * 4 + ti
            tsl = slice(tb0 + ti * 128, tb0 + (ti + 1) * 128)
            xb = xt[xi % 3]
            xi += 1
            P.dma("sp", xb[:], x_d["ap"][tsl, :], reads=[x_d["b"]], writes=[xb.b])
            if has_c:
                yp = yps[tt % 2]
                for nb in range(2):
                    for wc in range(WC):
                        P.op("pe", "matmul", yp[nb][:], gTb[:, wc, ti * 128:(ti + 1) * 128],
                             wout[:, wc, nb * 512:(nb + 1) * 512], start=(wc == 0), stop=(wc == WC - 1),
                             reads=[gTb.b, wout.b], writes=[yp[nb].b])
                for nb in range(2):
                    P.op("dve", "tensor_tensor", xb[:, nb * 512:(nb + 1) * 512], xb[:, nb * 512:(nb + 1) * 512], yp[nb][:],
                         ALU.add, reads=[xb.b, yp[nb].b], writes=[xb.b])
                if not final:
                    P.dma("act", c["xout"]["ap"][tsl, :], xb[:], reads=[xb.b], writes=[c["xout"]["b"]])
            if has_a or final:
                P.op("act", "activation", junk[:], xb[:], AF.Square, accum_out=ssq[:], reads=[xb.b], writes=[junk.b, ssq.b])
                P.op("act", "activation", rstd[:], ssq[:], AF.Sqrt, bias=EPS, scale=1.0 / D, reads=[ssq.b], writes=[rstd.b])
                P.op("dve", "reciprocal", rstd[:], rstd[:], reads=[rstd.b], writes=[rstd.b])
            if final:
                ot_ = ost[tt % 2]
                P.op("dve", "scalar_tensor_tensor", ot_[:], xb[:], rstd[:, 0:1], grep[:], ALU.mult, ALU.mult,
                     reads=[xb.b, rstd.b, grep.b], writes=[ot_.b])
                P.dma("act", fin["out"][tsl, :], ot_[:], reads=[ot_.b], writes=[out_b])
            if has_a:
                hb = hnb[tt % 2]
                P.op("dve", "scalar_tensor_tensor", hb[:], xb[:], rstd[:, 0:1], grep[:], ALU.mult, ALU.mult,
                     reads=[xb.b, rstd.b, grep.b], writes=[hb.b])
                for kc in range(8):
                    P.op("pe", "transpose", tps[:, kc, :], hb[:, kc * 128:(kc + 1) * 128], ident[:],
                         reads=[hb.b, ident.b], writes=[tps.b])
                P.op("act", "copy", hnT[:, :, tt * 128:(tt + 1) * 128], tps[:], reads=[tps.b], writes=[hnT_tb[tt]])
    if has_a:
        N = n_fm + n_tm
        wv = a["w_in"].rearrange("(kc p) n -> p kc n", p=128)
        nblk = (N + 511) // 512
        assert n_fm % 512 == 0
        ui = 0
        for nb in range(nblk):
            c0 = nb * 512
            cw = min(512, N - c0)
            wb = winb[nb % 2]
            for k0 in (0, 4):
                load_w_bf(wb, k0, wv[:, k0:k0 + 4, c0:c0 + cw], cw)
            if c0 < n_fm:
                for cc in range(4):
                    for tb in range(TT // 512):
                        up, us = ups[ui % 3], ust[ui % 4]
                        for kc in range(8):
                            P.op("pe", "matmul", up[:], wb[:, kc, cc * 128:(cc + 1) * 128], hnT[:, kc, tb * 512:(tb + 1) * 512],
                                 start=(kc == 0), stop=(kc == 7),
                                 reads=[wb.b] + hnT_tb[tb * 4:tb * 4 + 4], writes=[up.b])
                        P.op("act", "copy", us[:], up[:], reads=[up.b], writes=[us.b])
                        r0 = c0 + cc * 128
                        P.dma("act", a["ufm"][r0:r0 + 128, t0 + tb * 512:t0 + (tb + 1) * 512], us[:], reads=[us.b], writes=[u_b])
                        ui += 1
            else:
                for tt in range(NTT):
                    up, us = ups[ui % 3], ust[ui % 4]
                    for kc in range(8):
                        P.op("pe", "matmul", up[:, 0:cw], hnT[:, kc, tt * 128:(tt + 1) * 128], wb[:, kc, 0:cw],
                             start=(kc == 0), stop=(kc == 7), reads=[hnT_tb[tt], wb.b], writes=[up.b])
                    P.op("act", "copy", us[:, 0:cw], up[:, 0:cw], reads=[up.b], writes=[us.b])
                    P.dma("act", a["utm"][t0 + tt * 128:t0 + (tt + 1) * 128, c0 - n_fm:c0 - n_fm + cw], us[:, 0:cw],
                          reads=[us.b], writes=[u_b])
                    ui += 1


SSD_NEG = -30000.0


def emit_ssd(P, A, hh, io):
    A.reset()
    T = A.T
    NCK = S // 128
    ufm, utm, o_d = io["ufm"], io["utm"], io["o"]
    o_b = Buf("o_d")

    def const(name, d_ap, shape, dtype=F32):
        t = T(name, shape, dtype)
        P.dma("sp", t[:], d_ap, writes=[t.b])
        return t
    cw = const("cw", io["convw"][hh], [128, 2, 6, 7])
    cbias = const("cbias", io["convb"][hh], [128, 2, 6])
    dtb = const("dtb", io["dtb_rep"][hh], [128, 2, 2, 8])
    alog = const("alog", io["alog_rep"][hh], [128, 2, 2, 8])
    dsk = const("dsk", io["d_rep"][hh], [128, 2, 8])
    ngr = const("ngr", io["ssd_ng_rep"][hh], [128, 2, 512])
    tmat = const("tmatc", io["tmat"], [128, 2, 128])
    negm = const("negmc", io["negm"], [128, 2, 4, 128])
    identf = const("identf", io["identf"], [128, 128])
    ident = const("ident", io["ident"], [128, 128], BF16)
    onesf = T("onesf", [128, 128])
    P.op("pool", "memset", onesf[:], 1.0, writes=[onesf.b])
    P.op("act", "activation", alog[:], alog[:], AF.Exp, reads=[alog.b], writes=[alog.b])
    P.op("dve", "tensor_scalar", alog[:], alog[:], -1.0, None, ALU.mult, reads=[alog.b], writes=[alog.b])

    xcT = T("xcT", [128, 6, S], BF16)
    yb = T("yb", [128, NCK, 512])
    ybf = yb[:].rearrange("p c e -> p (c e)")
    raws = [ybf[:, 0:S + 6], ybf[:, 4104:4104 + S + 6]]
    raw_b = [Buf("raw0"), Buf("raw1")]
    acc = ybf[:, 8216:8216 + S]
    acc_b = Buf("acc")
    rci = 0
    dtall = T("dtall", [128, NCK, 2, 8])
    dA = T("dA", [128, NCK, 2, 8])
    ncum = T("ncum", [128, NCK, 2, 8])
    ecum = T("ecum", [128, NCK, 2, 8])
    dtd = T("dtd", [128, NCK, 2, 8])
    etot = T("etot", [128, NCK, 2, 8])
    fr = T("fr", [128, 512], F32, psum=True)
    tpx = fr[:, 0:256].bitcast(BF16).rearrange("p (k e) -> p k e", e=128)
    tpb = fr[:, 256:320].bitcast(BF16)
    cbp = fr[:, 320:448]
    bcp = [T("bcp%d" % i, [128, 4, 128], F32, psum=True) for i in range(4)]
    yps = T("yps", [128, 512], F32, psum=True)
    yop = T("yop", [128, 512], F32, psum=True)
    stp = T("stp", [128, 512], F32, psum=True)
    xst = [T("xst%d" % i, [128, 8, 64], BF16) for i in range(2)]
    bst = [T("bst%d" % i, [128, 128], BF16) for i in range(2)]
    cbT = [T("cbT%d" % i, [128, 128]) for i in range(2)]
    dAb = [T("dAb%d" % i, [128, 8, 128]) for i in range(2)]
    xdt = [T("xdt%d" % i, [128, 8, 64], BF16) for i in range(2)]
    xdd = [T("xdd%d" % i, [128, 8, 64], BF16) for i in range(2)]
    NLT = 6
    LT = [T("LT%d" % i, [128, 128]) for i in range(NLT)]
    MT = [T("MT%d" % i, [128, 128], BF16) for i in range(NLT)]
    ytm = [T("ytm%d" % i, [128, 8, 64]) for i in range(2)]
    yt = [T("yt%d" % i, [128, 8, 64]) for i in range(2)]
    zt = [T("zt%d" % i, [128, 512]) for i in range(2)]
    Sst = T("Sst", [128, 8, 64])
    Sb = [T("Sb%d" % i, [128, 512], BF16) for i in range(2)]
    ssq = T("ssq", [128, 1])
    junk = T("junk", [128, 512])
    li = 0

    for gl in range(2):
        g = 2 * hh + gl
        rows = [g * 512 + k * 128 for k in range(4)] + [2048 + g * 128, 2048 + 512 + g * 128]
        P.fence()
        for ci in range(6):
            raw, rb = raws[rci % 2], raw_b[rci % 2]
            rci += 1
            P.op("pool", "memset", raw[:, 0:3], 0.0, writes=[rb])
            P.op("pool", "memset", raw[:, S + 3:S + 6], 0.0, writes=[rb])
            P.dma("sp", raw[:, 3:3 + S], ufm[rows[ci]:rows[ci] + 128, :], writes=[rb])
            P.op("dve", "tensor_scalar", acc, raw[:, 0:S], cw[:, gl, ci, 0:1], None, ALU.mult, reads=[rb, cw.b], writes=[acc_b])
            for k in range(1, 7):
                P.op("dve", "scalar_tensor_tensor", acc, raw[:, k:k + S], cw[:, gl, ci, k:k + 1], acc, ALU.mult, ALU.add,
                     reads=[rb, cw.b, acc_b], writes=[acc_b])
            P.op("act", "activation", xcT[:, ci, :], acc, AF.Silu, bias=cbias[:, gl, ci:ci + 1],
                 reads=[acc_b, cbias.b], writes=[xcT.b])
        P.fence()
        for dr in range(2):
            c0 = 2048 + dr * 32 + g * 8
            P.dma("sp", dtall[:, :, dr, :], utm[:, c0:c0 + 8].rearrange("(c p) j -> p c j", p=128), writes=[dtall.b])
        P.op("dve", "tensor_tensor", dtall[:], dtall[:], dtb[:, gl].unsqueeze(1).to_broadcast([128, NCK, 2, 8]), ALU.add,
             reads=[dtall.b, dtb.b], writes=[dtall.b])
        P.op("act", "activation", dtall[:], dtall[:], AF.Exp, reads=[dtall.b], writes=[dtall.b])
        P.op("act", "activation", dtall[:], dtall[:], AF.Ln, bias=1.0, reads=[dtall.b], writes=[dtall.b])
        P.op("dve", "tensor_tensor", dA[:], dtall[:], alog[:, gl].unsqueeze(1).to_broadcast([128, NCK, 2, 8]), ALU.mult,
             reads=[dtall.b, alog.b], writes=[dA.b])
        cum_ps = yps[:, 0:256].rearrange("p (c j) -> p c j", j=8)
        tot_ps = yop[:, 0:256].rearrange("p (c j) -> p c j", j=8)
        for dr in range(2):
            P.op("pe", "matmul", cum_ps, tmat[:, dr, :], dA[:, :, dr, :], start=True, stop=True, reads=[tmat.b, dA.b], writes=[yps.b])
            P.op("pe", "matmul", tot_ps, onesf[:], dA[:, :, dr, :], start=True, stop=True, reads=[onesf.b, dA.b], writes=[yop.b])
            P.op("dve", "tensor_scalar", ncum[:, :, dr, :], cum_ps, -1.0, None, ALU.mult, reads=[yps.b], writes=[ncum.b])
            P.op("act", "activation", ecum[:, :, dr, :], cum_ps, AF.Exp, reads=[yps.b], writes=[ecum.b])
            P.op("act", "activation", etot[:, :, dr, :], tot_ps, AF.Exp, reads=[yop.b], writes=[etot.b])
            P.op("dve", "tensor_tensor", dtd[:, :, dr, :], ncum[:, :, dr, :], tot_ps, ALU.add, reads=[ncum.b, yop.b], writes=[dtd.b])
        P.op("act", "activation", dtd[:], dtd[:], AF.Exp, reads=[dtd.b], writes=[dtd.b])
        P.op("dve", "tensor_tensor", dtd[:], dtd[:], dtall[:], ALU.mult, reads=[dtd.b, dtall.b], writes=[dtd.b])

        for dr in (1, 0):
            P.op("dve", "memset", Sst[:], 0.0, writes=[Sst.b])
            P.op("pool", "memset", Sb[0][:], 0.0, writes=[Sb[0].b])
            order = list(range(NCK)) if dr == 0 else list(range(NCK - 1, -1, -1))
            def front(n, c):
                cs = slice(c * 128, (c + 1) * 128)
                i2 = n % 2
                xs_, bs_, cb_, dab_, xd_, xq_ = xst[i2], bst[i2], cbT[i2], dAb[i2], xdt[i2], xdd[i2]
                for k in range(4):
                    P.op("pe", "transpose", tpx[:, k, :], xcT[:, k, cs], ident[:], reads=[xcT.b, ident.b], writes=[fr.b])
                P.op("pe", "transpose", tpb, xcT[:, 4, cs], ident[:], reads=[xcT.b, ident.b], writes=[fr.b])
                P.op("pe", "matmul", cbp, xcT[:, 4, cs], xcT[:, 5, cs], start=True, stop=True, reads=[xcT.b], writes=[fr.b])
                P.op("act", "copy", xs_[:].rearrange("p j e -> p (j e)"), tpx.rearrange("p k e -> p (k e)"),
                     reads=[fr.b], writes=[xs_.b])
                P.op("act", "copy", bs_[:], tpb, reads=[fr.b], writes=[bs_.b])
                P.op("act", "copy", cb_[:], cbp, reads=[fr.b], writes=[cb_.b])
                P.op("dve", "tensor_tensor", xd_[:], xs_[:], dtall[:, c, dr, :].unsqueeze(2).to_broadcast([128, 8, 64]), ALU.mult,
                     reads=[xs_.b, dtall.b], writes=[xd_.b])
                P.op("pool", "tensor_tensor", xq_[:], xs_[:], dtd[:, c, dr, :].unsqueeze(2).to_broadcast([128, 8, 64]), ALU.mult,
                     reads=[xs_.b, dtd.b], writes=[xq_.b])
                P.op("dve", "tensor_tensor", dab_[:], tmat[:, dr, :].unsqueeze(1).to_broadcast([128, 8, 128]),
                     dA[:, c, dr, :].unsqueeze(2).to_broadcast([128, 8, 128]), ALU.mult,
                     reads=[tmat.b, dA.b], writes=[dab_.b])
                for h2 in range(2):
                    bp = bcp[i2 * 2 + h2]
                    P.op("pe", "matmul", bp[:].rearrange("p r l -> p (r l)"), identf[:],
                         negm[:, dr].rearrange("p r l -> p (r l)"), start=True, stop=False,
                         reads=[identf.b, negm.b], writes=[bp.b])
                    P.op("pe", "matmul", bp[:].rearrange("p r l -> p (r l)"), onesf[:],
                         dab_[:, h2 * 4:h2 * 4 + 4, :].rearrange("p r l -> p (r l)"), start=False, stop=True,
                         reads=[onesf.b, dab_.b], writes=[bp.b])
            front(0, order[0])
            for n, c in enumerate(order):
                cs = slice(c * 128, (c + 1) * 128)
                i2 = n % 2
                xs_, bs_, cb_, dab_, xd_, xq_ = xst[i2], bst[i2], cbT[i2], dAb[i2], xdt[i2], xdd[i2]
                sb_cur, sb_nxt = Sb[n % 2], Sb[(n + 1) % 2]
                if n + 1 < NCK:
                    front(n + 1, order[n + 1])
                for h2 in range(2):
                    bp = bcp[i2 * 2 + h2]
                    for jj in range(4):
                        j = h2 * 4 + jj
                        lt, mt = LT[li % NLT], MT[li % NLT]
                        li += 1
                        P.op("act", "activation", lt[:], bp[:, jj, :], AF.Exp, bias=ncum[:, c, dr, j:j + 1],
                             reads=[bp.b, ncum.b], writes=[lt.b])
                        P.op("pool" if li % 2 == 0 else "dve", "tensor_tensor", mt[:], lt[:], cb_[:], ALU.mult,
                             reads=[lt.b, cb_.b], writes=[mt.b])
                        P.op("pe", "matmul", yps[:, j * 64:(j + 1) * 64], mt[:], xd_[:, j, :], start=True, stop=True,
                             reads=[mt.b, xd_.b], writes=[yps.b])
                P.op("pe", "matmul", yop[:], xcT[:, 5, cs], sb_cur[:], start=True, stop=True, reads=[xcT.b, sb_cur.b], writes=[yop.b])
                ym, y_ = ytm[i2], yt[i2]
                P.op("dve", "tensor_tensor", ym[:], yop[:].rearrange("p (j e) -> p j e", e=64),
                     ecum[:, c, dr, :].unsqueeze(2).to_broadcast([128, 8, 64]), ALU.mult, reads=[yop.b, ecum.b], writes=[ym.b])
                if dr == 1:
                    P.op("dve", "tensor_tensor", yb[:, c, :], ym[:].rearrange("p j e -> p (j e)"), yps[:], ALU.add,
                         reads=[ym.b, yps.b], writes=[yb.b])
                else:
                    P.op("dve", "tensor_tensor", y_[:].rearrange("p j e -> p (j e)"), ym[:].rearrange("p j e -> p (j e)"), yps[:],
                         ALU.add, reads=[ym.b, yps.b], writes=[y_.b])
                P.op("pe", "matmul", stp[:], bs_[:], xq_[:].rearrange("p j e -> p (j e)"), start=True, stop=True,
                     reads=[bs_.b, xq_.b], writes=[stp.b])
                P.op("dve", "tensor_tensor", Sst[:], Sst[:], etot[:, c, dr, :].unsqueeze(2).to_broadcast([128, 8, 64]), ALU.mult,
                     reads=[Sst.b, etot.b], writes=[Sst.b])
                P.op("dve", "tensor_tensor", Sst[:].rearrange("p j e -> p (j e)"), Sst[:].rearrange("p j e -> p (j e)"), stp[:],
                     ALU.add, reads=[Sst.b, stp.b], writes=[Sst.b])
                P.op("act", "copy", sb_nxt[:], Sst[:].rearrange("p j e -> p (j e)"), reads=[Sst.b], writes=[sb_nxt.b])
                if dr == 0:
                    z_ = zt[i2]
                    P.dma("sp", z_[:], utm[cs, g * 512:(g + 1) * 512], writes=[z_.b])
                    P.op("act", "activation", z_[:], z_[:], AF.Silu, reads=[z_.b], writes=[z_.b])
                    yf = y_[:].rearrange("p j e -> p (j e)")
                    P.op("dve", "tensor_tensor", yf, yf, yb[:, c, :], ALU.add, reads=[y_.b, yb.b], writes=[y_.b])
                    P.op("pool", "tensor_tensor", ym[:], xs_[:], dsk[:, gl, :].unsqueeze(2).to_broadcast([128, 8, 64]), ALU.mult,
                         reads=[xs_.b, dsk.b], writes=[ym.b])
                    P.op("dve", "tensor_tensor", y_[:], y_[:], ym[:], ALU.add, reads=[y_.b, ym.b], writes=[y_.b])
                    P.op("dve", "tensor_tensor", yf, yf, z_[:], ALU.mult, reads=[y_.b, z_.b], writes=[y_.b])
                    P.op("act", "activation", junk[:], yf, AF.Square, accum_out=ssq[:], reads=[y_.b], writes=[junk.b, ssq.b])
                    P.op("act", "activation", ssq[:], ssq[:], AF.Sqrt, bias=EPS, scale=1.0 / 512, reads=[ssq.b], writes=[ssq.b])
                    P.op("dve", "reciprocal", ssq[:], ssq[:], reads=[ssq.b], writes=[ssq.b])
                    P.op("dve", "scalar_tensor_tensor", z_[:], yf, ssq[:, 0:1], ngr[:, gl, :], ALU.mult, ALU.mult,
                         reads=[y_.b, ssq.b, ngr.b], writes=[z_.b])
                    P.dma("sp", o_d[cs, g * 512:(g + 1) * 512], z_[:], reads=[z_.b], writes=[o_b])


def ssd_consts():
    s = np.arange(128)[:, None]
    l = np.arange(128)[None, :]
    tm = np.stack([(s <= l), (s >= l)], 1).astype(np.float32)
    nm = np.stack([np.where(l < s, SSD_NEG, 0.0), np.where(l > s, SSD_NEG, 0.0)], 1).astype(np.float32)
    nm = np.ascontiguousarray(np.repeat(nm[:, :, None, :], 4, axis=2))
    return tm, nm


def ssd_small(hh, conv_w, conv_b, dt_bias, a_log, d_skip, norm_g):
    cwt = np.empty((128, 2, 6, 7), np.float32)
    cbt = np.empty((128, 2, 6), np.float32)
    dtb = np.empty((128, 2, 2, 8), np.float32)
    alog = np.empty((128, 2, 2, 8), np.float32)
    dsk = np.empty((128, 2, 8), np.float32)
    ng = np.empty((128, 2, 512), np.float32)
    for gl in range(2):
        g = 2 * hh + gl
        chans = [np.arange(g * 512 + k * 128, g * 512 + (k + 1) * 128) for k in range(4)]
        chans.append(np.arange(2048 + g * 128, 2048 + (g + 1) * 128))
        chans.append(np.arange(2048 + 512 + g * 128, 2048 + 512 + (g + 1) * 128))
        for ci, ch in enumerate(chans):
            cwt[:, gl, ci, :] = conv_w[:, ch].T
            cbt[:, gl, ci] = conv_b[ch]
        hs = slice(g * 8, (g + 1) * 8)
        dtb[:, gl] = dt_bias[:, hs][None]
        alog[:, gl] = a_log[:, hs][None]
        dsk[:, gl] = d_skip[hs][None]
        ng[:, gl] = norm_g[g * 512:(g + 1) * 512][None]
    return cwt, cbt, dtb, alog, dsk, ng


HG_LAYER = 1


def emit_hg(P, A, hh, io):
    A.reset()
    T = A.T
    NCH = S // 32
    ufm, utm, od_d, o_d = io["ufm"], io["utm"], io["od"], io["o"]
    od_b = Buf("od_d")
    o_b = Buf("o_d")

    def const(name, d_ap, shape, dtype=F32):
        t = T(name, shape, dtype)
        P.dma("sp", t[:], d_ap, writes=[t.b])
        return t
    ident = const("ident", io["ident"], [128, 128], BF16)
    smask = const("smask", io["smask"], [128, 1024])
    masks = const("masks", io["masks"], [32, 2, 32])
    ng = const("ng", io["hg_ng_rep"][hh], [128, 512])
    lbt = const("lbt", io["lbT"][hh], [128, 4, 4])
    lsum = T("lsum", [128, 4])
    lb = T("lb", [128, 4])
    oml = T("oml", [128, 4])
    P.op("act", "activation", lbt[:], lbt[:], AF.Exp, reads=[lbt.b], writes=[lbt.b])
    P.op("dve", "tensor_reduce", lsum[:], lbt[:].rearrange("p l h -> p h l"), AX.X, ALU.add, reads=[lbt.b], writes=[lsum.b])
    P.op("dve", "reciprocal", lsum[:], lsum[:], reads=[lsum.b], writes=[lsum.b])
    P.op("dve", "tensor_tensor", lb[:], lbt[:, HG_LAYER, :], lsum[:], ALU.mult, reads=[lbt.b, lsum.b], writes=[lb.b])
    P.op("dve", "tensor_scalar", oml[:], lb[:], -1.0, 1.0, ALU.mult, ALU.add, reads=[lb.b], writes=[oml.b])

    PW = 1024
    NP_ = S // PW
    qs = T("qs", [128, S])
    vstg = T("vstg", [32, 16, 128])
    v32 = T("v32b", [32, NCH, 128], BF16)
    qtl = T("qtl", [128, S], BF16)
    ktl = T("ktl", [128, S], BF16)
    qht = T("qht", [128, S], BF16)
    kh32 = T("kh32", [32, NCH, 128], BF16)
    ec = T("ec", [128, NCH])
    fr = [T("fr%d" % i, [128, PW]) for i in range(2)]
    tf2 = [T("tf0", [128, PW])] * 2
    tg2 = [T("tg0", [128, PW])] * 2
    tk2 = [T("tk%d" % i, [128, PW]) for i in range(2)]
    tX2 = [T("tX%d" % i, [128, PW]) for i in range(2)]
    tD2 = [T("tD%d" % i, [128, PW]) for i in range(2)]
    tE4 = [T("tE%d" % i, [128, PW]) for i in range(4)]
    khT2 = [T("khT%d" % i, [128, PW], BF16) for i in range(2)]
    tpk = [T("tpk%d" % i, [32, 4, 128], BF16, psum=True) for i in range(2)]
    aps = [T("aps%d" % i, [32, 32], F32, psum=True) for i in range(2)]
    ops_ = [T("ops%d" % i, [32, 128], F32, psum=True) for i in range(2)]
    ups = [T("ups%d" % i, [128, 128], F32, psum=True) for i in range(2)]
    asb = [T("asb%d" % i, [32, 32], BF16) for i in range(2)]
    st2 = [T("st%d" % i, [128, 128]) for i in range(2)]
    stb = [T("stb%d" % i, [128, 128], BF16) for i in range(2)]
    ost = [T("ost%d" % i, [32, 8, 128]) for i in range(2)]

    def v3(t):
        return t[:, 0:PW].rearrange("p (c j) -> p c j", j=32)

    fi = 0
    oi = 0
    for h in range(4):
        hg = 4 * hh + h
        for p in range(NP_):
            sl = slice(p * PW, (p + 1) * PW)
            P.dma("sp", qs[:, sl], ufm[hg * 128:(hg + 1) * 128, sl], writes=[qs.b])
        P.op("act", "activation", qs[:], qs[:], AF.Silu, reads=[qs.b], writes=[qs.b])
        for p in range(8):
            src = utm[p * 512:(p + 1) * 512, hg * 128:(hg + 1) * 128].rearrange("(c j) e -> j c e", j=32)
            P.dma("sp", vstg[:], src, writes=[vstg.b])
            P.op("pool", "tensor_copy", v32[:, p * 16:(p + 1) * 16, :], vstg[:], reads=[vstg.b], writes=[v32.b])
        for dr in range(2):
            frow = (1 + dr) * 1024 + hg * 128
            for p in range(NP_):
                sl = slice(p * PW, (p + 1) * PW)
                frt = fr[fi % 2]
                tf, tg, tk, tX, tD, khT = tf2[fi % 2], tg2[fi % 2], tk2[fi % 2], tX2[fi % 2], tD2[fi % 2], khT2[fi % 2]
                tE = tE4[(fi % 2) * 2:(fi % 2) * 2 + 2]
                fi += 1
                P.dma("sp", frt[:], ufm[frow:frow + 128, sl], writes=[frt.b])
                P.op("act", "activation", frt[:], frt[:], AF.Sigmoid, reads=[frt.b], writes=[frt.b])
                P.op("dve", "tensor_scalar", tf[:], frt[:], oml[:, h:h + 1], lb[:, h:h + 1], ALU.mult, ALU.add,
                     reads=[frt.b, oml.b, lb.b], writes=[tf.b])
                P.op("act", "activation", tg[:], tf[:], AF.Ln, reads=[tf.b], writes=[tg.b])
                P.op("pool", "tensor_scalar", tk[:], tf[:], -1.0, 1.0, ALU.mult, ALU.add, reads=[tf.b], writes=[tk.b])
                P.op("dve", "tensor_tensor_scan", tX[:], smask[:], tg[:], 0.0, ALU.mult, ALU.add,
                     reads=[smask.b, tg.b], writes=[tX.b])
                X3 = v3(tX)
                if dr == 1:
                    P.op("dve", "tensor_tensor", v3(tD), X3[:, :, 31:32].to_broadcast([128, 32, 32]), X3, ALU.subtract,
                         reads=[tX.b], writes=[tD.b])
                    P.op("dve", "tensor_tensor", tX[:], tD[:], tg[:], ALU.add, reads=[tD.b, tg.b], writes=[tX.b])
                edge = 31 if dr == 0 else 0
                P.op("act", "activation", ec[:, p * 32:(p + 1) * 32], X3[:, :, edge], AF.Exp, reads=[tX.b], writes=[ec.b])
                P.op("dve", "tensor_tensor", v3(tD), X3, X3[:, :, 16:17].to_broadcast([128, 32, 32]), ALU.subtract,
                     reads=[tX.b], writes=[tD.b])
                P.op("act", "activation", tE[0][:], tD[:], AF.Exp, reads=[tD.b], writes=[tE[0].b])
                P.op("act", "activation", tE[1][:], tD[:], AF.Exp, scale=-1.0, reads=[tD.b], writes=[tE[1].b])
                P.op("pool", "tensor_tensor", qtl[:, sl], qs[:, sl], tE[0][:], ALU.mult, reads=[qs.b, tE[0].b], writes=[qtl.b])
                P.op("dve", "tensor_tensor", ktl[:, sl], tk[:], tE[1][:], ALU.mult, reads=[tk.b, tE[1].b], writes=[ktl.b])
                P.op("act", "activation", tE[0][:], tX[:], AF.Exp, reads=[tX.b], writes=[tE[0].b])
                P.op("pool", "tensor_tensor", qht[:, sl], qs[:, sl], tE[0][:], ALU.mult, reads=[qs.b, tE[0].b], writes=[qht.b])
                P.op("dve", "tensor_tensor", v3(tD), X3[:, :, edge:edge + 1].to_broadcast([128, 32, 32]), X3, ALU.subtract,
                     reads=[tX.b], writes=[tD.b])
                P.op("act", "activation", tE[1][:], tD[:], AF.Exp, reads=[tD.b], writes=[tE[1].b])
                P.op("dve", "tensor_tensor", khT[:], tk[:], tE[1][:], ALU.mult, reads=[tk.b, tE[1].b], writes=[khT.b])
                for g8 in range(PW // 128):
                    tp = tpk[g8 % 2]
                    for q4 in range(4):
                        c = g8 * 4 + q4
                        P.op("pe", "transpose", tp[:, q4, :], khT[:, c * 32:(c + 1) * 32], ident[:],
                             reads=[khT.b, ident.b], writes=[tp.b])
                    c0 = p * 32 + g8 * 4
                    P.op("act", "copy", kh32[:, c0:c0 + 4, :], tp[:], reads=[tp.b], writes=[kh32.b])
            P.op("dve", "memset", st2[1][:], 0.0, writes=[st2[1].b])
            P.op("pool", "memset", stb[0][:], 0.0, writes=[stb[0].b])
            order = list(range(NCH)) if dr == 0 else list(range(NCH - 1, -1, -1))
            def front(n, c):
                cs = slice(c * 32, (c + 1) * 32)
                ap_, up_, as_ = aps[n % 2], ups[n % 2], asb[n % 2]
                P.op("pe", "matmul", ap_[:], ktl[:, cs], qtl[:, cs], start=True, stop=True, reads=[ktl.b, qtl.b], writes=[ap_.b])
                P.op("pe", "matmul", up_[:], kh32[:, c, :], v32[:, c, :], start=True, stop=True, reads=[kh32.b, v32.b], writes=[up_.b])
                P.op("dve", "tensor_tensor", as_[:], ap_[:], masks[:, dr, :], ALU.mult, reads=[ap_.b, masks.b], writes=[as_.b])
            front(0, order[0])
            for n, c in enumerate(order):
                cs = slice(c * 32, (c + 1) * 32)
                ap_, op_, up_, as_ = aps[n % 2], ops_[n % 2], ups[n % 2], asb[n % 2]
                sb_cur, sb_nxt = stb[n % 2], stb[(n + 1) % 2]
                if n + 1 < NCH:
                    front(n + 1, order[n + 1])
                P.op("pe", "matmul", op_[:], as_[:], v32[:, c, :], start=True, stop=False, reads=[as_.b, v32.b], writes=[op_.b])
                P.op("pe", "matmul", op_[:], qht[:, cs], sb_cur[:], start=False, stop=True, reads=[qht.b, sb_cur.b], writes=[op_.b])
                os_ = ost[oi % 2]
                slot = c % 8
                s_old, s_new = st2[(n + 1) % 2], st2[n % 2]
                P.op("dve", "scalar_tensor_tensor", s_new[:], s_old[:], ec[:, c:c + 1], up_[:], ALU.mult, ALU.add,
                     reads=[s_old.b, ec.b, up_.b], writes=[s_new.b])
                P.op("act", "copy", sb_nxt[:], s_new[:], reads=[s_new.b], writes=[sb_nxt.b])
                P.op("act", "copy", os_[:, slot, :], op_[:], reads=[op_.b], writes=[os_.b])
                if n % 8 == 7:
                    cb = (c // 8) * 8
                    dst = od_d[dr][cb * 32:(cb + 8) * 32, h * 128:(h + 1) * 128].rearrange("(c i) e -> i c e", i=32)
                    P.dma("sp", dst, os_[:], reads=[os_.b], writes=[od_b])
                    oi += 1
    fa = [T("fa%d" % i, [128, 512]) for i in range(2)]
    fb = [T("fb%d" % i, [128, 512]) for i in range(2)]
    fo = [T("fo%d" % i, [128, 512]) for i in range(2)]
    fss = [T("fss%d" % i, [128, 4]) for i in range(2)]
    fj = T("fj", [128, 128])
    for tt in range(NST):
        i = tt % 2
        sl = slice(tt * 128, (tt + 1) * 128)
        P.dma("sp", fa[i][:], od_d[0][sl], reads=[od_b], writes=[fa[i].b])
        P.dma("sp", fb[i][:], od_d[1][sl], reads=[od_b], writes=[fb[i].b])
        P.op("dve", "tensor_tensor", fa[i][:], fa[i][:], fb[i][:], ALU.add, reads=[fa[i].b, fb[i].b], writes=[fa[i].b])
        for h in range(4):
            P.op("act", "activation", fj[:], fa[i][:, h * 128:(h + 1) * 128], AF.Square, accum_out=fss[i][:, h:h + 1],
                 reads=[fa[i].b], writes=[fj.b, fss[i].b])
        P.op("act", "activation", fss[i][:], fss[i][:], AF.Sqrt, bias=EPS, scale=1.0 / 128, reads=[fss[i].b], writes=[fss[i].b])
        P.op("dve", "reciprocal", fss[i][:], fss[i][:], reads=[fss[i].b], writes=[fss[i].b])
        for h in range(4):
            hs = slice(h * 128, (h + 1) * 128)
            P.op("dve", "scalar_tensor_tensor", fo[i][:, hs], fa[i][:, hs], fss[i][:, h:h + 1], ng[:, hs], ALU.mult, ALU.mult,
                 reads=[fa[i].b, fss[i].b, ng.b], writes=[fo[i].b])
        P.dma("sp", o_d[sl, hh * 512:(hh + 1) * 512], fo[i][:], reads=[fo[i].b], writes=[o_b])


def hg_consts():
    sm = np.ones((128, 1024), np.float32)
    sm[:, ::32] = 0.0
    j = np.arange(32)[:, None]
    i = np.arange(32)[None, :]
    masks = np.stack([(j <= i), (j >= i)], 1).astype(np.float32)
    return sm, masks


def emit_at(P, A, hh, io):
    A.reset()
    T = A.T
    utm, oT_d = io["utm"], io["oT"]
    oT_b = Buf("oT_d")
    ident = T("ident", [128, 128], BF16)
    gain = T("gain", [128, 12, 128])
    ones = T("ones", [128, 128], BF16)
    P.dma("sp", ident[:], io["ident"], writes=[ident.b])
    P.dma("sp", gain[:], io["gain"], writes=[gain.b])
    P.op("pool", "memset", ones[:], 1.0, writes=[ones.b])
    qT = T("qT", [128, 8, S], BF16)
    kT = T("kT", [128, 4, S], BF16)
    vb = T("vb", [128, NST, 4, 128], BF16)
    qT_b = [Buf("qT%d" % i) for i in range(NST)]
    kT_b = [Buf("kT%d" % i) for i in range(NST)]
    vb_b = [Buf("vb%d" % i) for i in range(NST)]
    qk = [T("qk%d" % i, [128, 12, 128]) for i in range(2)]
    vt = [T("vt%d" % i, [128, 4, 128]) for i in range(2)]
    cs = [T("cs%d" % i, [128, 2, 2, 32]) for i in range(2)]
    junk = T("junk", [128, 128])
    ssq = [T("ssq%d" % i, [128, 12]) for i in range(2)]
    qn = [T("qn0", [128, 12, 128])] * 2
    ta = [T("ta0", [128, 12, 2, 32])] * 2
    tb = [T("tb0", [128, 12, 2, 32])] * 2
    tc = [T("tc0", [128, 12, 2, 32])] * 2
    td = [T("td0", [128, 12, 2, 32])] * 2
    qr = [T("qr%d" % i, [128, 12, 128], BF16) for i in range(2)]
    tpq = T("tpq", [128, 8, 128], BF16, psum=True)
    tpk = T("tpk", [128, 4, 128], BF16, psum=True)
    for tt in range(NST):
        i = tt % 2
        sl = slice(tt * 128, (tt + 1) * 128)
        qkt, vtt, cst, sst, qnt, qrt = qk[i], vt[i], cs[i], ssq[i], qn[i], qr[i]
        P.dma("sp", qkt[:, 0:8, :], utm[sl, hh * 1024:(hh + 1) * 1024].rearrange("t (h e) -> t h e", h=8), writes=[qkt.b])
        P.dma("sp", qkt[:, 8:12, :], utm[sl, 2048 + hh * 512:2048 + (hh + 1) * 512].rearrange("t (h e) -> t h e", h=4), writes=[qkt.b])
        P.dma("sp", vtt[:], utm[sl, 3072 + hh * 512:3072 + (hh + 1) * 512].rearrange("t (h e) -> t h e", h=4), writes=[vtt.b])
        P.dma("sp", cst[:, 0], io["cos"][sl], writes=[cst.b])
        P.dma("sp", cst[:, 1], io["sin"][sl], writes=[cst.b])
        P.op("pool", "tensor_copy", vb[:, tt], vtt[:], reads=[vtt.b], writes=[vb_b[tt]])
        for h in range(12):
            P.op("act", "activation", junk[:], qkt[:, h, :], AF.Square, accum_out=sst[:, h:h + 1], reads=[qkt.b], writes=[junk.b, sst.b])
        P.op("act", "activation", sst[:], sst[:], AF.Sqrt, bias=EPS, scale=1.0 / 128, reads=[sst.b], writes=[sst.b])
        P.op("dve", "reciprocal", sst[:], sst[:], reads=[sst.b], writes=[sst.b])
        for h in range(12):
            P.op("dve", "scalar_tensor_tensor", qnt[:, h, :], qkt[:, h, :], sst[:, h:h + 1], gain[:, h, :], ALU.mult, ALU.mult,
                 reads=[qkt.b, sst.b, gain.b], writes=[qnt.b])
        xv = qnt[:].rearrange("p h (a two f) -> p h a two f", a=2, two=2)
        ov = qrt[:].rearrange("p h (a two f) -> p h a two f", a=2, two=2)
        x1, x2 = xv[:, :, :, 0, :], xv[:, :, :, 1, :]
        cb = cst[:, 0].unsqueeze(1).to_broadcast([128, 12, 2, 32])
        sb = cst[:, 1].unsqueeze(1).to_broadcast([128, 12, 2, 32])
        a_, b_, c_, d_ = ta[i], tb[i], tc[i], td[i]
        P.op("dve", "tensor_tensor", a_[:], x1, cb, ALU.mult, reads=[qnt.b, cst.b], writes=[a_.b])
        P.op("pool", "tensor_tensor", b_[:], x2, sb, ALU.mult, reads=[qnt.b, cst.b], writes=[b_.b])
        P.op("dve", "tensor_tensor", c_[:], x2, cb, ALU.mult, reads=[qnt.b, cst.b], writes=[c_.b])
        P.op("pool", "tensor_tensor", d_[:], x1, sb, ALU.mult, reads=[qnt.b, cst.b], writes=[d_.b])
        P.op("dve", "tensor_tensor", ov[:, :, :, 0, :], a_[:], b_[:], ALU.subtract, reads=[a_.b, b_.b], writes=[qrt.b])
        P.op("pool", "tensor_tensor", ov[:, :, :, 1, :], c_[:], d_[:], ALU.add, reads=[c_.b, d_.b], writes=[qrt.b])
        for h in range(8):
            P.op("pe", "transpose", tpq[:, h, :], qrt[:, h, :], ident[:], reads=[qrt.b, ident.b], writes=[tpq.b])
        for h in range(4):
            P.op("pe", "transpose", tpk[:, h, :], qrt[:, 8 + h, :], ident[:], reads=[qrt.b, ident.b], writes=[tpk.b])
        P.op("act", "copy", qT[:, :, sl], tpq[:], reads=[tpq.b], writes=[qT_b[tt]])
        P.op("act", "copy", kT[:, :, sl], tpk[:], reads=[tpk.b], writes=[kT_b[tt]])
    sps = [T("sps%d" % i, [128, 512], F32, psum=True) for i in range(2)]
    ops_ = [T("ops%d" % i, [128, 512], F32, psum=True) for i in range(2)]
    dps = [T("dps%d" % i, [128, 512], F32, psum=True) for i in range(2)]
    NPT = 6
    pT = [T("pT%d" % i, [128, 512], BF16) for i in range(NPT)]
    psum4 = [T("psum4_%d" % i, [128, 512], BF16) for i in range(2)]
    rec = [T("rec%d" % i, [128, 512]) for i in range(2)]
    osb = [T("osb%d" % i, [128, 512]) for i in range(2)]
    scale = 128.0 ** -0.5
    it = 0
    for h in range(8):
        kv = h // 2
        for qb in range(S // 512):
            qsl = slice(qb * 512, (qb + 1) * 512)
            q_reads = [qT_b[qb * 4 + j] for j in range(4)]
            op_, dp_ = ops_[it % 2], dps[it % 2]

            def mm1(kt):
                sp_ = sps[kt % 2]
                P.op("pe", "matmul", sp_[:], kT[:, kv, kt * 128:(kt + 1) * 128], qT[:, h, qsl], start=True, stop=True,
                     reads=[kT_b[kt]] + q_reads, writes=[sp_.b])

            def rest(kt):
                sp_ = sps[kt % 2]
                p_ = pT[kt % NPT]
                P.op("act", "activation", p_[:], sp_[:], AF.Exp, scale=scale, reads=[sp_.b], writes=[p_.b])
                P.op("pe", "matmul", op_[:], vb[:, kt, kv, :], p_[:], start=(kt == 0), stop=(kt == NST - 1),
                     reads=[vb_b[kt], p_.b], writes=[op_.b])
                if kt % 4 == 1:
                    ps_ = psum4[(kt // 4) % 2]
                    P.op("dve", "tensor_tensor", ps_[:], pT[(kt - 1) % NPT][:], p_[:], ALU.add,
                         reads=[pT[(kt - 1) % NPT].b, p_.b], writes=[ps_.b])
                elif kt % 4 in (2, 3):
                    ps_ = psum4[(kt // 4) % 2]
                    P.op("dve", "tensor_tensor", ps_[:], ps_[:], p_[:], ALU.add, reads=[ps_.b, p_.b], writes=[ps_.b])
                G = (kt - 5) // 4
                if kt >= 5 and (kt - 5) % 4 == 0:
                    pg = psum4[G % 2]
                    P.op("pe", "matmul", dp_[:], ones[:], pg[:], start=(G == 0), stop=False,
                         reads=[ones.b, pg.b], writes=[dp_.b])
                if kt == NST - 1:
                    pg = psum4[(NST // 4 - 1) % 2]
                    P.op("pe", "matmul", dp_[:], ones[:], pg[:], start=False, stop=True,
                         reads=[ones.b, pg.b], writes=[dp_.b])
            mm1(0)
            for kt in range(NST):
                if kt + 1 < NST:
                    mm1(kt + 1)
                rest(kt)
            r_, o_ = rec[it % 2], osb[it % 2]
            P.op("dve", "reciprocal", r_[:], dp_[:], reads=[dp_.b], writes=[r_.b])
            P.op("dve", "tensor_tensor", o_[:], op_[:], r_[:], ALU.mult, reads=[op_.b, r_.b], writes=[o_.b])
            hgl = hh * 8 + h
            P.dma("sp", oT_d[hgl * 128:(hgl + 1) * 128, qsl], o_[:], reads=[o_.b], writes=[oT_b])
            it += 1


def rope_tables():
    row = np.repeat(np.arange(S // 64), 64).astype(np.float32)
    col = (np.arange(S) % 64).astype(np.float32)
    inv = (np.float32(10000.0) ** (-np.arange(0, 64, 2, dtype=np.float32) / np.float32(64))).astype(np.float32)
    ang = np.stack([row[:, None] * inv, col[:, None] * inv], 1).astype(np.float32)
    return np.cos(ang).astype(np.float32), np.sin(ang).astype(np.float32)


DL_PAIRS = ((128, 1), (512, 4), (2048, 16))
NEG = -30000.0


def dl_geom(d):
    Ls = S // d
    nt = Ls // 128
    return Ls, nt, d * (Ls + 128), d * (nt + 1)


def emit_dl(P, A, hh, io):
    A.reset()
    T = A.T
    ufm, utm, nd_d, o_d = io["ufm"], io["utm"], io["nd"], io["o"]
    nd_b = Buf("nd_d")
    o_b = Buf("o_d")
    bias = T("bias", [128, 24, 256])
    P.dma("sp", bias[:], io["dl_bias"][hh], writes=[bias.b])
    KMAX = 6144
    VMAX = 48
    qst = T("qst", [64, S])
    kst = T("kst", [64, S])
    vst = T("vst", [128, VMAX, 65])
    qb_ = [T("qb%d" % i, [64, S], BF16) for i in range(2)]
    kb_ = [T("kb%d" % i, [64, KMAX], BF16) for i in range(2)]
    vb_ = [T("vb%d" % i, [128, VMAX, 65], BF16) for i in range(2)]
    nds = [T("nds%d" % i, [128, 32, 65]) for i in range(2)]
    NPB = 4
    sps = [T("sps%d" % i, [128, 2, 128], F32, psum=True) for i in range(NPB)]
    ops_ = [T("ops%d" % i, [128, 65], F32, psum=True) for i in range(NPB)]
    tmp = [T("tmp%d" % i, [128, 256]) for i in range(NPB)]
    pb = [T("pb%d" % i, [128, 2, 128], BF16) for i in range(NPB)]
    items = [(g, d, h) for g, (_, d) in enumerate(DL_PAIRS) for h in range(8)]

    def prep(it):
        g, d, h = items[it]
        Ls, nt, klen, vt_n = dl_geom(d)
        i = it % 2
        hgl = 8 * hh + h
        qrow = (g * 2) * 1024 + hgl * 64
        krow = (g * 2 + 1) * 1024 + hgl * 64
        P.dma("sp", qst[:], ufm[qrow:qrow + 64, :], writes=[qst.b])
        P.dma("sp", kst[:], ufm[krow:krow + 64, :], writes=[kst.b])
        P.op("act", "copy", qb_[i][:].rearrange("p (r m) -> p r m", r=d), qst[:].rearrange("p (m r) -> p r m", r=d),
             reads=[qst.b], writes=[qb_[i].b])
        P.op("pool", "memset", kb_[i][:, 0:klen], 0.0, writes=[kb_[i].b])
        P.op("act", "copy", kb_[i][:, 0:klen].rearrange("p (r m) -> p r m", r=d)[:, :, 64:64 + Ls],
             kst[:].rearrange("p (m r) -> p r m", r=d), reads=[kst.b], writes=[kb_[i].b])
        v4 = vst[:, 0:vt_n, :].rearrange("p (r i) c -> p r i c", r=d)
        vsrc = utm[:, g * 1024 + hgl * 64:g * 1024 + (hgl + 1) * 64].rearrange("(m r) e -> r m e", r=d)
        P.op("pool", "memset", vst[:, 0:vt_n, :], 0.0, writes=[vst.b])
        if nt > 1:
            P.op("pool", "memset", v4[:, :, 1:nt, 64:65], 1.0, writes=[vst.b])
        P.op("pool", "memset", v4[64:128, :, 0, 64:65], 1.0, writes=[vst.b])
        P.op("pool", "memset", v4[0:64, :, nt, 64:65], 1.0, writes=[vst.b])
        if nt - 1 == 1:
            P.dma("sp", v4[:, :, 1, 0:64], vsrc[:, 64:192, :].rearrange("r k e -> k r e"), writes=[vst.b])
        elif nt > 1:
            for r in range(d):
                P.dma("sp", v4[:, r, 1:nt, 0:64], vsrc[r][64:64 + 128 * (nt - 1), :].rearrange("(i k) e -> k i e", k=128),
                      writes=[vst.b])
        P.dma("sp", v4[64:128, :, 0, 0:64], vsrc[:, 0:64, :].rearrange("r k e -> k r e"), writes=[vst.b])
        P.dma("sp", v4[0:64, :, nt, 0:64], vsrc[:, Ls - 64:Ls, :].rearrange("r k e -> k r e"), writes=[vst.b])
        P.op("dve", "tensor_copy", vb_[i][:, 0:vt_n, :], vst[:, 0:vt_n, :], reads=[vst.b], writes=[vb_[i].b])

    ucnt = [0]

    def compute(it):
        g, d, h = items[it]
        Ls, nt, klen, vt_n = dl_geom(d)
        i = it % 2
        nd_ = nds[i]
        u0 = ucnt[0]
        ucnt[0] += 32

        def geo(j):
            r, jj = divmod(j, nt)
            return r * Ls + jj * 128, r * (Ls + 128) + jj * 128, r * (nt + 1) + jj

        def qk(j):
            n0, kbase, _ = geo(j)
            sp_ = sps[(u0 + j) % NPB]
            for c in range(2):
                P.op("pe", "matmul", sp_[:, c, :], kb_[i][:, kbase + c * 128:kbase + (c + 1) * 128],
                     qb_[i][:, n0:n0 + 128], start=True, stop=True, reads=[kb_[i].b, qb_[i].b], writes=[sp_.b])
            tm_, p_ = tmp[(u0 + j) % NPB], pb[(u0 + j) % NPB]
            P.op("dve", "scalar_tensor_tensor", tm_[:], sp_[:].rearrange("p c q -> p (c q)"), 0.125,
                 bias[:, g * 8 + h, :], ALU.mult, ALU.add, reads=[sp_.b, bias.b], writes=[tm_.b])
            P.op("act", "activation", p_[:].rearrange("p c q -> p (c q)"), tm_[:], AF.Exp, reads=[tm_.b], writes=[p_.b])

        def pv(j):
            _, _, vbase = geo(j)
            op_, p_ = ops_[(u0 + j) % NPB], pb[(u0 + j) % NPB]
            for c in range(2):
                P.op("pe", "matmul", op_[:], p_[:, c, :], vb_[i][:, vbase + c, :], start=(c == 0), stop=(c == 1),
                     reads=[p_.b, vb_[i].b], writes=[op_.b])
            P.op("dve", "tensor_copy", nd_[:, j, :], op_[:], reads=[op_.b], writes=[nd_.b])
        AHEAD = 2
        for j in range(min(AHEAD, 32)):
            qk(j)
        for j in range(32):
            if j + AHEAD < 32:
                qk(j + AHEAD)
            pv(j)

    def store(it):
        g, d, h = items[it]
        Ls, nt, klen, vt_n = dl_geom(d)
        nd_ = nds[it % 2]
        ndv = nd_d.rearrange("(m r) g h c -> r m g h c", r=d)
        for r in range(d):
            dst = ndv[r][:, g, h, :].rearrange("(jj q) c -> q jj c", q=128)
            for j0 in range(0, nt, 8):
                j1 = min(nt, j0 + 8)
                P.dma("pool", dst[:, j0:j1, :], nd_[:, r * nt + j0:r * nt + j1, :], reads=[nd_.b], writes=[nd_b])

    prep(0)
    for it in range(len(items)):
        if it + 1 < len(items):
            prep(it + 1)
        compute(it)
        store(it)
    mt = [T("mt%d" % i, [128, 3, 8, 65]) for i in range(2)]
    ms = [T("ms%d" % i, [128, 8, 65]) for i in range(2)]
    mr = [T("mr%d" % i, [128, 8]) for i in range(2)]
    mo = [T("mo%d" % i, [128, 8, 64]) for i in range(2)]
    for tt in range(NST):
        i = tt % 2
        sl = slice(tt * 128, (tt + 1) * 128)
        P.dma("sp", mt[i][:], nd_d[sl], reads=[nd_b], writes=[mt[i].b])
        P.op("dve", "tensor_tensor", ms[i][:], mt[i][:, 0], mt[i][:, 1], ALU.add, reads=[mt[i].b], writes=[ms[i].b])
        P.op("dve", "tensor_tensor", ms[i][:], ms[i][:], mt[i][:, 2], ALU.add, reads=[mt[i].b, ms[i].b], writes=[ms[i].b])
        P.op("dve", "reciprocal", mr[i][:], ms[i][:, :, 64], reads=[ms[i].b], writes=[mr[i].b])
        P.op("dve", "tensor_tensor", mo[i][:], ms[i][:, :, 0:64], mr[i][:].unsqueeze(2).to_broadcast([128, 8, 64]),
             ALU.mult, reads=[ms[i].b, mr[i].b], writes=[mo[i].b])
        P.dma("sp", o_d[sl, hh * 512:(hh + 1) * 512].rearrange("t (h e) -> t h e", h=8), mo[i][:], reads=[mo[i].b], writes=[o_b])


def t5_bucket_np(rel):
    half, exact = 16, 8
    n = np.abs(rel)
    large = exact + (np.log(np.maximum(n, 1).astype(np.float32) / np.float32(exact))
                     / np.float32(np.log(1024 / exact)) * np.float32(half - exact)).astype(np.int32)
    large = np.minimum(large, half - 1)
    return np.where(rel > 0, half, 0) + np.where(n < exact, n, large)


def dl_bias_tables(rel_bias, heads):
    kk = np.arange(128)[:, None]
    qq = np.arange(128)[None, :]
    out = np.empty((128, 24, 256), np.float32)
    for g, (_, d) in enumerate(DL_PAIRS):
        bA = t5_bucket_np((kk - 64 - qq) * d)
        bB = t5_bucket_np((64 + kk - qq) * d)
        for hi, h in enumerate(heads):
            out[:, g * 8 + hi, 0:128] = np.where(kk >= qq, rel_bias[bA, h], NEG)
            out[:, g * 8 + hi, 128:256] = np.where(kk <= qq, rel_bias[bB, h], NEG)
    return out


def build_fused():
    P = Prog()
    nc = P.nc
    A = Arena(P, kib=207)
    dr = P.dram

    def scratch(name, shape):
        return nc.dram_tensor(name, list(shape), F32, kind="Internal").ap()

    def db(ap):
        return {"ap": ap, "b": Buf("d")}
    x_in = dr("x", [S, D])
    out_d = dr("out", [S, D], kind="ExternalOutput")
    ident = dr("ident", [128, 128], BF16)
    io = {
        "ident": ident, "identf": dr("identf", [128, 128]),
        "convw": dr("convw", [2, 128, 2, 6, 7]), "convb": dr("convb", [2, 128, 2, 6]),
        "dtb_rep": dr("dtb_rep", [2, 128, 2, 2, 8]), "alog_rep": dr("alog_rep", [2, 128, 2, 2, 8]),
        "d_rep": dr("d_rep", [2, 128, 2, 8]), "ssd_ng_rep": dr("ssd_ng_rep", [2, 128, 2, 512]),
        "tmat": dr("tmat", [128, 2, 128]), "negm": dr("negm", [128, 2, 4, 128]),
        "lbT": dr("lbT", [2, 128, 4, 4]), "hg_ng_rep": dr("hg_ng_rep", [2, 128, 512]),
        "smask": dr("smask", [128, 1024]), "masks": dr("masks", [32, 2, 32]),
        "cos": dr("cos", [S, 2, 32]), "sin": dr("sin", [S, 2, 32]), "gain": dr("gain", [128, 12, 128]),
        "dl_bias": dr("dl_bias", [2, 128, 24, 256]),
    }
    NIN = (5184, 5120, 6144, 10240)
    NFM = (3072, 3072, 2048, 6144)
    WOUT = (2048, 1024, 2048, 1024)
    g_rep = [dr("g_rep%d" % l, [128, D]) for l in range(4)]
    g_fin = dr("g_fin", [128, D])
    w_in = [dr("w_in%d" % l, [D, NIN[l]]) for l in range(4)]
    w_out = [dr("w_out%d" % l, [WOUT[l], D]) for l in range(4)]
    ufm = [scratch("ufm%d" % i, [6144, S]) for i in range(2)]
    utm = [scratch("utm%d" % i, [S, 4096]) for i in range(2)]
    o_tm = scratch("o_tm", [S, 2048])
    oT_fm = scratch("oT_fm", [2048, S])
    xs = [scratch("xs%d" % i, [S, D]) for i in range(2)]
    od = scratch("od", [2, S, 512])
    nd = scratch("nd", [S, 3, 8, 65])

    def lin(layer, x_src, c, x_dst, final=False):
        for t0 in (0, TT):
            a = None
            if layer < 4:
                a = {"g_rep": g_rep[layer], "w_in": w_in[layer], "n_fm": NFM[layer], "n_tm": NIN[layer] - NFM[layer],
                     "ufm": ufm[layer % 2], "utm": utm[layer % 2]}
            cc = None
            if c is not None:
                cc = dict(c)
                cc["o_b"] = Buf("o")
                cc["gate_b"] = Buf("g")
                cc["xout"] = db(x_dst) if x_dst is not None else None
            fin = {"g_rep": g_fin, "out": out_d} if final else None
            emit_lin(P, A, t0, ident, db(x_src), c=cc, a=a, fin=fin)

    lin(0, x_in, None, None)
    for hh in range(2):
        emit_ssd(P, A, hh, dict(io, ufm=ufm[0], utm=utm[0], o=o_tm))
    lin(1, x_in, {"mode": "tm", "W": 2048, "o": o_tm, "gate": None, "w_out": w_out[0]}, xs[0])
    for hh in range(2):
        emit_hg(P, A, hh, dict(io, ufm=ufm[1], utm=utm[1], od=od, o=o_tm))
    lin(2, xs[0], {"mode": "tm", "W": 1024, "o": o_tm, "gate": utm[1][:, 1024:2048], "w_out": w_out[1]}, xs[1])
    for hh in range(2):
        emit_at(P, A, hh, dict(io, utm=utm[0], oT=oT_fm))
    lin(3, xs[1], {"mode": "fm", "W": 2048, "o": oT_fm, "gate": ufm[0][0:2048, :], "w_out": w_out[2]}, xs[0])
    for hh in range(2):
        emit_dl(P, A, hh, dict(io, ufm=ufm[1], utm=utm[1], nd=nd, o=o_tm))
    lin(4, xs[0], {"mode": "tm", "W": 1024, "o": o_tm, "gate": utm[1][:, 3072:4096], "w_out": w_out[3]}, None, final=True)
    print("fused program ops:", P.n_ops, dict(P.ecnt))
    return P.emit()


def fused_inputs(x, norm_g, final_g, rel_bias, hgrn_lb,
                 ssd_w_in, ssd_conv_w, ssd_conv_b, ssd_dt_bias, ssd_a_log, ssd_d, ssd_norm_g, ssd_w_out,
                 hg_w_in, hg_norm_g, hg_w_out, at_w_in, at_q_norm_g, at_k_norm_g, at_w_out, dl_w_in, dl_w_out):
    f = lambda a: np.ascontiguousarray(np.asarray(a, dtype=np.float32))
    rep = lambda v: np.ascontiguousarray(np.broadcast_to(f(v)[None], (128,) + tuple(np.shape(v))))
    tm, nm = ssd_consts()
    sm, masks = hg_consts()
    cos, sin = rope_tables()
    w0 = f(ssd_w_in)[0]
    w2 = f(at_w_in)[0]
    w3 = f(dl_w_in)[0]
    fm3 = np.concatenate([np.arange(g * 3072 + s * 1024, g * 3072 + (s + 1) * 1024) for g in range(3) for s in range(2)])
    tm3 = np.concatenate([np.arange(g * 3072 + 2048, g * 3072 + 3072) for g in range(3)] + [np.arange(9216, 10240)])
    small = [ssd_small(hh, f(ssd_conv_w)[0], f(ssd_conv_b)[0], f(ssd_dt_bias)[0], f(ssd_a_log)[0], f(ssd_d)[0], f(ssd_norm_g)[0])
             for hh in range(2)]
    lb4 = f(hgrn_lb).reshape(4, 8, 128)
    gain = np.concatenate([np.broadcast_to(f(at_q_norm_g)[0][None, None], (128, 8, 128)),
                           np.broadcast_to(f(at_k_norm_g)[0][None, None], (128, 4, 128))], 1)
    common = {
        "ident": bf16_np(np.eye(128)), "identf": np.eye(128, dtype=np.float32),
        "convw": np.stack([s_[0] for s_ in small]), "convb": np.stack([s_[1] for s_ in small]),
        "dtb_rep": np.stack([s_[2] for s_ in small]), "alog_rep": np.stack([s_[3] for s_ in small]),
        "d_rep": np.stack([s_[4] for s_ in small]), "ssd_ng_rep": np.stack([s_[5] for s_ in small]),
        "tmat": tm, "negm": nm,
        "lbT": np.stack([np.ascontiguousarray(lb4[:, 4 * hh:4 * hh + 4].transpose(2, 0, 1)) for hh in range(2)]),
        "hg_ng_rep": np.stack([rep(f(hg_norm_g)[0][hh * 512:(hh + 1) * 512]) for hh in range(2)]),
        "smask": sm, "masks": masks, "cos": cos, "sin": sin, "gain": np.ascontiguousarray(gain, dtype=np.float32),
        "dl_bias": np.stack([dl_bias_tables(f(rel_bias), list(range(8 * hh, 8 * hh + 8))) for hh in range(2)]),
        "g_rep0": rep(f(norm_g)[0]), "g_rep1": rep(f(norm_g)[1]), "g_rep2": rep(f(norm_g)[2]), "g_rep3": rep(f(norm_g)[3]),
        "g_fin": rep(f(final_g)),
        "w_in0": np.ascontiguousarray(np.concatenate([w0[:, 2048:5120], w0[:, 0:2048], w0[:, 5120:5184]], 1)),
        "w_in1": f(hg_w_in)[0],
        "w_in2": np.ascontiguousarray(np.concatenate([w2[:, 4096:6144], w2[:, 0:4096]], 1)),
        "w_in3": np.ascontiguousarray(np.concatenate([w3[:, fm3], w3[:, tm3]], 1)),
        "w_out0": f(ssd_w_out)[0], "w_out1": f(hg_w_out)[0], "w_out2": f(at_w_out)[0], "w_out3": f(dl_w_out)[0],
    }
    x = f(x)
    return [dict(common, x=x[core % x.shape[0]]) for core in range(8)]


def kernel(**inputs):
    B = np.asarray(inputs["x"]).shape[0]
    maps = fused_inputs(**inputs)
    nc = build_fused()
    res = run_bass_kernel_spmd(nc, maps, core_ids=list(range(8)))
    return np.stack([res.results[b]["out"] for b in range(B)], 0)
```

```python
from contextlib import ExitStack
import numpy as np
import ml_dtypes
import concourse.bass as bass
import concourse.mybir as mybir
from concourse.bass_utils import run_bass_kernel_spmd

F32 = mybir.dt.float32
BF16 = mybir.dt.bfloat16
I32 = mybir.dt.int32
AF = mybir.ActivationFunctionType
ALU = mybir.AluOpType
AX = mybir.AxisListType

ENGS = ("pe", "act", "dve", "pool", "sp")
DMA_SLOTS = 6


class Buf:
    __slots__ = ("name", "w", "r", "excl")

    def __init__(self, name, excl=False):
        self.name = name
        self.excl = excl
        self.w = None
        self.r = []


class Prog:
    def __init__(self):
        self.nc = bass.Bass("TRN2", target_bir_lowering=False)
        self.es = ExitStack()
        self.q = {e: [] for e in ENGS}
        self.sem = {}
        self.seen = {e: {} for e in ENGS}
        self.ecnt = {e: 0 for e in ENGS}
        self.dma_i = {qe: 0 for qe in ("sp", "act", "pool")}
        self.dcnt = {}
        self.n_ops = 0

    EPOCH = 20000

    def _sem(self, k):
        if k not in self.sem:
            self.sem[k] = self.es.enter_context(self.nc.semaphore("s%d" % len(self.sem)))
        return self.sem[k]

    def dram(self, name, shape, dtype=F32, kind="ExternalInput"):
        return self.nc.dram_tensor(name, list(shape), dtype, kind=kind).ap()

    def sbuf(self, name, shape, dtype=F32):
        return self.es.enter_context(self.nc.sbuf_tensor(name, list(shape), dtype))

    def psum(self, name, shape, dtype=F32):
        return self.es.enter_context(self.nc.psum_tensor(name, list(shape), dtype))

    def _deps(self, reads, writes):
        need = {}
        def add(ev):
            if ev is None:
                return
            k, v = ev
            if need.get(k, 0) < v:
                need[k] = v
        for b in reads:
            add(b.w)
            if b.excl:
                for ev in b.r:
                    add(ev)
        for b in writes:
            add(b.w)
            for ev in b.r:
                add(ev)
        return need

    def _commit(self, ev, reads, writes):
        for b in reads:
            if b.excl:
                b.r = [ev]
            else:
                b.r.append(ev)
        for b in writes:
            b.w = ev
            b.r = []

    def _waits(self, eng, need):
        ws = []
        seen = self.seen[eng]
        for k, v in need.items():
            if eng == "pe" and k[0] == "pe":
                continue
            if seen.get(k, 0) < v:
                seen[k] = v
                ws.append((k, v))
        return ws

    def op(self, eng, meth, *args, reads=(), writes=(), **kw):
        fn = (meth, args, kw)
        need = self._deps(reads, writes)
        ws = self._waits(eng, need)
        n = self.ecnt[eng]
        self.ecnt[eng] = n + 1
        k = (eng, n // self.EPOCH)
        ev = (k, n % self.EPOCH + 1)
        self.q[eng].append((ws, fn, k, 1))
        self._commit(ev, reads, writes)
        self.n_ops += 1
        return ev

    def dma(self, qe, out, in_, reads=(), writes=(), **kw):
        need = self._deps(reads, writes)
        i = self.dma_i[qe]
        self.dma_i[qe] = i + 1
        slot = i % DMA_SLOTS
        ep = (i // DMA_SLOTS) // 1000
        k = ("dma", qe, slot, ep)
        c = self.dcnt.get(k, 0)
        if c > 0:
            need[k] = max(need.get(k, 0), c)
        elif ep > 0:
            kp = ("dma", qe, slot, ep - 1)
            need[kp] = max(need.get(kp, 0), self.dcnt[kp])
        ws = self._waits(qe, need)
        self.dcnt[k] = c + 16
        ev = (k, c + 16)
        self.q[qe].append((ws, ("dma_start", (), dict(out=out, in_=in_, **kw)), k, 16))
        self._commit(ev, reads, writes)
        self.n_ops += 1
        return ev

    def wait_all_dma(self):
        for qe in ("sp", "act", "pool"):
            need = {k: c for k, c in self.dcnt.items() if k[1] == qe}
            ws = self._waits(qe, need)
            if ws:
                self.q[qe].append((ws, None, None, 0))


    def fence(self):
        need = {}
        for e in ENGS:
            n = self.ecnt[e]
            if n > 0:
                need[(e, (n - 1) // self.EPOCH)] = (n - 1) % self.EPOCH + 1
        for k, c in self.dcnt.items():
            need[k] = c
        for e in ENGS:
            ws = self._waits(e, dict(need))
            if ws:
                self.q[e].append((ws, None, None, 0))

    def emit(self):
        self.wait_all_dma()
        nc = self.nc
        engobj = {"pe": "tensor", "act": "scalar", "dve": "vector", "pool": "gpsimd", "sp": "sync"}
        with nc.Block() as block:
            for e in ENGS:
                items = self.q[e]

                def body(eng, items=items):
                    for ws, fn, sk, inc in items:
                        for k, v in ws:
                            eng.wait_ge(self._sem(k), v)
                        if fn is not None:
                            ins = getattr(eng, fn[0])(*fn[1], **fn[2])
                            ins.then_inc(self._sem(sk), inc)
                getattr(block, engobj[e])(body)
        self.es.close()
        return nc


class T:
    def __init__(self, P, name, shape, dtype=F32, psum=False):
        if psum:
            esz = 4 if dtype == F32 else 2
            n = int(np.prod(shape[1:]))
            assert n * esz <= 2048
            h = P.psum("t_" + name, [128, 2048 // esz], dtype)
            v = h[0:shape[0], 0:n]
            if len(shape) == 3:
                v = v.rearrange("p (a b) -> p a b", b=shape[2])
            elif len(shape) == 4:
                v = v.rearrange("p (a b c) -> p a b c", b=shape[2], c=shape[3])
            self.t = v
        else:
            self.t = P.sbuf("t_" + name, shape, dtype)
        self.b = Buf(name, excl=psum)

    def __getitem__(self, k):
        return self.t[k]


def bf16_np(a):
    return np.asarray(a, dtype=np.float32).astype(ml_dtypes.bfloat16)


class TV:
    def __init__(self, ap, name, excl):
        self.t = ap
        self.b = Buf(name, excl=excl)

    def __getitem__(self, k):
        return self.t[k]


class Arena:
    def __init__(self, P, kib=188):
        self.P = P
        self.n32 = kib * 256
        self.h = P.sbuf("arena", [128, self.n32], F32)
        self.banks = [P.psum("bank%d" % i, [128, 512], F32) for i in range(8)]
        self.off = 0
        self.nb = 0

    def reset(self):
        self.P.fence()
        self.off = 0
        self.nb = 0

    def T(self, name, shape, dtype=F32, psum=False):
        esz = 4 if dtype == F32 else 2
        n = int(np.prod(shape[1:]))
        n32 = (n * esz + 3) // 4
        if psum:
            assert n32 <= 512 and self.nb < 8, name
            v = self.banks[self.nb][0:shape[0], 0:n32]
            self.nb += 1
        else:
            n32 = (n32 + 7) // 8 * 8
            assert self.off + n32 <= self.n32, (name, self.off, n32)
            v = self.h[0:shape[0], self.off:self.off + n32]
            self.off += n32
        if dtype != F32:
            v = v.bitcast(dtype)
        v = v[:, 0:n]
        if len(shape) == 3:
            v = v.rearrange("p (a b) -> p a b", b=shape[2])
        elif len(shape) == 4:
            v = v.rearrange("p (a b c) -> p a b c", b=shape[2], c=shape[3])
        return TV(v, name, psum)


TT = 2048
NTT = TT // 128
D = 1024
EPS = 1e-6
S = 4096
NST = S // 128


def emit_lin(P, A, t0, ident_d, x_d, c=None, a=None, fin=None):
    A.reset()
    T = A.T
    has_c, has_a, final = c is not None, a is not None, fin is not None
    ident = T("ident", [128, 128], BF16)
    P.dma("sp", ident[:], ident_d, writes=[ident.b])
    wstage = [T("wstage%d" % i, [128, 4, 512]) for i in range(2)]
    ws_i = [0]

    def load_w_bf(dst, dst_k0, src_ap, ncols):
        st = wstage[ws_i[0] % 2]
        ws_i[0] += 1
        P.dma("sp", st[:, :, 0:ncols], src_ap, writes=[st.b])
        P.op("pool", "tensor_copy", dst[:, dst_k0:dst_k0 + 4, 0:ncols], st[:, :, 0:ncols], reads=[st.b], writes=[dst.b])

    xt = [T("xt%d" % i, [128, D]) for i in range(3)]
    tps = T("tps", [128, 8, 128], BF16, psum=True)
    if has_c:
        W, mode = c["W"], c["mode"]
        WC = W // 128
        use_gate = c["gate"] is not None
        wout = T("wout", [128, WC, D], BF16)
        wv = c["w_out"].rearrange("(kc p) n -> p kc n", p=128)
        for k0 in range(0, WC, 4):
            for nb in range(2):
                st = wstage[ws_i[0] % 2]
                ws_i[0] += 1
                P.dma("sp", st[:, :, :], wv[:, k0:k0 + 4, nb * 512:(nb + 1) * 512], writes=[st.b])
                P.op("pool", "tensor_copy", wout[:, k0:k0 + 4, nb * 512:(nb + 1) * 512], st[:, :, :],
                     reads=[st.b], writes=[wout.b])
        gTb = T("gTb", [128, WC, 512], BF16)
        yps = [[T("yps%d_%d" % (i, nb), [128, 512], F32, psum=True) for nb in range(2)] for i in range(2)]
        xout_b = Buf("xout")
        if mode == "fm":
            o_sb = [T("o_sb%d" % i, [128, 4, 512]) for i in range(2)]
            g_sb = [T("g_sb%d" % i, [128, 4, 512]) for i in range(2)] if use_gate else None
        else:
            o_tm = [T("o_tm%d" % i, [128, W]) for i in range(2)]
            g_tm = [T("g_tm%d" % i, [128, W]) for i in range(2)] if use_gate else None
            g_bf = [T("g_bf%d" % i, [128, W], BF16) for i in range(2)]
    if has_a or final:
        grep = T("grep", [128, D])
        P.dma("sp", grep[:], (a or fin)["g_rep"], writes=[grep.b])
        ssq = T("ssq", [128, 1])
        rstd = T("rstd", [128, 1])
        junk = T("junk", [128, D])
    if has_a:
        n_fm, n_tm = a["n_fm"], a["n_tm"]
        hnb = [T("hnb%d" % i, [128, D], BF16) for i in range(2)]
        hnT = T("hnT", [128, 8, TT], BF16)
        hnT_tb = [Buf("hnT%d" % i) for i in range(NTT)]
        winb = [T("winb%d" % i, [128, 8, 512], BF16) for i in range(2)]
        ups = [T("ups%d" % i, [128, 512], F32, psum=True) for i in range(3)]
        ust = [T("ust%d" % i, [128, 512]) for i in range(4)]
        u_b = Buf("u_scratch")
    if final:
        out_b = Buf("out_d")
        ost = [T("ost%d" % i, [128, D]) for i in range(2)]

    xi = 0
    ci = 0
    for blk in range(TT // 512):
        tb0 = t0 + blk * 512
        if has_c and mode == "fm":
            ov = c["o"].rearrange("(kc p) t -> p kc t", p=128)
            gv = c["gate"].rearrange("(kc p) t -> p kc t", p=128) if use_gate else None
            for k0 in range(0, WC, 4):
                i = ci % 2
                ci += 1
                ot = o_sb[i]
                P.dma("sp", ot[:], ov[:, k0:k0 + 4, tb0:tb0 + 512], reads=[c["o_b"]], writes=[ot.b])
                if use_gate:
                    gt = g_sb[i]
                    P.dma("sp", gt[:], gv[:, k0:k0 + 4, tb0:tb0 + 512], reads=[c["gate_b"]], writes=[gt.b])
                    P.op("act", "activation", gt[:], gt[:], AF.Silu, reads=[gt.b], writes=[gt.b])
                    P.op("dve", "tensor_tensor", gTb[:, k0:k0 + 4, :], ot[:], gt[:], ALU.mult, reads=[ot.b, gt.b], writes=[gTb.b])
                else:
                    P.op("dve", "tensor_copy", gTb[:, k0:k0 + 4, :], ot[:], reads=[ot.b], writes=[gTb.b])
        if has_c and mode == "tm":
            for ti in range(4):
                tsl = slice(tb0 + ti * 128, tb0 + (ti + 1) * 128)
                i = ci % 2
                ci += 1
                ot, gb = o_tm[i], g_bf[i]
                P.dma("sp", ot[:], c["o"][tsl, 0:W], reads=[c["o_b"]], writes=[ot.b])
                if use_gate:
                    gt = g_tm[i]
                    P.dma("sp", gt[:], c["gate"][tsl, :], reads=[c["gate_b"]], writes=[gt.b])
                    P.op("act", "activation", gt[:], gt[:], AF.Silu, reads=[gt.b], writes=[gt.b])
                    P.op("dve", "tensor_tensor", gb[:], ot[:], gt[:], ALU.mult, reads=[ot.b, gt.b], writes=[gb.b])
                else:
                    P.op("dve", "tensor_copy", gb[:], ot[:], reads=[ot.b], writes=[gb.b])
                for w0 in range(0, WC, 8):
                    nw = min(8, WC - w0)
                    for k in range(nw):
                        P.op("pe", "transpose", tps[:, k, :], gb[:, (w0 + k) * 128:(w0 + k + 1) * 128], ident[:],
                             reads=[gb.b, ident.b], writes=[tps.b])
                    P.op("act", "copy", gTb[:, w0:w0 + nw, ti * 128:(ti + 1) * 128], tps[:, 0:nw, :],
                         reads=[tps.b], writes=[gTb.b])
        for ti in range(4):
            tt = blk * 4 + ti
            tsl = slice(tb0 + ti * 128, tb0 + (ti + 1) * 128)
            xb = xt[xi % 3]
            xi += 1
            P.dma("sp", xb[:], x_d["ap"][tsl, :], reads=[x_d["b"]], writes=[xb.b])
            if has_c:
                yp = yps[tt % 2]
                for nb in range(2):
                    for wc in range(WC):
                        P.op("pe", "matmul", yp[nb][:], gTb[:, wc, ti * 128:(ti + 1) * 128],
                             wout[:, wc, nb * 512:(nb + 1) * 512], start=(wc == 0), stop=(wc == WC - 1),
                             reads=[gTb.b, wout.b], writes=[yp[nb].b])
                for nb in range(2):
                    P.op("dve", "tensor_tensor", xb[:, nb * 512:(nb + 1) * 512], xb[:, nb * 512:(nb + 1) * 512], yp[nb][:],
                         ALU.add, reads=[xb.b, yp[nb].b], writes=[xb.b])
                if not final:
                    P.dma("act", c["xout"]["ap"][tsl, :], xb[:], reads=[xb.b], writes=[c["xout"]["b"]])
            if has_a or final:
                P.op("act", "activation", junk[:], xb[:], AF.Square, accum_out=ssq[:], reads=[xb.b], writes=[junk.b, ssq.b])
                P.op("act", "activation", rstd[:], ssq[:], AF.Sqrt, bias=EPS, scale=1.0 / D, reads=[ssq.b], writes=[rstd.b])
                P.op("dve", "reciprocal", rstd[:], rstd[:], reads=[rstd.b], writes=[rstd.b])
            if final:
                ot_ = ost[tt % 2]
                P.op("dve", "scalar_tensor_tensor", ot_[:], xb[:], rstd[:, 0:1], grep[:], ALU.mult, ALU.mult,
                     reads=[xb.b, rstd.b, grep.b], writes=[ot_.b])
                P.dma("act", fin["out"][tsl, :], ot_[:], reads=[ot_.b], writes=[out_b])
            if has_a:
                hb = hnb[tt % 2]
                P.op("dve", "scalar_tensor_tensor", hb[:], xb[:], rstd[:, 0:1], grep[:], ALU.mult, ALU.mult,
                     reads=[xb.b, rstd.b, grep.b], writes=[hb.b])
                for kc in range(8):
                    P.op("pe", "transpose", tps[:, kc, :], hb[:, kc * 128:(kc + 1) * 128], ident[:],
                         reads=[hb.b, ident.b], writes=[tps.b])
                P.op("act", "copy", hnT[:, :, tt * 128:(tt + 1) * 128], tps[:], reads=[tps.b], writes=[hnT_tb[tt]])
    if has_a:
        N = n_fm + n_tm
        wv = a["w_in"].rearrange("(kc p) n -> p kc n", p=128)
        nblk = (N + 511) // 512
        assert n_fm % 512 == 0
        ui = 0
        for nb in range(nblk):
            c0 = nb * 512
            cw = min(512, N - c0)
            wb = winb[nb % 2]
            for k0 in (0, 4):
                load_w_bf(wb, k0, wv[:, k0:k0 + 4, c0:c0 + cw], cw)
            if c0 < n_fm:
                for cc in range(4):
                    for tb in range(TT // 512):
                        up, us = ups[ui % 3], ust[ui % 4]
                        for kc in range(8):
                            P.op("pe", "matmul", up[:], wb[:, kc, cc * 128:(cc + 1) * 128], hnT[:, kc, tb * 512:(tb + 1) * 512],
                                 start=(kc == 0), stop=(kc == 7),
                                 reads=[wb.b] + hnT_tb[tb * 4:tb * 4 + 4], writes=[up.b])
                        P.op("act", "copy", us[:], up[:], reads=[up.b], writes=[us.b])
                        r0 = c0 + cc * 128
                        P.dma("act", a["ufm"][r0:r0 + 128, t0 + tb * 512:t0 + (tb + 1) * 512], us[:], reads=[us.b], writes=[u_b])
                        ui += 1
            else:
                for tt in range(NTT):
                    up, us = ups[ui % 3], ust[ui % 4]
                    for kc in range(8):
                        P.op("pe", "matmul", up[:, 0:cw], hnT[:, kc, tt * 128:(tt + 1) * 128], wb[:, kc, 0:cw],
                             start=(kc == 0), stop=(kc == 7), reads=[hnT_tb[tt], wb.b], writes=[up.b])
                    P.op("act", "copy", us[:, 0:cw], up[:, 0:cw], reads=[up.b], writes=[us.b])
                    P.dma("act", a["utm"][t0 + tt * 128:t0 + (tt + 1) * 128, c0 - n_fm:c0 - n_fm + cw], us[:, 0:cw],
                          reads=[us.b], writes=[u_b])
                    ui += 1


SSD_NEG = -30000.0


def emit_ssd(P, A, hh, io):
    A.reset()
    T = A.T
    NCK = S // 128
    ufm, utm, o_d = io["ufm"], io["utm"], io["o"]
    o_b = Buf("o_d")

    def const(name, d_ap, shape, dtype=F32):
        t = T(name, shape, dtype)
        P.dma("sp", t[:], d_ap, writes=[t.b])
        return t
    cw = const("cw", io["convw"][hh], [128, 2, 6, 7])
    cbias = const("cbias", io["convb"][hh], [128, 2, 6])
    dtb = const("dtb", io["dtb_rep"][hh], [128, 2, 2, 8])
    alog = const("alog", io["alog_rep"][hh], [128, 2, 2, 8])
    dsk = const("dsk", io["d_rep"][hh], [128, 2, 8])
    ngr = const("ngr", io["ssd_ng_rep"][hh], [128, 2, 512])
    tmat = const("tmatc", io["tmat"], [128, 2, 128])
    negm = const("negmc", io["negm"], [128, 2, 4, 128])
    identf = const("identf", io["identf"], [128, 128])
    ident = const("ident", io["ident"], [128, 128], BF16)
    onesf = T("onesf", [128, 128])
    P.op("pool", "memset", onesf[:], 1.0, writes=[onesf.b])
    P.op("act", "activation", alog[:], alog[:], AF.Exp, reads=[alog.b], writes=[alog.b])
    P.op("dve", "tensor_scalar", alog[:], alog[:], -1.0, None, ALU.mult, reads=[alog.b], writes=[alog.b])

    xcT = T("xcT", [128, 6, S], BF16)
    yb = T("yb", [128, NCK, 512])
    ybf = yb[:].rearrange("p c e -> p (c e)")
    raws = [ybf[:, 0:S + 6], ybf[:, 4104:4104 + S + 6]]
    raw_b = [Buf("raw0"), Buf("raw1")]
    rawbf = [ybf[:, 8208:8208 + 2056].bitcast(BF16)[:, 0:S + 6], ybf[:, 10272:10272 + 2056].bitcast(BF16)[:, 0:S + 6]]
    rawbf_b = [Buf("rawbf0"), Buf("rawbf1")]
    dgs = [T("dg%d" % i, [128, 7, 128], BF16) for i in range(2)]
    rci = 0
    cpi = 0
    dtall = T("dtall", [128, NCK, 2, 8])
    dA = T("dA", [128, NCK, 2, 8])
    ncum = T("ncum", [128, NCK, 2, 8])
    ecum = T("ecum", [128, NCK, 2, 8])
    dtd = T("dtd", [128, NCK, 2, 8])
    etot = T("etot", [128, NCK, 2, 8])
    fr = T("fr", [128, 512], F32, psum=True)
    tpx = fr[:, 0:256].bitcast(BF16).rearrange("p (k e) -> p k e", e=128)
    tpb = fr[:, 256:320].bitcast(BF16)
    cbp = fr[:, 320:448]
    bcp = [T("bcp%d" % i, [128, 4, 128], F32, psum=True) for i in range(4)]
    yps = T("yps", [128, 512], F32, psum=True)
    yop = T("yop", [128, 512], F32, psum=True)
    stp = T("stp", [128, 512], F32, psum=True)
    xst = [T("xst%d" % i, [128, 8, 64], BF16) for i in range(2)]
    bst = [T("bst%d" % i, [128, 128], BF16) for i in range(2)]
    cbT = [T("cbT%d" % i, [128, 128]) for i in range(2)]
    dAb = [T("dAb%d" % i, [128, 8, 128]) for i in range(2)]
    xdt = [T("xdt%d" % i, [128, 8, 64], BF16) for i in range(2)]
    xdd = [T("xdd%d" % i, [128, 8, 64], BF16) for i in range(2)]
    NLT = 6
    LT = [T("LT%d" % i, [128, 128]) for i in range(NLT)]
    MT = [T("MT%d" % i, [128, 128], BF16) for i in range(NLT)]
    ytm = [T("ytm%d" % i, [128, 8, 64]) for i in range(2)]
    yt = [T("yt%d" % i, [128, 8, 64]) for i in range(2)]
    zt = [T("zt%d" % i, [128, 512]) for i in range(2)]
    Sst = T("Sst", [128, 8, 64])
    Sb = [T("Sb%d" % i, [128, 512], BF16) for i in range(2)]
    ssq = T("ssq", [128, 1])
    junk = T("junk", [128, 512])
    li = 0

    for gl in range(2):
        g = 2 * hh + gl
        rows = [g * 512 + k * 128 for k in range(4)] + [2048 + g * 128, 2048 + 512 + g * 128]
        P.fence()
        cps = [yps, yop, stp]
        for ci in range(6):
            raw, rb = raws[rci % 2], raw_b[rci % 2]
            rwb, rwb_b = rawbf[rci % 2], rawbf_b[rci % 2]
            rci += 1
            P.op("pool", "memset", raw[:, 0:3], 0.0, writes=[rb])
            P.op("pool", "memset", raw[:, S + 3:S + 6], 0.0, writes=[rb])
            P.dma("sp", raw[:, 3:3 + S], ufm[rows[ci]:rows[ci] + 128, :], writes=[rb])
            P.op("act", "copy", rwb, raw, reads=[rb], writes=[rwb_b])
            dg = dgs[ci % 2]
            P.op("dve", "tensor_tensor", dg[:], ident[:].unsqueeze(1).to_broadcast([128, 7, 128]),
                 cw[:, gl, ci, :].unsqueeze(2).to_broadcast([128, 7, 128]), ALU.mult, reads=[ident.b, cw.b], writes=[dg.b])
            for blk in range(S // 512):
                cp = cps[cpi % 3]
                cpi += 1
                for k in range(7):
                    P.op("pe", "matmul", cp[:], dg[:, k, :], rwb[:, blk * 512 + k:blk * 512 + k + 512], start=(k == 0), stop=(k == 6),
                         reads=[dg.b, rwb_b], writes=[cp.b])
                P.op("act", "activation", xcT[:, ci, blk * 512:(blk + 1) * 512], cp[:], AF.Silu, bias=cbias[:, gl, ci:ci + 1],
                     reads=[cp.b, cbias.b], writes=[xcT.b])
        P.fence()
        for dr in range(2):
            c0 = 2048 + dr * 32 + g * 8
            P.dma("sp", dtall[:, :, dr, :], utm[:, c0:c0 + 8].rearrange("(c p) j -> p c j", p=128), writes=[dtall.b])
        P.op("dve", "tensor_tensor", dtall[:], dtall[:], dtb[:, gl].unsqueeze(1).to_broadcast([128, NCK, 2, 8]), ALU.add,
             reads=[dtall.b, dtb.b], writes=[dtall.b])
        P.op("act", "activation", dtall[:], dtall[:], AF.Exp, reads=[dtall.b], writes=[dtall.b])
        P.op("act", "activation", dtall[:], dtall[:], AF.Ln, bias=1.0, reads=[dtall.b], writes=[dtall.b])
        P.op("dve", "tensor_tensor", dA[:], dtall[:], alog[:, gl].unsqueeze(1).to_broadcast([128, NCK, 2, 8]), ALU.mult,
             reads=[dtall.b, alog.b], writes=[dA.b])
        cum_ps = yps[:, 0:256].rearrange("p (c j) -> p c j", j=8)
        tot_ps = yop[:, 0:256].rearrange("p (c j) -> p c j", j=8)
        for dr in range(2):
            P.op("pe", "matmul", cum_ps, tmat[:, dr, :], dA[:, :, dr, :], start=True, stop=True, reads=[tmat.b, dA.b], writes=[yps.b])
            P.op("pe", "matmul", tot_ps, onesf[:], dA[:, :, dr, :], start=True, stop=True, reads=[onesf.b, dA.b], writes=[yop.b])
            P.op("dve", "tensor_scalar", ncum[:, :, dr, :], cum_ps, -1.0, None, ALU.mult, reads=[yps.b], writes=[ncum.b])
            P.op("act", "activation", ecum[:, :, dr, :], cum_ps, AF.Exp, reads=[yps.b], writes=[ecum.b])
            P.op("act", "activation", etot[:, :, dr, :], tot_ps, AF.Exp, reads=[yop.b], writes=[etot.b])
            P.op("dve", "tensor_tensor", dtd[:, :, dr, :], ncum[:, :, dr, :], tot_ps, ALU.add, reads=[ncum.b, yop.b], writes=[dtd.b])
        P.op("act", "activation", dtd[:], dtd[:], AF.Exp, reads=[dtd.b], writes=[dtd.b])
        P.op("dve", "tensor_tensor", dtd[:], dtd[:], dtall[:], ALU.mult, reads=[dtd.b, dtall.b], writes=[dtd.b])

        for dr in (1, 0):
            P.op("dve", "memset", Sst[:], 0.0, writes=[Sst.b])
            P.op("pool", "memset", Sb[0][:], 0.0, writes=[Sb[0].b])
            order = list(range(NCK)) if dr == 0 else list(range(NCK - 1, -1, -1))
            def front(n, c):
                cs = slice(c * 128, (c + 1) * 128)
                i2 = n % 2
                xs_, bs_, cb_, dab_, xd_, xq_ = xst[i2], bst[i2], cbT[i2], dAb[i2], xdt[i2], xdd[i2]
                for k in range(4):
                    P.op("pe", "transpose", tpx[:, k, :], xcT[:, k, cs], ident[:], reads=[xcT.b, ident.b], writes=[fr.b])
                P.op("pe", "transpose", tpb, xcT[:, 4, cs], ident[:], reads=[xcT.b, ident.b], writes=[fr.b])
                P.op("pe", "matmul", cbp, xcT[:, 4, cs], xcT[:, 5, cs], start=True, stop=True, reads=[xcT.b], writes=[fr.b])
                P.op("act", "copy", xs_[:].rearrange("p j e -> p (j e)"), tpx.rearrange("p k e -> p (k e)"),
                     reads=[fr.b], writes=[xs_.b])
                P.op("act", "copy", bs_[:], tpb, reads=[fr.b], writes=[bs_.b])
                P.op("act", "copy", cb_[:], cbp, reads=[fr.b], writes=[cb_.b])
                P.op("dve", "tensor_tensor", xd_[:], xs_[:], dtall[:, c, dr, :].unsqueeze(2).to_broadcast([128, 8, 64]), ALU.mult,
                     reads=[xs_.b, dtall.b], writes=[xd_.b])
                P.op("pool", "tensor_tensor", xq_[:], xs_[:], dtd[:, c, dr, :].unsqueeze(2).to_broadcast([128, 8, 64]), ALU.mult,
                     reads=[xs_.b, dtd.b], writes=[xq_.b])
                P.op("dve", "tensor_tensor", dab_[:], tmat[:, dr, :].unsqueeze(1).to_broadcast([128, 8, 128]),
                     dA[:, c, dr, :].unsqueeze(2).to_broadcast([128, 8, 128]), ALU.mult,
                     reads=[tmat.b, dA.b], writes=[dab_.b])
                for h2 in range(2):
                    bp = bcp[i2 * 2 + h2]
                    P.op("pe", "matmul", bp[:].rearrange("p r l -> p (r l)"), identf[:],
                         negm[:, dr].rearrange("p r l -> p (r l)"), start=True, stop=False,
                         reads=[identf.b, negm.b], writes=[bp.b])
                    P.op("pe", "matmul", bp[:].rearrange("p r l -> p (r l)"), onesf[:],
                         dab_[:, h2 * 4:h2 * 4 + 4, :].rearrange("p r l -> p (r l)"), start=False, stop=True,
                         reads=[onesf.b, dab_.b], writes=[bp.b])
            front(0, order[0])
            for n, c in enumerate(order):
                cs = slice(c * 128, (c + 1) * 128)
                i2 = n % 2
                xs_, bs_, cb_, dab_, xd_, xq_ = xst[i2], bst[i2], cbT[i2], dAb[i2], xdt[i2], xdd[i2]
                sb_cur, sb_nxt = Sb[n % 2], Sb[(n + 1) % 2]
                if n + 1 < NCK:
                    front(n + 1, order[n + 1])
                for h2 in range(2):
                    bp = bcp[i2 * 2 + h2]
                    for jj in range(4):
                        j = h2 * 4 + jj
                        lt, mt = LT[li % NLT], MT[li % NLT]
                        li += 1
                        P.op("act", "activation", lt[:], bp[:, jj, :], AF.Exp, bias=ncum[:, c, dr, j:j + 1],
                             reads=[bp.b, ncum.b], writes=[lt.b])
                        P.op("pool" if li % 2 == 0 else "dve", "tensor_tensor", mt[:], lt[:], cb_[:], ALU.mult,
                             reads=[lt.b, cb_.b], writes=[mt.b])
                        P.op("pe", "matmul", yps[:, j * 64:(j + 1) * 64], mt[:], xd_[:, j, :], start=True, stop=True,
                             reads=[mt.b, xd_.b], writes=[yps.b])
                P.op("pe", "matmul", yop[:], xcT[:, 5, cs], sb_cur[:], start=True, stop=True, reads=[xcT.b, sb_cur.b], writes=[yop.b])
                ym, y_ = ytm[i2], yt[i2]
                P.op("dve", "tensor_tensor", ym[:], yop[:].rearrange("p (j e) -> p j e", e=64),
                     ecum[:, c, dr, :].unsqueeze(2).to_broadcast([128, 8, 64]), ALU.mult, reads=[yop.b, ecum.b], writes=[ym.b])
                if dr == 1:
                    P.op("dve", "tensor_tensor", yb[:, c, :], ym[:].rearrange("p j e -> p (j e)"), yps[:], ALU.add,
                         reads=[ym.b, yps.b], writes=[yb.b])
                else:
                    P.op("dve", "tensor_tensor", y_[:].rearrange("p j e -> p (j e)"), ym[:].rearrange("p j e -> p (j e)"), yps[:],
                         ALU.add, reads=[ym.b, yps.b], writes=[y_.b])
                P.op("pe", "matmul", stp[:], bs_[:], xq_[:].rearrange("p j e -> p (j e)"), start=True, stop=True,
                     reads=[bs_.b, xq_.b], writes=[stp.b])
                P.op("dve", "tensor_tensor", Sst[:], Sst[:], etot[:, c, dr, :].unsqueeze(2).to_broadcast([128, 8, 64]), ALU.mult,
                     reads=[Sst.b, etot.b], writes=[Sst.b])
                P.op("dve", "tensor_tensor", Sst[:].rearrange("p j e -> p (j e)"), Sst[:].rearrange("p j e -> p (j e)"), stp[:],
                     ALU.add, reads=[Sst.b, stp.b], writes=[Sst.b])
                P.op("act", "copy", sb_nxt[:], Sst[:].rearrange("p j e -> p (j e)"), reads=[Sst.b], writes=[sb_nxt.b])
                if dr == 0:
                    z_ = zt[i2]
                    P.dma("sp", z_[:], utm[cs, g * 512:(g + 1) * 512], writes=[z_.b])
                    P.op("act", "activation", z_[:], z_[:], AF.Silu, reads=[z_.b], writes=[z_.b])
                    yf = y_[:].rearrange("p j e -> p (j e)")
                    P.op("dve", "tensor_tensor", yf, yf, yb[:, c, :], ALU.add, reads=[y_.b, yb.b], writes=[y_.b])
                    P.op("pool", "tensor_tensor", ym[:], xs_[:], dsk[:, gl, :].unsqueeze(2).to_broadcast([128, 8, 64]), ALU.mult,
                         reads=[xs_.b, dsk.b], writes=[ym.b])
                    P.op("dve", "tensor_tensor", y_[:], y_[:], ym[:], ALU.add, reads=[y_.b, ym.b], writes=[y_.b])
                    P.op("dve", "tensor_tensor", yf, yf, z_[:], ALU.mult, reads=[y_.b, z_.b], writes=[y_.b])
                    P.op("act", "activation", junk[:], yf, AF.Square, accum_out=ssq[:], reads=[y_.b], writes=[junk.b, ssq.b])
                    P.op("act", "activation", ssq[:], ssq[:], AF.Sqrt, bias=EPS, scale=1.0 / 512, reads=[ssq.b], writes=[ssq.b])
                    P.op("dve", "reciprocal", ssq[:], ssq[:], reads=[ssq.b], writes=[ssq.b])
                    P.op("dve", "scalar_tensor_tensor", z_[:], yf, ssq[:, 0:1], ngr[:, gl, :], ALU.mult, ALU.mult,
                         reads=[y_.b, ssq.b, ngr.b], writes=[z_.b])
                    P.dma("sp", o_d[cs, g * 512:(g + 1) * 512], z_[:], reads=[z_.b], writes=[o_b])


def ssd_consts():
    s = np.arange(128)[:, None]
    l = np.arange(128)[None, :]
    tm = np.stack([(s <= l), (s >= l)], 1).astype(np.float32)
    nm = np.stack([np.where(l < s, SSD_NEG, 0.0), np.where(l > s, SSD_NEG, 0.0)], 1).astype(np.float32)
    nm = np.ascontiguousarray(np.repeat(nm[:, :, None, :], 4, axis=2))
    return tm, nm


def ssd_small(hh, conv_w, conv_b, dt_bias, a_log, d_skip, norm_g):
    cwt = np.empty((128, 2, 6, 7), np.float32)
    cbt = np.empty((128, 2, 6), np.float32)
    dtb = np.empty((128, 2, 2, 8), np.float32)
    alog = np.empty((128, 2, 2, 8), np.float32)
    dsk = np.empty((128, 2, 8), np.float32)
    ng = np.empty((128, 2, 512), np.float32)
    for gl in range(2):
        g = 2 * hh + gl
        chans = [np.arange(g * 512 + k * 128, g * 512 + (k + 1) * 128) for k in range(4)]
        chans.append(np.arange(2048 + g * 128, 2048 + (g + 1) * 128))
        chans.append(np.arange(2048 + 512 + g * 128, 2048 + 512 + (g + 1) * 128))
        for ci, ch in enumerate(chans):
            cwt[:, gl, ci, :] = conv_w[:, ch].T
            cbt[:, gl, ci] = conv_b[ch]
        hs = slice(g * 8, (g + 1) * 8)
        dtb[:, gl] = dt_bias[:, hs][None]
        alog[:, gl] = a_log[:, hs][None]
        dsk[:, gl] = d_skip[hs][None]
        ng[:, gl] = norm_g[g * 512:(g + 1) * 512][None]
    return cwt, cbt, dtb, alog, dsk, ng


HG_LAYER = 1


def emit_hg(P, A, hh, io):
    A.reset()
    T = A.T
    NCH = S // 32
    ufm, utm, od_d, o_d = io["ufm"], io["utm"], io["od"], io["o"]
    od_b = Buf("od_d")
    o_b = Buf("o_d")

    def const(name, d_ap, shape, dtype=F32):
        t = T(name, shape, dtype)
        P.dma("sp", t[:], d_ap, writes=[t.b])
        return t
    ident = const("ident", io["ident"], [128, 128], BF16)
    smask = const("smask", io["smask"], [128, 1024])
    masks = const("masks", io["masks"], [32, 2, 32])
    ng = const("ng", io["hg_ng_rep"][hh], [128, 512])
    lbt = const("lbt", io["lbT"][hh], [128, 4, 4])
    lsum = T("lsum", [128, 4])
    lb = T("lb", [128, 4])
    oml = T("oml", [128, 4])
    P.op("act", "activation", lbt[:], lbt[:], AF.Exp, reads=[lbt.b], writes=[lbt.b])
    P.op("dve", "tensor_reduce", lsum[:], lbt[:].rearrange("p l h -> p h l"), AX.X, ALU.add, reads=[lbt.b], writes=[lsum.b])
    P.op("dve", "reciprocal", lsum[:], lsum[:], reads=[lsum.b], writes=[lsum.b])
    P.op("dve", "tensor_tensor", lb[:], lbt[:, HG_LAYER, :], lsum[:], ALU.mult, reads=[lbt.b, lsum.b], writes=[lb.b])
    P.op("dve", "tensor_scalar", oml[:], lb[:], -1.0, 1.0, ALU.mult, ALU.add, reads=[lb.b], writes=[oml.b])

    PW = 1024
    NP_ = S // PW
    qs = T("qs", [128, S])
    vstg = T("vstg", [32, 16, 128])
    v32 = T("v32b", [32, NCH, 128], BF16)
    qtl = T("qtl", [128, S], BF16)
    ktl = T("ktl", [128, S], BF16)
    qht = T("qht", [128, S], BF16)
    kh32 = T("kh32", [32, NCH, 128], BF16)
    ec = T("ec", [128, NCH])
    fr = [T("fr%d" % i, [128, PW]) for i in range(2)]
    tf2 = [T("tf0", [128, PW])] * 2
    tg2 = [T("tg0", [128, PW])] * 2
    tk2 = [T("tk%d" % i, [128, PW]) for i in range(2)]
    tX2 = [T("tX%d" % i, [128, PW]) for i in range(2)]
    tD2 = [T("tD%d" % i, [128, PW]) for i in range(2)]
    tE4 = [T("tE%d" % i, [128, PW]) for i in range(4)]
    khT2 = [T("khT%d" % i, [128, PW], BF16) for i in range(2)]
    tpk = [T("tpk%d" % i, [32, 4, 128], BF16, psum=True) for i in range(2)]
    aps = [T("aps%d" % i, [32, 32], F32, psum=True) for i in range(2)]
    ops_ = [T("ops%d" % i, [32, 128], F32, psum=True) for i in range(2)]
    ups = [T("ups%d" % i, [128, 128], F32, psum=True) for i in range(2)]
    asb = [T("asb%d" % i, [32, 32], BF16) for i in range(2)]
    st2 = [T("st%d" % i, [128, 128]) for i in range(2)]
    stb = [T("stb%d" % i, [128, 128], BF16) for i in range(2)]
    ost = [T("ost%d" % i, [32, 8, 128]) for i in range(2)]

    def v3(t):
        return t[:, 0:PW].rearrange("p (c j) -> p c j", j=32)

    fi = 0
    oi = 0
    for h in range(4):
        hg = 4 * hh + h
        for p in range(NP_):
            sl = slice(p * PW, (p + 1) * PW)
            P.dma("sp", qs[:, sl], ufm[hg * 128:(hg + 1) * 128, sl], writes=[qs.b])
        P.op("act", "activation", qs[:], qs[:], AF.Silu, reads=[qs.b], writes=[qs.b])
        for p in range(8):
            src = utm[p * 512:(p + 1) * 512, hg * 128:(hg + 1) * 128].rearrange("(c j) e -> j c e", j=32)
            P.dma("sp", vstg[:], src, writes=[vstg.b])
            P.op("pool", "tensor_copy", v32[:, p * 16:(p + 1) * 16, :], vstg[:], reads=[vstg.b], writes=[v32.b])
        for dr in range(2):
            frow = (1 + dr) * 1024 + hg * 128
            for p in range(NP_):
                sl = slice(p * PW, (p + 1) * PW)
                frt = fr[fi % 2]
                tf, tg, tk, tX, tD, khT = tf2[fi % 2], tg2[fi % 2], tk2[fi % 2], tX2[fi % 2], tD2[fi % 2], khT2[fi % 2]
                tE = tE4[(fi % 2) * 2:(fi % 2) * 2 + 2]
                fi += 1
                P.dma("sp", frt[:], ufm[frow:frow + 128, sl], writes=[frt.b])
                P.op("act", "activation", frt[:], frt[:], AF.Sigmoid, reads=[frt.b], writes=[frt.b])
                P.op("dve", "tensor_scalar", tf[:], frt[:], oml[:, h:h + 1], lb[:, h:h + 1], ALU.mult, ALU.add,
                     reads=[frt.b, oml.b, lb.b], writes=[tf.b])
                P.op("act", "activation", tg[:], tf[:], AF.Ln, reads=[tf.b], writes=[tg.b])
                P.op("pool", "tensor_scalar", tk[:], tf[:], -1.0, 1.0, ALU.mult, ALU.add, reads=[tf.b], writes=[tk.b])
                P.op("dve", "tensor_tensor_scan", tX[:], smask[:], tg[:], 0.0, ALU.mult, ALU.add,
                     reads=[smask.b, tg.b], writes=[tX.b])
                X3 = v3(tX)
                if dr == 1:
                    P.op("dve", "tensor_tensor", v3(tD), X3[:, :, 31:32].to_broadcast([128, 32, 32]), X3, ALU.subtract,
                         reads=[tX.b], writes=[tD.b])
                    P.op("dve", "tensor_tensor", tX[:], tD[:], tg[:], ALU.add, reads=[tD.b, tg.b], writes=[tX.b])
                edge = 31 if dr == 0 else 0
                P.op("act", "activation", ec[:, p * 32:(p + 1) * 32], X3[:, :, edge], AF.Exp, reads=[tX.b], writes=[ec.b])
                P.op("dve", "tensor_tensor", v3(tD), X3, X3[:, :, 16:17].to_broadcast([128, 32, 32]), ALU.subtract,
                     reads=[tX.b], writes=[tD.b])
                P.op("act", "activation", tE[0][:], tD[:], AF.Exp, reads=[tD.b], writes=[tE[0].b])
                P.op("act", "activation", tE[1][:], tD[:], AF.Exp, scale=-1.0, reads=[tD.b], writes=[tE[1].b])
                P.op("pool", "tensor_tensor", qtl[:, sl], qs[:, sl], tE[0][:], ALU.mult, reads=[qs.b, tE[0].b], writes=[qtl.b])
                P.op("dve", "tensor_tensor", ktl[:, sl], tk[:], tE[1][:], ALU.mult, reads=[tk.b, tE[1].b], writes=[ktl.b])
                P.op("act", "activation", tE[0][:], tX[:], AF.Exp, reads=[tX.b], writes=[tE[0].b])
                P.op("pool", "tensor_tensor", qht[:, sl], qs[:, sl], tE[0][:], ALU.mult, reads=[qs.b, tE[0].b], writes=[qht.b])
                P.op("dve", "tensor_tensor", v3(tD), X3[:, :, edge:edge + 1].to_broadcast([128, 32, 32]), X3, ALU.subtract,
                     reads=[tX.b], writes=[tD.b])
                P.op("act", "activation", tE[1][:], tD[:], AF.Exp, reads=[tD.b], writes=[tE[1].b])
                P.op("dve", "tensor_tensor", khT[:], tk[:], tE[1][:], ALU.mult, reads=[tk.b, tE[1].b], writes=[khT.b])
                for g8 in range(PW // 128):
                    tp = tpk[g8 % 2]
                    for q4 in range(4):
                        c = g8 * 4 + q4
                        P.op("pe", "transpose", tp[:, q4, :], khT[:, c * 32:(c + 1) * 32], ident[:],
                             reads=[khT.b, ident.b], writes=[tp.b])
                    c0 = p * 32 + g8 * 4
                    P.op("act", "copy", kh32[:, c0:c0 + 4, :], tp[:], reads=[tp.b], writes=[kh32.b])
            P.op("dve", "memset", st2[1][:], 0.0, writes=[st2[1].b])
            P.op("pool", "memset", stb[0][:], 0.0, writes=[stb[0].b])
            order = list(range(NCH)) if dr == 0 else list(range(NCH - 1, -1, -1))
            def front(n, c):
                cs = slice(c * 32, (c + 1) * 32)
                ap_, up_, as_ = aps[n % 2], ups[n % 2], asb[n % 2]
                P.op("pe", "matmul", ap_[:], ktl[:, cs], qtl[:, cs], start=True, stop=True, reads=[ktl.b, qtl.b], writes=[ap_.b])
                P.op("pe", "matmul", up_[:], kh32[:, c, :], v32[:, c, :], start=True, stop=True, reads=[kh32.b, v32.b], writes=[up_.b])
                P.op("dve", "tensor_tensor", as_[:], ap_[:], masks[:, dr, :], ALU.mult, reads=[ap_.b, masks.b], writes=[as_.b])
            front(0, order[0])
            for n, c in enumerate(order):
                cs = slice(c * 32, (c + 1) * 32)
                ap_, op_, up_, as_ = aps[n % 2], ops_[n % 2], ups[n % 2], asb[n % 2]
                sb_cur, sb_nxt = stb[n % 2], stb[(n + 1) % 2]
                if n + 1 < NCH:
                    front(n + 1, order[n + 1])
                P.op("pe", "matmul", op_[:], as_[:], v32[:, c, :], start=True, stop=False, reads=[as_.b, v32.b], writes=[op_.b])
                P.op("pe", "matmul", op_[:], qht[:, cs], sb_cur[:], start=False, stop=True, reads=[qht.b, sb_cur.b], writes=[op_.b])
                os_ = ost[oi % 2]
                slot = c % 8
                s_old, s_new = st2[(n + 1) % 2], st2[n % 2]
                P.op("dve", "scalar_tensor_tensor", s_new[:], s_old[:], ec[:, c:c + 1], up_[:], ALU.mult, ALU.add,
                     reads=[s_old.b, ec.b, up_.b], writes=[s_new.b])
                P.op("act", "copy", sb_nxt[:], s_new[:], reads=[s_new.b], writes=[sb_nxt.b])
                P.op("act", "copy", os_[:, slot, :], op_[:], reads=[op_.b], writes=[os_.b])
                if n % 8 == 7:
                    cb = (c // 8) * 8
                    dst = od_d[dr][cb * 32:(cb + 8) * 32, h * 128:(h + 1) * 128].rearrange("(c i) e -> i c e", i=32)
                    P.dma("sp", dst, os_[:], reads=[os_.b], writes=[od_b])
                    oi += 1
    fa = [T("fa%d" % i, [128, 512]) for i in range(2)]
    fb = [T("fb%d" % i, [128, 512]) for i in range(2)]
    fo = [T("fo%d" % i, [128, 512]) for i in range(2)]
    fss = [T("fss%d" % i, [128, 4]) for i in range(2)]
    fj = T("fj", [128, 128])
    for tt in range(NST):
        i = tt % 2
        sl = slice(tt * 128, (tt + 1) * 128)
        P.dma("sp", fa[i][:], od_d[0][sl], reads=[od_b], writes=[fa[i].b])
        P.dma("sp", fb[i][:], od_d[1][sl], reads=[od_b], writes=[fb[i].b])
        P.op("dve", "tensor_tensor", fa[i][:], fa[i][:], fb[i][:], ALU.add, reads=[fa[i].b, fb[i].b], writes=[fa[i].b])
        for h in range(4):
            P.op("act", "activation", fj[:], fa[i][:, h * 128:(h + 1) * 128], AF.Square, accum_out=fss[i][:, h:h + 1],
                 reads=[fa[i].b], writes=[fj.b, fss[i].b])
        P.op("act", "activation", fss[i][:], fss[i][:], AF.Sqrt, bias=EPS, scale=1.0 / 128, reads=[fss[i].b], writes=[fss[i].b])
        P.op("dve", "reciprocal", fss[i][:], fss[i][:], reads=[fss[i].b], writes=[fss[i].b])
        for h in range(4):
            hs = slice(h * 128, (h + 1) * 128)
            P.op("dve", "scalar_tensor_tensor", fo[i][:, hs], fa[i][:, hs], fss[i][:, h:h + 1], ng[:, hs], ALU.mult, ALU.mult,
                 reads=[fa[i].b, fss[i].b, ng.b], writes=[fo[i].b])
        P.dma("sp", o_d[sl, hh * 512:(hh + 1) * 512], fo[i][:], reads=[fo[i].b], writes=[o_b])


def hg_consts():
    sm = np.ones((128, 1024), np.float32)
    sm[:, ::32] = 0.0
    j = np.arange(32)[:, None]
    i = np.arange(32)[None, :]
    masks = np.stack([(j <= i), (j >= i)], 1).astype(np.float32)
    return sm, masks


def emit_at(P, A, hh, io):
    A.reset()
    T = A.T
    utm, oT_d = io["utm"], io["oT"]
    oT_b = Buf("oT_d")
    ident = T("ident", [128, 128], BF16)
    gain = T("gain", [128, 12, 128])
    ones = T("ones", [128, 128], BF16)
    P.dma("sp", ident[:], io["ident"], writes=[ident.b])
    P.dma("sp", gain[:], io["gain"], writes=[gain.b])
    P.op("pool", "memset", ones[:], 1.0, writes=[ones.b])
    qT = T("qT", [128, 8, S], BF16)
    kT = T("kT", [128, 4, S], BF16)
    vb = T("vb", [128, NST, 4, 128], BF16)
    qT_b = [Buf("qT%d" % i) for i in range(NST)]
    kT_b = [Buf("kT%d" % i) for i in range(NST)]
    vb_b = [Buf("vb%d" % i) for i in range(NST)]
    qk = [T("qk%d" % i, [128, 12, 128]) for i in range(2)]
    vt = [T("vt%d" % i, [128, 4, 128]) for i in range(2)]
    cs = [T("cs%d" % i, [128, 2, 2, 32]) for i in range(2)]
    junk = T("junk", [128, 128])
    ssq = [T("ssq%d" % i, [128, 12]) for i in range(2)]
    qn = [T("qn0", [128, 12, 128])] * 2
    ta = [T("ta0", [128, 12, 2, 32])] * 2
    tb = [T("tb0", [128, 12, 2, 32])] * 2
    tc = [T("tc0", [128, 12, 2, 32])] * 2
    td = [T("td0", [128, 12, 2, 32])] * 2
    qr = [T("qr%d" % i, [128, 12, 128], BF16) for i in range(2)]
    tpq = T("tpq", [128, 8, 128], BF16, psum=True)
    tpk = T("tpk", [128, 4, 128], BF16, psum=True)
    for tt in range(NST):
        i = tt % 2
        sl = slice(tt * 128, (tt + 1) * 128)
        qkt, vtt, cst, sst, qnt, qrt = qk[i], vt[i], cs[i], ssq[i], qn[i], qr[i]
        P.dma("sp", qkt[:, 0:8, :], utm[sl, hh * 1024:(hh + 1) * 1024].rearrange("t (h e) -> t h e", h=8), writes=[qkt.b])
        P.dma("sp", qkt[:, 8:12, :], utm[sl, 2048 + hh * 512:2048 + (hh + 1) * 512].rearrange("t (h e) -> t h e", h=4), writes=[qkt.b])
        P.dma("sp", vtt[:], utm[sl, 3072 + hh * 512:3072 + (hh + 1) * 512].rearrange("t (h e) -> t h e", h=4), writes=[vtt.b])
        P.dma("sp", cst[:, 0], io["cos"][sl], writes=[cst.b])
        P.dma("sp", cst[:, 1], io["sin"][sl], writes=[cst.b])
        P.op("pool", "tensor_copy", vb[:, tt], vtt[:], reads=[vtt.b], writes=[vb_b[tt]])
        for h in range(12):
            P.op("act", "activation", junk[:], qkt[:, h, :], AF.Square, accum_out=sst[:, h:h + 1], reads=[qkt.b], writes=[junk.b, sst.b])
        P.op("act", "activation", sst[:], sst[:], AF.Sqrt, bias=EPS, scale=1.0 / 128, reads=[sst.b], writes=[sst.b])
        P.op("dve", "reciprocal", sst[:], sst[:], reads=[sst.b], writes=[sst.b])
        for h in range(12):
            P.op("dve", "scalar_tensor_tensor", qnt[:, h, :], qkt[:, h, :], sst[:, h:h + 1], gain[:, h, :], ALU.mult, ALU.mult,
                 reads=[qkt.b, sst.b, gain.b], writes=[qnt.b])
        xv = qnt[:].rearrange("p h (a two f) -> p h a two f", a=2, two=2)
        ov = qrt[:].rearrange("p h (a two f) -> p h a two f", a=2, two=2)
        x1, x2 = xv[:, :, :, 0, :], xv[:, :, :, 1, :]
        cb = cst[:, 0].unsqueeze(1).to_broadcast([128, 12, 2, 32])
        sb = cst[:, 1].unsqueeze(1).to_broadcast([128, 12, 2, 32])
        a_, b_, c_, d_ = ta[i], tb[i], tc[i], td[i]
        P.op("dve", "tensor_tensor", a_[:], x1, cb, ALU.mult, reads=[qnt.b, cst.b], writes=[a_.b])
        P.op("pool", "tensor_tensor", b_[:], x2, sb, ALU.mult, reads=[qnt.b, cst.b], writes=[b_.b])
        P.op("dve", "tensor_tensor", c_[:], x2, cb, ALU.mult, reads=[qnt.b, cst.b], writes=[c_.b])
        P.op("pool", "tensor_tensor", d_[:], x1, sb, ALU.mult, reads=[qnt.b, cst.b], writes=[d_.b])
        P.op("dve", "tensor_tensor", ov[:, :, :, 0, :], a_[:], b_[:], ALU.subtract, reads=[a_.b, b_.b], writes=[qrt.b])
        P.op("pool", "tensor_tensor", ov[:, :, :, 1, :], c_[:], d_[:], ALU.add, reads=[c_.b, d_.b], writes=[qrt.b])
        for h in range(8):
            P.op("pe", "transpose", tpq[:, h, :], qrt[:, h, :], ident[:], reads=[qrt.b, ident.b], writes=[tpq.b])
        for h in range(4):
            P.op("pe", "transpose", tpk[:, h, :], qrt[:, 8 + h, :], ident[:], reads=[qrt.b, ident.b], writes=[tpk.b])
        P.op("act", "copy", qT[:, :, sl], tpq[:], reads=[tpq.b], writes=[qT_b[tt]])
        P.op("act", "copy", kT[:, :, sl], tpk[:], reads=[tpk.b], writes=[kT_b[tt]])
    sps = [T("sps%d" % i, [128, 512], F32, psum=True) for i in range(2)]
    ops_ = [T("ops%d" % i, [128, 512], F32, psum=True) for i in range(2)]
    dps = [T("dps%d" % i, [128, 512], F32, psum=True) for i in range(2)]
    NPT = 6
    pT = [T("pT%d" % i, [128, 512], BF16) for i in range(NPT)]
    psum4 = [T("psum4_%d" % i, [128, 512], BF16) for i in range(2)]
    rec = [T("rec%d" % i, [128, 512]) for i in range(2)]
    osb = [T("osb%d" % i, [128, 512]) for i in range(2)]
    scale = 128.0 ** -0.5
    it = 0
    for h in range(8):
        kv = h // 2
        for qb in range(S // 512):
            qsl = slice(qb * 512, (qb + 1) * 512)
            q_reads = [qT_b[qb * 4 + j] for j in range(4)]
            op_, dp_ = ops_[it % 2], dps[it % 2]

            def mm1(kt):
                sp_ = sps[kt % 2]
                P.op("pe", "matmul", sp_[:], kT[:, kv, kt * 128:(kt + 1) * 128], qT[:, h, qsl], start=True, stop=True,
                     reads=[kT_b[kt]] + q_reads, writes=[sp_.b])

            def rest(kt):
                sp_ = sps[kt % 2]
                p_ = pT[kt % NPT]
                P.op("act", "activation", p_[:], sp_[:], AF.Exp, scale=scale, reads=[sp_.b], writes=[p_.b])
                P.op("pe", "matmul", op_[:], vb[:, kt, kv, :], p_[:], start=(kt == 0), stop=(kt == NST - 1),
                     reads=[vb_b[kt], p_.b], writes=[op_.b])
                if kt % 4 == 1:
                    ps_ = psum4[(kt // 4) % 2]
                    P.op("dve", "tensor_tensor", ps_[:], pT[(kt - 1) % NPT][:], p_[:], ALU.add,
                         reads=[pT[(kt - 1) % NPT].b, p_.b], writes=[ps_.b])
                elif kt % 4 in (2, 3):
                    ps_ = psum4[(kt // 4) % 2]
                    P.op("dve", "tensor_tensor", ps_[:], ps_[:], p_[:], ALU.add, reads=[ps_.b, p_.b], writes=[ps_.b])
                G = (kt - 5) // 4
                if kt >= 5 and (kt - 5) % 4 == 0:
                    pg = psum4[G % 2]
                    P.op("pe", "matmul", dp_[:], ones[:], pg[:], start=(G == 0), stop=False,
                         reads=[ones.b, pg.b], writes=[dp_.b])
                if kt == NST - 1:
                    pg = psum4[(NST // 4 - 1) % 2]
                    P.op("pe", "matmul", dp_[:], ones[:], pg[:], start=False, stop=True,
                         reads=[ones.b, pg.b], writes=[dp_.b])
            mm1(0)
            for kt in range(NST):
                if kt + 1 < NST:
                    mm1(kt + 1)
                rest(kt)
            r_, o_ = rec[it % 2], osb[it % 2]
            P.op("dve", "reciprocal", r_[:], dp_[:], reads=[dp_.b], writes=[r_.b])
            P.op("dve", "tensor_tensor", o_[:], op_[:], r_[:], ALU.mult, reads=[op_.b, r_.b], writes=[o_.b])
            hgl = hh * 8 + h
            P.dma("sp", oT_d[hgl * 128:(hgl + 1) * 128, qsl], o_[:], reads=[o_.b], writes=[oT_b])
            it += 1


def rope_tables():
    row = np.repeat(np.arange(S // 64), 64).astype(np.float32)
    col = (np.arange(S) % 64).astype(np.float32)
    inv = (np.float32(10000.0) ** (-np.arange(0, 64, 2, dtype=np.float32) / np.float32(64))).astype(np.float32)
    ang = np.stack([row[:, None] * inv, col[:, None] * inv], 1).astype(np.float32)
    return np.cos(ang).astype(np.float32), np.sin(ang).astype(np.float32)


DL_PAIRS = ((128, 1), (512, 4), (2048, 16))
NEG = -30000.0


def dl_geom(d):
    Ls = S // d
    nt = Ls // 128
    return Ls, nt, d * (Ls + 128), d * (nt + 1)


def emit_dl(P, A, hh, io):
    A.reset()
    T = A.T
    ufm, utm, nd_d, o_d = io["ufm"], io["utm"], io["nd"], io["o"]
    nd_b = Buf("nd_d")
    o_b = Buf("o_d")
    bias = T("bias", [128, 24, 256])
    P.dma("sp", bias[:], io["dl_bias"][hh], writes=[bias.b])
    KMAX = 6144
    VMAX = 48
    qst = T("qst", [64, S])
    kst = T("kst", [64, S])
    vst = T("vst", [128, VMAX, 65])
    qb_ = [T("qb%d" % i, [64, S], BF16) for i in range(2)]
    kb_ = [T("kb%d" % i, [64, KMAX], BF16) for i in range(2)]
    vb_ = [T("vb%d" % i, [128, VMAX, 65], BF16) for i in range(2)]
    nds = [T("nds%d" % i, [128, 32, 65]) for i in range(2)]
    NPB = 4
    sps = [T("sps%d" % i, [128, 2, 128], F32, psum=True) for i in range(NPB)]
    ops_ = [T("ops%d" % i, [128, 65], F32, psum=True) for i in range(NPB)]
    tmp = [T("tmp%d" % i, [128, 256]) for i in range(NPB)]
    pb = [T("pb%d" % i, [128, 2, 128], BF16) for i in range(NPB)]
    items = [(g, d, h) for g, (_, d) in enumerate(DL_PAIRS) for h in range(8)]

    def prep(it):
        g, d, h = items[it]
        Ls, nt, klen, vt_n = dl_geom(d)
        i = it % 2
        hgl = 8 * hh + h
        qrow = (g * 2) * 1024 + hgl * 64
        krow = (g * 2 + 1) * 1024 + hgl * 64
        P.dma("sp", qst[:], ufm[qrow:qrow + 64, :], writes=[qst.b])
        P.dma("sp", kst[:], ufm[krow:krow + 64, :], writes=[kst.b])
        P.op("act", "copy", qb_[i][:].rearrange("p (r m) -> p r m", r=d), qst[:].rearrange("p (m r) -> p r m", r=d),
             reads=[qst.b], writes=[qb_[i].b])
        kb3 = kb_[i][:, 0:klen].rearrange("p (r m) -> p r m", r=d)
        P.op("dve", "memset", kb3[:, :, 0:64], 0.0, writes=[kb_[i].b])
        P.op("dve", "memset", kb3[:, :, 64 + Ls:128 + Ls], 0.0, writes=[kb_[i].b])
        P.op("act", "copy", kb_[i][:, 0:klen].rearrange("p (r m) -> p r m", r=d)[:, :, 64:64 + Ls],
             kst[:].rearrange("p (m r) -> p r m", r=d), reads=[kst.b], writes=[kb_[i].b])
        v4 = vst[:, 0:vt_n, :].rearrange("p (r i) c -> p r i c", r=d)
        vsrc = utm[:, g * 1024 + hgl * 64:g * 1024 + (hgl + 1) * 64].rearrange("(m r) e -> r m e", r=d)
        P.op("dve", "memset", v4[0:64, :, 0, :], 0.0, writes=[vst.b])
        P.op("dve", "memset", v4[64:128, :, nt, :], 0.0, writes=[vst.b])
        if nt > 1:
            P.op("dve", "memset", v4[:, :, 1:nt, 64:65], 1.0, writes=[vst.b])
        P.op("dve", "memset", v4[64:128, :, 0, 64:65], 1.0, writes=[vst.b])
        P.op("dve", "memset", v4[0:64, :, nt, 64:65], 1.0, writes=[vst.b])
        if nt - 1 == 1:
            P.dma("sp", v4[:, :, 1, 0:64], vsrc[:, 64:192, :].rearrange("r k e -> k r e"), writes=[vst.b])
        elif nt > 1:
            for r in range(d):
                P.dma("sp", v4[:, r, 1:nt, 0:64], vsrc[r][64:64 + 128 * (nt - 1), :].rearrange("(i k) e -> k i e", k=128),
                      writes=[vst.b])
        P.dma("sp", v4[64:128, :, 0, 0:64], vsrc[:, 0:64, :].rearrange("r k e -> k r e"), writes=[vst.b])
        P.dma("sp", v4[0:64, :, nt, 0:64], vsrc[:, Ls - 64:Ls, :].rearrange("r k e -> k r e"), writes=[vst.b])
        P.op("dve", "tensor_copy", vb_[i][:, 0:vt_n, :], vst[:, 0:vt_n, :], reads=[vst.b], writes=[vb_[i].b])

    ucnt = [0]

    def compute(it):
        g, d, h = items[it]
        Ls, nt, klen, vt_n = dl_geom(d)
        i = it % 2
        nd_ = nds[i]
        u0 = ucnt[0]
        ucnt[0] += 32

        def geo(j):
            r, jj = divmod(j, nt)
            return r * Ls + jj * 128, r * (Ls + 128) + jj * 128, r * (nt + 1) + jj

        def qk(j):
            n0, kbase, _ = geo(j)
            sp_ = sps[(u0 + j) % NPB]
            for c in range(2):
                P.op("pe", "matmul", sp_[:, c, :], kb_[i][:, kbase + c * 128:kbase + (c + 1) * 128],
                     qb_[i][:, n0:n0 + 128], start=True, stop=True, reads=[kb_[i].b, qb_[i].b], writes=[sp_.b])
            tm_, p_ = tmp[(u0 + j) % NPB], pb[(u0 + j) % NPB]
            P.op("dve", "scalar_tensor_tensor", tm_[:], sp_[:].rearrange("p c q -> p (c q)"), 0.125,
                 bias[:, g * 8 + h, :], ALU.mult, ALU.add, reads=[sp_.b, bias.b], writes=[tm_.b])
            P.op("act", "activation", p_[:].rearrange("p c q -> p (c q)"), tm_[:], AF.Exp, reads=[tm_.b], writes=[p_.b])

        def pv(j):
            _, _, vbase = geo(j)
            op_, p_ = ops_[(u0 + j) % NPB], pb[(u0 + j) % NPB]
            for c in range(2):
                P.op("pe", "matmul", op_[:], p_[:, c, :], vb_[i][:, vbase + c, :], start=(c == 0), stop=(c == 1),
                     reads=[p_.b, vb_[i].b], writes=[op_.b])
            P.op("dve", "tensor_copy", nd_[:, j, :], op_[:], reads=[op_.b], writes=[nd_.b])
        AHEAD = 2
        for j in range(min(AHEAD, 32)):
            qk(j)
        for j in range(32):
            if j + AHEAD < 32:
                qk(j + AHEAD)
            pv(j)

    def store(it):
        g, d, h = items[it]
        Ls, nt, klen, vt_n = dl_geom(d)
        nd_ = nds[it % 2]
        ndv = nd_d.rearrange("(m r) g h c -> r m g h c", r=d)
        for r in range(d):
            dst = ndv[r][:, g, h, :].rearrange("(jj q) c -> q jj c", q=128)
            for j0 in range(0, nt, 8):
                j1 = min(nt, j0 + 8)
                P.dma("pool", dst[:, j0:j1, :], nd_[:, r * nt + j0:r * nt + j1, :], reads=[nd_.b], writes=[nd_b])

    prep(0)
    for it in range(len(items)):
        if it + 1 < len(items):
            prep(it + 1)
        compute(it)
        store(it)
    mt = [T("mt%d" % i, [128, 3, 8, 65]) for i in range(2)]
    ms = [T("ms%d" % i, [128, 8, 65]) for i in range(2)]
    mr = [T("mr%d" % i, [128, 8]) for i in range(2)]
    mo = [T("mo%d" % i, [128, 8, 64]) for i in range(2)]
    for tt in range(NST):
        i = tt % 2
        sl = slice(tt * 128, (tt + 1) * 128)
        P.dma("sp", mt[i][:], nd_d[sl], reads=[nd_b], writes=[mt[i].b])
        P.op("dve", "tensor_tensor", ms[i][:], mt[i][:, 0], mt[i][:, 1], ALU.add, reads=[mt[i].b], writes=[ms[i].b])
        P.op("dve", "tensor_tensor", ms[i][:], ms[i][:], mt[i][:, 2], ALU.add, reads=[mt[i].b, ms[i].b], writes=[ms[i].b])
        P.op("dve", "reciprocal", mr[i][:], ms[i][:, :, 64], reads=[ms[i].b], writes=[mr[i].b])
        P.op("dve", "tensor_tensor", mo[i][:], ms[i][:, :, 0:64], mr[i][:].unsqueeze(2).to_broadcast([128, 8, 64]),
             ALU.mult, reads=[ms[i].b, mr[i].b], writes=[mo[i].b])
        P.dma("sp", o_d[sl, hh * 512:(hh + 1) * 512].rearrange("t (h e) -> t h e", h=8), mo[i][:], reads=[mo[i].b], writes=[o_b])


def t5_bucket_np(rel):
    half, exact = 16, 8
    n = np.abs(rel)
    large = exact + (np.log(np.maximum(n, 1).astype(np.float32) / np.float32(exact))
                     / np.float32(np.log(1024 / exact)) * np.float32(half - exact)).astype(np.int32)
    large = np.minimum(large, half - 1)
    return np.where(rel > 0, half, 0) + np.where(n < exact, n, large)


def dl_bias_tables(rel_bias, heads):
    kk = np.arange(128)[:, None]
    qq = np.arange(128)[None, :]
    out = np.empty((128, 24, 256), np.float32)
    for g, (_, d) in enumerate(DL_PAIRS):
        bA = t5_bucket_np((kk - 64 - qq) * d)
        bB = t5_bucket_np((64 + kk - qq) * d)
        for hi, h in enumerate(heads):
            out[:, g * 8 + hi, 0:128] = np.where(kk >= qq, rel_bias[bA, h], NEG)
            out[:, g * 8 + hi, 128:256] = np.where(kk <= qq, rel_bias[bB, h], NEG)
    return out


def build_fused():
    P = Prog()
    nc = P.nc
    A = Arena(P, kib=207)
    dr = P.dram

    def scratch(name, shape):
        return nc.dram_tensor(name, list(shape), F32, kind="Internal").ap()

    def db(ap):
        return {"ap": ap, "b": Buf("d")}
    x_in = dr("x", [S, D])
    out_d = dr("out", [S, D], kind="ExternalOutput")
    ident = dr("ident", [128, 128], BF16)
    io = {
        "ident": ident, "identf": dr("identf", [128, 128]),
        "convw": dr("convw", [2, 128, 2, 6, 7]), "convb": dr("convb", [2, 128, 2, 6]),
        "dtb_rep": dr("dtb_rep", [2, 128, 2, 2, 8]), "alog_rep": dr("alog_rep", [2, 128, 2, 2, 8]),
        "d_rep": dr("d_rep", [2, 128, 2, 8]), "ssd_ng_rep": dr("ssd_ng_rep", [2, 128, 2, 512]),
        "tmat": dr("tmat", [128, 2, 128]), "negm": dr("negm", [128, 2, 4, 128]),
        "lbT": dr("lbT", [2, 128, 4, 4]), "hg_ng_rep": dr("hg_ng_rep", [2, 128, 512]),
        "smask": dr("smask", [128, 1024]), "masks": dr("masks", [32, 2, 32]),
        "cos": dr("cos", [S, 2, 32]), "sin": dr("sin", [S, 2, 32]), "gain": dr("gain", [128, 12, 128]),
        "dl_bias": dr("dl_bias", [2, 128, 24, 256]),
    }
    NIN = (5184, 5120, 6144, 10240)
    NFM = (3072, 3072, 2048, 6144)
    WOUT = (2048, 1024, 2048, 1024)
    g_rep = [dr("g_rep%d" % l, [128, D]) for l in range(4)]
    g_fin = dr("g_fin", [128, D])
    w_in = [dr("w_in%d" % l, [D, NIN[l]]) for l in range(4)]
    w_out = [dr("w_out%d" % l, [WOUT[l], D]) for l in range(4)]
    ufm = [scratch("ufm%d" % i, [6144, S]) for i in range(2)]
    utm = [scratch("utm%d" % i, [S, 4096]) for i in range(2)]
    o_tm = scratch("o_tm", [S, 2048])
    oT_fm = scratch("oT_fm", [2048, S])
    xs = [scratch("xs%d" % i, [S, D]) for i in range(2)]
    od = scratch("od", [2, S, 512])
    nd = scratch("nd", [S, 3, 8, 65])

    def lin(layer, x_src, c, x_dst, final=False):
        for t0 in (0, TT):
            a = None
            if layer < 4:
                a = {"g_rep": g_rep[layer], "w_in": w_in[layer], "n_fm": NFM[layer], "n_tm": NIN[layer] - NFM[layer],
                     "ufm": ufm[layer % 2], "utm": utm[layer % 2]}
            cc = None
            if c is not None:
                cc = dict(c)
                cc["o_b"] = Buf("o")
                cc["gate_b"] = Buf("g")
                cc["xout"] = db(x_dst) if x_dst is not None else None
            fin = {"g_rep": g_fin, "out": out_d} if final else None
            emit_lin(P, A, t0, ident, db(x_src), c=cc, a=a, fin=fin)

    lin(0, x_in, None, None)
    for hh in range(2):
        emit_ssd(P, A, hh, dict(io, ufm=ufm[0], utm=utm[0], o=o_tm))
    lin(1, x_in, {"mode": "tm", "W": 2048, "o": o_tm, "gate": None, "w_out": w_out[0]}, xs[0])
    for hh in range(2):
        emit_hg(P, A, hh, dict(io, ufm=ufm[1], utm=utm[1], od=od, o=o_tm))
    lin(2, xs[0], {"mode": "tm", "W": 1024, "o": o_tm, "gate": utm[1][:, 1024:2048], "w_out": w_out[1]}, xs[1])
    for hh in range(2):
        emit_at(P, A, hh, dict(io, utm=utm[0], oT=oT_fm))
    lin(3, xs[1], {"mode": "fm", "W": 2048, "o": oT_fm, "gate": ufm[0][0:2048, :], "w_out": w_out[2]}, xs[0])
    for hh in range(2):
        emit_dl(P, A, hh, dict(io, ufm=ufm[1], utm=utm[1], nd=nd, o=o_tm))
    lin(4, xs[0], {"mode": "tm", "W": 1024, "o": o_tm, "gate": utm[1][:, 3072:4096], "w_out": w_out[3]}, None, final=True)
    print("fused program ops:", P.n_ops, dict(P.ecnt))
    return P.emit()


def fused_inputs(x, norm_g, final_g, rel_bias, hgrn_lb,
                 ssd_w_in, ssd_conv_w, ssd_conv_b, ssd_dt_bias, ssd_a_log, ssd_d, ssd_norm_g, ssd_w_out,
                 hg_w_in, hg_norm_g, hg_w_out, at_w_in, at_q_norm_g, at_k_norm_g, at_w_out, dl_w_in, dl_w_out):
    f = lambda a: np.ascontiguousarray(np.asarray(a, dtype=np.float32))
    rep = lambda v: np.ascontiguousarray(np.broadcast_to(f(v)[None], (128,) + tuple(np.shape(v))))
    tm, nm = ssd_consts()
    sm, masks = hg_consts()
    cos, sin = rope_tables()
    w0 = f(ssd_w_in)[0]
    w2 = f(at_w_in)[0]
    w3 = f(dl_w_in)[0]
    fm3 = np.concatenate([np.arange(g * 3072 + s * 1024, g * 3072 + (s + 1) * 1024) for g in range(3) for s in range(2)])
    tm3 = np.concatenate([np.arange(g * 3072 + 2048, g * 3072 + 3072) for g in range(3)] + [np.arange(9216, 10240)])
    small = [ssd_small(hh, f(ssd_conv_w)[0], f(ssd_conv_b)[0], f(ssd_dt_bias)[0], f(ssd_a_log)[0], f(ssd_d)[0], f(ssd_norm_g)[0])
             for hh in range(2)]
    lb4 = f(hgrn_lb).reshape(4, 8, 128)
    gain = np.concatenate([np.broadcast_to(f(at_q_norm_g)[0][None, None], (128, 8, 128)),
                           np.broadcast_to(f(at_k_norm_g)[0][None, None], (128, 4, 128))], 1)
    common = {
        "ident": bf16_np(np.eye(128)), "identf": np.eye(128, dtype=np.float32),
        "convw": np.stack([s_[0] for s_ in small]), "convb": np.stack([s_[1] for s_ in small]),
        "dtb_rep": np.stack([s_[2] for s_ in small]), "alog_rep": np.stack([s_[3] for s_ in small]),
        "d_rep": np.stack([s_[4] for s_ in small]), "ssd_ng_rep": np.stack([s_[5] for s_ in small]),
        "tmat": tm, "negm": nm,
        "lbT": np.stack([np.ascontiguousarray(lb4[:, 4 * hh:4 * hh + 4].transpose(2, 0, 1)) for hh in range(2)]),
        "hg_ng_rep": np.stack([rep(f(hg_norm_g)[0][hh * 512:(hh + 1) * 512]) for hh in range(2)]),
        "smask": sm, "masks": masks, "cos": cos, "sin": sin, "gain": np.ascontiguousarray(gain, dtype=np.float32),
        "dl_bias": np.stack([dl_bias_tables(f(rel_bias), list(range(8 * hh, 8 * hh + 8))) for hh in range(2)]),
        "g_rep0": rep(f(norm_g)[0]), "g_rep1": rep(f(norm_g)[1]), "g_rep2": rep(f(norm_g)[2]), "g_rep3": rep(f(norm_g)[3]),
        "g_fin": rep(f(final_g)),
        "w_in0": np.ascontiguousarray(np.concatenate([w0[:, 2048:5120], w0[:, 0:2048], w0[:, 5120:5184]], 1)),
        "w_in1": f(hg_w_in)[0],
        "w_in2": np.ascontiguousarray(np.concatenate([w2[:, 4096:6144], w2[:, 0:4096]], 1)),
        "w_in3": np.ascontiguousarray(np.concatenate([w3[:, fm3], w3[:, tm3]], 1)),
        "w_out0": f(ssd_w_out)[0], "w_out1": f(hg_w_out)[0], "w_out2": f(at_w_out)[0], "w_out3": f(dl_w_out)[0],
    }
    x = f(x)
    return [dict(common, x=x[core % x.shape[0]]) for core in range(8)]


def kernel(**inputs):
    B = np.asarray(inputs["x"]).shape[0]
    maps = fused_inputs(**inputs)
    nc = build_fused()
    res = run_bass_kernel_spmd(nc, maps, core_ids=list(range(8)))
    return np.stack([res.results[b]["out"] for b in range(B)], 0)
```

```python
from contextlib import ExitStack
import numpy as np
import ml_dtypes
import concourse.bass as bass
import concourse.mybir as mybir
from concourse.bass_utils import run_bass_kernel_spmd

F32 = mybir.dt.float32
BF16 = mybir.dt.bfloat16
I32 = mybir.dt.int32
AF = mybir.ActivationFunctionType
ALU = mybir.AluOpType
AX = mybir.AxisListType

ENGS = ("pe", "act", "dve", "pool", "sp")
DMA_SLOTS = 6


class Buf:
    __slots__ = ("name", "w", "r", "excl")

    def __init__(self, name, excl=False):
        self.name = name
        self.excl = excl
        self.w = None
        self.r = []


class Prog:
    def __init__(self):
        self.nc = bass.Bass("TRN2", target_bir_lowering=False)
        self.es = ExitStack()
        self.q = {e: [] for e in ENGS}
        self.sem = {}
        self.seen = {e: {} for e in ENGS}
        self.ecnt = {e: 0 for e in ENGS}
        self.dma_i = {qe: 0 for qe in ("sp", "act", "pool")}
        self.dcnt = {}
        self.n_ops = 0

    EPOCH = 20000

    def _sem(self, k):
        if k not in self.sem:
            self.sem[k] = self.es.enter_context(self.nc.semaphore("s%d" % len(self.sem)))
        return self.sem[k]

    def dram(self, name, shape, dtype=F32, kind="ExternalInput"):
        return self.nc.dram_tensor(name, list(shape), dtype, kind=kind).ap()

    def sbuf(self, name, shape, dtype=F32):
        return self.es.enter_context(self.nc.sbuf_tensor(name, list(shape), dtype))

    def psum(self, name, shape, dtype=F32):
        return self.es.enter_context(self.nc.psum_tensor(name, list(shape), dtype))

    def _deps(self, reads, writes):
        need = {}
        def add(ev):
            if ev is None:
                return
            k, v = ev
            if need.get(k, 0) < v:
                need[k] = v
        for b in reads:
            add(b.w)
            if b.excl:
                for ev in b.r:
                    add(ev)
        for b in writes:
            add(b.w)
            for ev in b.r:
                add(ev)
        return need

    def _commit(self, ev, reads, writes):
        for b in reads:
            if b.excl:
                b.r = [ev]
            else:
                b.r.append(ev)
        for b in writes:
            b.w = ev
            b.r = []

    def _waits(self, eng, need):
        ws = []
        seen = self.seen[eng]
        for k, v in need.items():
            if eng == "pe" and k[0] == "pe":
                continue
            if seen.get(k, 0) < v:
                seen[k] = v
                ws.append((k, v))
        return ws

    def op(self, eng, meth, *args, reads=(), writes=(), **kw):
        fn = (meth, args, kw)
        need = self._deps(reads, writes)
        ws = self._waits(eng, need)
        n = self.ecnt[eng]
        self.ecnt[eng] = n + 1
        k = (eng, n // self.EPOCH)
        ev = (k, n % self.EPOCH + 1)
        self.q[eng].append((ws, fn, k, 1))
        self._commit(ev, reads, writes)
        self.n_ops += 1
        return ev

    def dma(self, qe, out, in_, reads=(), writes=(), **kw):
        need = self._deps(reads, writes)
        i = self.dma_i[qe]
        self.dma_i[qe] = i + 1
        slot = i % DMA_SLOTS
        ep = (i // DMA_SLOTS) // 1000
        k = ("dma", qe, slot, ep)
        c = self.dcnt.get(k, 0)
        if c > 0:
            need[k] = max(need.get(k, 0), c)
        elif ep > 0:
            kp = ("dma", qe, slot, ep - 1)
            need[kp] = max(need.get(kp, 0), self.dcnt[kp])
        ws = self._waits(qe, need)
        self.dcnt[k] = c + 16
        ev = (k, c + 16)
        self.q[qe].append((ws, ("dma_start", (), dict(out=out, in_=in_, **kw)), k, 16))
        self._commit(ev, reads, writes)
        self.n_ops += 1
        return ev

    def wait_all_dma(self):
        for qe in ("sp", "act", "pool"):
            need = {k: c for k, c in self.dcnt.items() if k[1] == qe}
            ws = self._waits(qe, need)
            if ws:
                self.q[qe].append((ws, None, None, 0))


    def fence(self):
        need = {}
        for e in ENGS:
            n = self.ecnt[e]
            if n > 0:
                need[(e, (n - 1) // self.EPOCH)] = (n - 1) % self.EPOCH + 1
        for k, c in self.dcnt.items():
            need[k] = c
        for e in ENGS:
            ws = self._waits(e, dict(need))
            if ws:
                self.q[e].append((ws, None, None, 0))

    def emit(self):
        self.wait_all_dma()
        nc = self.nc
        engobj = {"pe": "tensor", "act": "scalar", "dve": "vector", "pool": "gpsimd", "sp": "sync"}
        with nc.Block() as block:
            for e in ENGS:
                items = self.q[e]

                def body(eng, items=items):
                    for ws, fn, sk, inc in items:
                        for k, v in ws:
                            eng.wait_ge(self._sem(k), v)
                        if fn is not None:
                            ins = getattr(eng, fn[0])(*fn[1], **fn[2])
                            ins.then_inc(self._sem(sk), inc)
                getattr(block, engobj[e])(body)
        self.es.close()
        return nc


class T:
    def __init__(self, P, name, shape, dtype=F32, psum=False):
        if psum:
            esz = 4 if dtype == F32 else 2
            n = int(np.prod(shape[1:]))
            assert n * esz <= 2048
            h = P.psum("t_" + name, [128, 2048 // esz], dtype)
            v = h[0:shape[0], 0:n]
            if len(shape) == 3:
                v = v.rearrange("p (a b) -> p a b", b=shape[2])
            elif len(shape) == 4:
                v = v.rearrange("p (a b c) -> p a b c", b=shape[2], c=shape[3])
            self.t = v
        else:
            self.t = P.sbuf("t_" + name, shape, dtype)
        self.b = Buf(name, excl=psum)

    def __getitem__(self, k):
        return self.t[k]


def bf16_np(a):
    return np.asarray(a, dtype=np.float32).astype(ml_dtypes.bfloat16)


class TV:
    def __init__(self, ap, name, excl):
        self.t = ap
        self.b = Buf(name, excl=excl)

    def __getitem__(self, k):
        return self.t[k]


class Arena:
    def __init__(self, P, kib=188):
        self.P = P
        self.n32 = kib * 256
        self.h = P.sbuf("arena", [128, self.n32], F32)
        self.banks = [P.psum("bank%d" % i, [128, 512], F32) for i in range(8)]
        self.off = 0
        self.nb = 0

    def reset(self):
        self.P.fence()
        self.off = 0
        self.nb = 0

    def T(self, name, shape, dtype=F32, psum=False):
        esz = 4 if dtype == F32 else 2
        n = int(np.prod(shape[1:]))
        n32 = (n * esz + 3) // 4
        bank = None
        if psum:
            assert n32 <= 512 and self.nb < 8, name
            bank = self.banks[self.nb]
            v = bank[0:shape[0], 0:n32]
            self.nb += 1
        else:
            n32 = (n32 + 7) // 8 * 8
            assert self.off + n32 <= self.n32, (name, self.off, n32)
            v = self.h[0:shape[0], self.off:self.off + n32]
            self.off += n32
        if dtype != F32:
            v = v.bitcast(dtype)
        v = v[:, 0:n]
        if len(shape) == 3:
            v = v.rearrange("p (a b) -> p a b", b=shape[2])
        elif len(shape) == 4:
            v = v.rearrange("p (a b c) -> p a b c", b=shape[2], c=shape[3])
        tv = TV(v, name, psum)
        tv.bank = bank
        return tv


TT = 2048
NTT = TT // 128
D = 1024
EPS = 1e-6
S = 4096
NST = S // 128


def emit_lin(P, A, t0, ident_d, x_d, c=None, a=None, fin=None):
    A.reset()
    T = A.T
    has_c, has_a, final = c is not None, a is not None, fin is not None
    ident = T("ident", [128, 128], BF16)
    P.dma("sp", ident[:], ident_d, writes=[ident.b])
    wstage = [T("wstage%d" % i, [128, 4, 512]) for i in range(2)]
    ws_i = [0]

    def load_w_bf(dst, dst_k0, src_ap, ncols):
        st = wstage[ws_i[0] % 2]
        ws_i[0] += 1
        P.dma("sp", st[:, :, 0:ncols], src_ap, writes=[st.b])
        P.op("pool", "tensor_copy", dst[:, dst_k0:dst_k0 + 4, 0:ncols], st[:, :, 0:ncols], reads=[st.b], writes=[dst.b])

    xt = [T("xt%d" % i, [128, D]) for i in range(3)]
    tps = T("tps", [128, 8, 128], BF16, psum=True)
    if has_c:
        W, mode = c["W"], c["mode"]
        WC = W // 128
        use_gate = c["gate"] is not None
        wout = T("wout", [128, WC, D], BF16)
        wv = c["w_out"].rearrange("(kc p) n -> p kc n", p=128)
        for k0 in range(0, WC, 4):
            for nb in range(2):
                st = wstage[ws_i[0] % 2]
                ws_i[0] += 1
                P.dma("sp", st[:, :, :], wv[:, k0:k0 + 4, nb * 512:(nb + 1) * 512], writes=[st.b])
                P.op("pool", "tensor_copy", wout[:, k0:k0 + 4, nb * 512:(nb + 1) * 512], st[:, :, :],
                     reads=[st.b], writes=[wout.b])
        gTb = T("gTb", [128, WC, 512], BF16)
        yps = [[T("yps%d_%d" % (i, nb), [128, 512], F32, psum=True) for nb in range(2)] for i in range(2)]
        xout_b = Buf("xout")
        if mode == "fm":
            o_sb = [T("o_sb%d" % i, [128, 4, 512]) for i in range(2)]
            g_sb = [T("g_sb%d" % i, [128, 4, 512]) for i in range(2)] if use_gate else None
        else:
            o_tm = [T("o_tm%d" % i, [128, W]) for i in range(2)]
            g_tm = [T("g_tm%d" % i, [128, W]) for i in range(2)] if use_gate else None
            g_bf = [T("g_bf%d" % i, [128, W], BF16) for i in range(2)]
    if has_a or final:
        grep = T("grep", [128, D])
        P.dma("sp", grep[:], (a or fin)["g_rep"], writes=[grep.b])
        ssq = T("ssq", [128, 1])
        rstd = T("rstd", [128, 1])
        junk = T("junk", [128, D])
    if has_a:
        n_fm, n_tm = a["n_fm"], a["n_tm"]
        hnb = [T("hnb%d" % i, [128, D], BF16) for i in range(2)]
        hnT = T("hnT", [128, 8, TT], BF16)
        hnT_tb = [Buf("hnT%d" % i) for i in range(NTT)]
        winb = [T("winb%d" % i, [128, 8, 512], BF16) for i in range(2)]
        ups = [T("ups%d" % i, [128, 512], F32, psum=True) for i in range(3)]
        ust = [T("ust%d" % i, [128, 512]) for i in range(4)]
        u_b = Buf("u_scratch")
    if final:
        out_b = Buf("out_d")
        ost = [T("ost%d" % i, [128, D]) for i in range(2)]

    xi = 0
    ci = 0
    for blk in range(TT // 512):
        tb0 = t0 + blk * 512
        if has_c and mode == "fm":
            ov = c["o"].rearrange("(kc p) t -> p kc t", p=128)
            gv = c["gate"].rearrange("(kc p) t -> p kc t", p=128) if use_gate else None
            for k0 in range(0, WC, 4):
                i = ci % 2
                ci += 1
                ot = o_sb[i]
                P.dma("sp", ot[:], ov[:, k0:k0 + 4, tb0:tb0 + 512], reads=[c["o_b"]], writes=[ot.b])
                if use_gate:
                    gt = g_sb[i]
                    P.dma("sp", gt[:], gv[:, k0:k0 + 4, tb0:tb0 + 512], reads=[c["gate_b"]], writes=[gt.b])
                    P.op("act", "activation", gt[:], gt[:], AF.Silu, reads=[gt.b], writes=[gt.b])
                    P.op("dve", "tensor_tensor", gTb[:, k0:k0 + 4, :], ot[:], gt[:], ALU.mult, reads=[ot.b, gt.b], writes=[gTb.b])
                else:
                    P.op("dve", "tensor_copy", gTb[:, k0:k0 + 4, :], ot[:], reads=[ot.b], writes=[gTb.b])
        if has_c and mode == "tm":
            for ti in range(4):
                tsl = slice(tb0 + ti * 128, tb0 + (ti + 1) * 128)
                i = ci % 2
                ci += 1
                ot, gb = o_tm[i], g_bf[i]
                P.dma("sp", ot[:], c["o"][tsl, 0:W], reads=[c["o_b"]], writes=[ot.b])
                if use_gate:
                    gt = g_tm[i]
                    P.dma("sp", gt[:], c["gate"][tsl, :], reads=[c["gate_b"]], writes=[gt.b])
                    P.op("act", "activation", gt[:], gt[:], AF.Silu, reads=[gt.b], writes=[gt.b])
                    P.op("dve", "tensor_tensor", gb[:], ot[:], gt[:], ALU.mult, reads=[ot.b, gt.b], writes=[gb.b])
                else:
                    P.op("dve", "tensor_copy", gb[:], ot[:], reads=[ot.b], writes=[gb.b])
                for w0 in range(0, WC, 8):
                    nw = min(8, WC - w0)
                    for k in range(nw):
                        P.op("pe", "transpose", tps[:, k, :], gb[:, (w0 + k) * 128:(w0 + k + 1) * 128], ident[:],
                             reads=[gb.b, ident.b], writes=[tps.b])
                    P.op("act", "copy", gTb[:, w0:w0 + nw, ti * 128:(ti + 1) * 128], tps[:, 0:nw, :],
                         reads=[tps.b], writes=[gTb.b])
        for ti in range(4):
            tt = blk * 4 + ti
            tsl = slice(tb0 + ti * 128, tb0 + (ti + 1) * 128)
            xb = xt[xi % 3]
            xi += 1
            P.dma("sp", xb[:], x_d["ap"][tsl, :], reads=[x_d["b"]], writes=[xb.b])
            if has_c:
                yp = yps[tt % 2]
                for nb in range(2):
                    for wc in range(WC):
                        P.op("pe", "matmul", yp[nb][:], gTb[:, wc, ti * 128:(ti + 1) * 128],
                             wout[:, wc, nb * 512:(nb + 1) * 512], start=(wc == 0), stop=(wc == WC - 1),
                             reads=[gTb.b, wout.b], writes=[yp[nb].b])
                for nb in range(2):
                    P.op("dve", "tensor_tensor", xb[:, nb * 512:(nb + 1) * 512], xb[:, nb * 512:(nb + 1) * 512], yp[nb][:],
                         ALU.add, reads=[xb.b, yp[nb].b], writes=[xb.b])
                if not final:
                    P.dma("act", c["xout"]["ap"][tsl, :], xb[:], reads=[xb.b], writes=[c["xout"]["b"]])
            if has_a or final:
                P.op("act", "activation", junk[:], xb[:], AF.Square, accum_out=ssq[:], reads=[xb.b], writes=[junk.b, ssq.b])
                P.op("act", "activation", rstd[:], ssq[:], AF.Sqrt, bias=EPS, scale=1.0 / D, reads=[ssq.b], writes=[rstd.b])
                P.op("dve", "reciprocal", rstd[:], rstd[:], reads=[rstd.b], writes=[rstd.b])
            if final:
                ot_ = ost[tt % 2]
                P.op("dve", "scalar_tensor_tensor", ot_[:], xb[:], rstd[:, 0:1], grep[:], ALU.mult, ALU.mult,
                     reads=[xb.b, rstd.b, grep.b], writes=[ot_.b])
                P.dma("act", fin["out"][tsl, :], ot_[:], reads=[ot_.b], writes=[out_b])
            if has_a:
                hb = hnb[tt % 2]
                P.op("dve", "scalar_tensor_tensor", hb[:], xb[:], rstd[:, 0:1], grep[:], ALU.mult, ALU.mult,
                     reads=[xb.b, rstd.b, grep.b], writes=[hb.b])
                for kc in range(8):
                    P.op("pe", "transpose", tps[:, kc, :], hb[:, kc * 128:(kc + 1) * 128], ident[:],
                         reads=[hb.b, ident.b], writes=[tps.b])
                P.op("act", "copy", hnT[:, :, tt * 128:(tt + 1) * 128], tps[:], reads=[tps.b], writes=[hnT_tb[tt]])
    if has_a:
        N = n_fm + n_tm
        wv = a["w_in"].rearrange("(kc p) n -> p kc n", p=128)
        nblk = (N + 511) // 512
        assert n_fm % 512 == 0
        ui = 0
        for nb in range(nblk):
            c0 = nb * 512
            cw = min(512, N - c0)
            wb = winb[nb % 2]
            for k0 in (0, 4):
                load_w_bf(wb, k0, wv[:, k0:k0 + 4, c0:c0 + cw], cw)
            if c0 < n_fm:
                for cc in range(4):
                    for tb in range(TT // 512):
                        up, us = ups[ui % 3], ust[ui % 4]
                        for kc in range(8):
                            P.op("pe", "matmul", up[:], wb[:, kc, cc * 128:(cc + 1) * 128], hnT[:, kc, tb * 512:(tb + 1) * 512],
                                 start=(kc == 0), stop=(kc == 7),
                                 reads=[wb.b] + hnT_tb[tb * 4:tb * 4 + 4], writes=[up.b])
                        P.op("act", "copy", us[:], up[:], reads=[up.b], writes=[us.b])
                        r0 = c0 + cc * 128
                        P.dma("act", a["ufm"][r0:r0 + 128, t0 + tb * 512:t0 + (tb + 1) * 512], us[:], reads=[us.b], writes=[u_b])
                        ui += 1
            else:
                for tt in range(NTT):
                    up, us = ups[ui % 3], ust[ui % 4]
                    for kc in range(8):
                        P.op("pe", "matmul", up[:, 0:cw], hnT[:, kc, tt * 128:(tt + 1) * 128], wb[:, kc, 0:cw],
                             start=(kc == 0), stop=(kc == 7), reads=[hnT_tb[tt], wb.b], writes=[up.b])
                    P.op("act", "copy", us[:, 0:cw], up[:, 0:cw], reads=[up.b], writes=[us.b])
                    P.dma("act", a["utm"][t0 + tt * 128:t0 + (tt + 1) * 128, c0 - n_fm:c0 - n_fm + cw], us[:, 0:cw],
                          reads=[us.b], writes=[u_b])
                    ui += 1


SSD_NEG = -30000.0


def emit_ssd(P, A, hh, io):
    A.reset()
    T = A.T
    NCK = S // 128
    ufm, utm, o_d = io["ufm"], io["utm"], io["o"]
    o_b = Buf("o_d")

    def const(name, d_ap, shape, dtype=F32):
        t = T(name, shape, dtype)
        P.dma("sp", t[:], d_ap, writes=[t.b])
        return t
    cw = const("cw", io["convw"][hh], [128, 2, 6, 7])
    cbias = const("cbias", io["convb"][hh], [128, 2, 6])
    dtb = const("dtb", io["dtb_rep"][hh], [128, 2, 2, 8])
    alog = const("alog", io["alog_rep"][hh], [128, 2, 2, 8])
    dsk = const("dsk", io["d_rep"][hh], [128, 2, 8])
    ngr = const("ngr", io["ssd_ng_rep"][hh], [128, 2, 512])
    tmat = const("tmatc", io["tmat"], [128, 2, 128])
    negm = const("negmc", io["negm"], [128, 2, 4, 128])
    identf = const("identf", io["identf"], [128, 128])
    ident = const("ident", io["ident"], [128, 128], BF16)
    onesf = T("onesf", [128, 128])
    P.op("pool", "memset", onesf[:], 1.0, writes=[onesf.b])
    negmb = T("negmb", [128, 2, 4, 128], BF16)
    P.op("pool", "tensor_copy", negmb[:], negm[:], reads=[negm.b], writes=[negmb.b])
    P.op("act", "activation", alog[:], alog[:], AF.Exp, reads=[alog.b], writes=[alog.b])
    P.op("dve", "tensor_scalar", alog[:], alog[:], -1.0, None, ALU.mult, reads=[alog.b], writes=[alog.b])

    xcT = T("xcT", [128, 6, S], BF16)
    yb = T("yb", [128, NCK, 512])
    ybf = yb[:].rearrange("p c e -> p (c e)")
    raws = [ybf[:, 0:S + 6], ybf[:, 4104:4104 + S + 6]]
    raw_b = [Buf("raw0"), Buf("raw1")]
    rawbf = [ybf[:, 8208:8208 + 2056].bitcast(BF16)[:, 0:S + 6], ybf[:, 10272:10272 + 2056].bitcast(BF16)[:, 0:S + 6]]
    rawbf_b = [Buf("rawbf0"), Buf("rawbf1")]
    dgs = [T("dg%d" % i, [128, 7, 128], BF16) for i in range(2)]
    rci = 0
    cpi = 0
    dtall = T("dtall", [128, NCK, 2, 8])
    dA = T("dA", [128, NCK, 2, 8])
    ncum = T("ncum", [128, NCK, 2, 8])
    ecum = T("ecum", [128, NCK, 2, 8])
    dtd = T("dtd", [128, NCK, 2, 8])
    etot = T("etot", [128, NCK, 2, 8])
    fr = T("fr", [128, 512], F32, psum=True)
    tpx = fr[:, 0:256].bitcast(BF16).rearrange("p (k e) -> p k e", e=128)
    tpb = fr[:, 256:320].bitcast(BF16)
    cbp = fr[:, 320:448]
    bcp = [T("bcp%d" % i, [128, 4, 128], F32, psum=True) for i in range(4)]
    yps = T("yps", [128, 512], F32, psum=True)
    yop = T("yop", [128, 512], F32, psum=True)
    stp = T("stp", [128, 512], F32, psum=True)
    xst = [T("xst%d" % i, [128, 8, 64], BF16) for i in range(2)]
    bst = [T("bst%d" % i, [128, 128], BF16) for i in range(2)]
    cbT = [T("cbT%d" % i, [128, 128]) for i in range(2)]
    dAb = [T("dAb%d" % i, [128, 8, 128]) for i in range(2)]
    xdt = [T("xdt%d" % i, [128, 8, 64], BF16) for i in range(2)]
    xdd = [T("xdd%d" % i, [128, 8, 64], BF16) for i in range(2)]
    NLT = 6
    LT = [T("LT%d" % i, [128, 128]) for i in range(NLT)]
    MT = [T("MT%d" % i, [128, 128], BF16) for i in range(NLT)]
    ytm = [T("ytm%d" % i, [128, 8, 64]) for i in range(2)]
    yt = [T("yt%d" % i, [128, 8, 64]) for i in range(2)]
    zt = [T("zt%d" % i, [128, 512]) for i in range(2)]
    Sst = T("Sst", [128, 8, 64])
    Sb = [T("Sb%d" % i, [128, 512], BF16) for i in range(2)]
    ssq = T("ssq", [128, 1])
    junk = T("junk", [128, 512])
    li = 0

    for gl in range(2):
        g = 2 * hh + gl
        rows = [g * 512 + k * 128 for k in range(4)] + [2048 + g * 128, 2048 + 512 + g * 128]
        P.fence()
        cps = [yps, yop, stp]
        for ci in range(6):
            raw, rb = raws[rci % 2], raw_b[rci % 2]
            rwb, rwb_b = rawbf[rci % 2], rawbf_b[rci % 2]
            rci += 1
            P.op("pool", "memset", raw[:, 0:3], 0.0, writes=[rb])
            P.op("pool", "memset", raw[:, S + 3:S + 6], 0.0, writes=[rb])
            P.dma("sp", raw[:, 3:3 + S], ufm[rows[ci]:rows[ci] + 128, :], writes=[rb])
            P.op("act", "copy", rwb, raw, reads=[rb], writes=[rwb_b])
            dg = dgs[ci % 2]
            P.op("dve", "tensor_tensor", dg[:], ident[:].unsqueeze(1).to_broadcast([128, 7, 128]),
                 cw[:, gl, ci, :].unsqueeze(2).to_broadcast([128, 7, 128]), ALU.mult, reads=[ident.b, cw.b], writes=[dg.b])
            for blk in range(S // 512):
                cp = cps[cpi % 3]
                cpi += 1
                for k in range(7):
                    P.op("pe", "matmul", cp[:], dg[:, k, :], rwb[:, blk * 512 + k:blk * 512 + k + 512], start=(k == 0), stop=(k == 6),
                         reads=[dg.b, rwb_b], writes=[cp.b])
                P.op("act", "activation", xcT[:, ci, blk * 512:(blk + 1) * 512], cp[:], AF.Silu, bias=cbias[:, gl, ci:ci + 1],
                     reads=[cp.b, cbias.b], writes=[xcT.b])
        P.fence()
        for dr in range(2):
            c0 = 2048 + dr * 32 + g * 8
            P.dma("sp", dtall[:, :, dr, :], utm[:, c0:c0 + 8].rearrange("(c p) j -> p c j", p=128), writes=[dtall.b])
        P.op("dve", "tensor_tensor", dtall[:], dtall[:], dtb[:, gl].unsqueeze(1).to_broadcast([128, NCK, 2, 8]), ALU.add,
             reads=[dtall.b, dtb.b], writes=[dtall.b])
        P.op("act", "activation", dtall[:], dtall[:], AF.Exp, reads=[dtall.b], writes=[dtall.b])
        P.op("act", "activation", dtall[:], dtall[:], AF.Ln, bias=1.0, reads=[dtall.b], writes=[dtall.b])
        P.op("dve", "tensor_tensor", dA[:], dtall[:], alog[:, gl].unsqueeze(1).to_broadcast([128, NCK, 2, 8]), ALU.mult,
             reads=[dtall.b, alog.b], writes=[dA.b])
        cum_ps = yps[:, 0:256].rearrange("p (c j) -> p c j", j=8)
        tot_ps = yop[:, 0:256].rearrange("p (c j) -> p c j", j=8)
        for dr in range(2):
            P.op("pe", "matmul", cum_ps, tmat[:, dr, :], dA[:, :, dr, :], start=True, stop=True, reads=[tmat.b, dA.b], writes=[yps.b])
            P.op("pe", "matmul", tot_ps, onesf[:], dA[:, :, dr, :], start=True, stop=True, reads=[onesf.b, dA.b], writes=[yop.b])
            P.op("dve", "tensor_scalar", ncum[:, :, dr, :], cum_ps, -1.0, None, ALU.mult, reads=[yps.b], writes=[ncum.b])
            P.op("act", "activation", ecum[:, :, dr, :], cum_ps, AF.Exp, reads=[yps.b], writes=[ecum.b])
            P.op("act", "activation", etot[:, :, dr, :], tot_ps, AF.Exp, reads=[yop.b], writes=[etot.b])
            P.op("dve", "tensor_tensor", dtd[:, :, dr, :], ncum[:, :, dr, :], tot_ps, ALU.add, reads=[ncum.b, yop.b], writes=[dtd.b])
        P.op("act", "activation", dtd[:], dtd[:], AF.Exp, reads=[dtd.b], writes=[dtd.b])
        P.op("dve", "tensor_tensor", dtd[:], dtd[:], dtall[:], ALU.mult, reads=[dtd.b, dtall.b], writes=[dtd.b])

        for dr in (1, 0):
            P.op("dve", "memset", Sst[:], 0.0, writes=[Sst.b])
            P.op("pool", "memset", Sb[0][:], 0.0, writes=[Sb[0].b])
            order = list(range(NCK)) if dr == 0 else list(range(NCK - 1, -1, -1))
            def front(n, c):
                cs = slice(c * 128, (c + 1) * 128)
                i2 = n % 2
                xs_, bs_, cb_, dab_, xd_, xq_ = xst[i2], bst[i2], cbT[i2], dAb[i2], xdt[i2], xdd[i2]
                for k in range(4):
                    P.op("pe", "transpose", tpx[:, k, :], xcT[:, k, cs], ident[:], reads=[xcT.b, ident.b], writes=[fr.b])
                P.op("pe", "transpose", tpb, xcT[:, 4, cs], ident[:], reads=[xcT.b, ident.b], writes=[fr.b])
                P.op("pe", "matmul", cbp, xcT[:, 4, cs], xcT[:, 5, cs], start=True, stop=True, reads=[xcT.b], writes=[fr.b])
                P.op("act", "copy", xs_[:].rearrange("p j e -> p (j e)"), tpx.rearrange("p k e -> p (k e)"),
                     reads=[fr.b], writes=[xs_.b])
                P.op("act", "copy", bs_[:], tpb, reads=[fr.b], writes=[bs_.b])
                P.op("act", "copy", cb_[:], cbp, reads=[fr.b], writes=[cb_.b])
                P.op("dve", "tensor_tensor", xd_[:], xs_[:], dtall[:, c, dr, :].unsqueeze(2).to_broadcast([128, 8, 64]), ALU.mult,
                     reads=[xs_.b, dtall.b], writes=[xd_.b])
                P.op("pool", "tensor_tensor", xq_[:], xs_[:], dtd[:, c, dr, :].unsqueeze(2).to_broadcast([128, 8, 64]), ALU.mult,
                     reads=[xs_.b, dtd.b], writes=[xq_.b])
                P.op("dve", "tensor_tensor", dab_[:], tmat[:, dr, :].unsqueeze(1).to_broadcast([128, 8, 128]),
                     dA[:, c, dr, :].unsqueeze(2).to_broadcast([128, 8, 128]), ALU.mult,
                     reads=[tmat.b, dA.b], writes=[dab_.b])
                for h2 in range(2):
                    bp = bcp[i2 * 2 + h2]
                    P.op("pe", "matmul", bp[:].rearrange("p r l -> p (r l)"), ident[:],
                         negmb[:, dr].rearrange("p r l -> p (r l)"), start=True, stop=False,
                         reads=[ident.b, negmb.b], writes=[bp.b])
                    P.op("pe", "matmul", bp[:].rearrange("p r l -> p (r l)"), onesf[:],
                         dab_[:, h2 * 4:h2 * 4 + 4, :].rearrange("p r l -> p (r l)"), start=False, stop=True,
                         reads=[onesf.b, dab_.b], writes=[bp.b])
            front(0, order[0])
            for n, c in enumerate(order):
                cs = slice(c * 128, (c + 1) * 128)
                i2 = n % 2
                xs_, bs_, cb_, dab_, xd_, xq_ = xst[i2], bst[i2], cbT[i2], dAb[i2], xdt[i2], xdd[i2]
                sb_cur, sb_nxt = Sb[n % 2], Sb[(n + 1) % 2]
                if n + 1 < NCK:
                    front(n + 1, order[n + 1])
                for h2 in range(2):
                    bp = bcp[i2 * 2 + h2]
                    for jj in range(4):
                        j = h2 * 4 + jj
                        lt, mt = LT[li % NLT], MT[li % NLT]
                        li += 1
                        P.op("act", "activation", lt[:], bp[:, jj, :], AF.Exp, bias=ncum[:, c, dr, j:j + 1],
                             reads=[bp.b, ncum.b], writes=[lt.b])
                        P.op("pool" if li % 2 == 0 else "dve", "tensor_tensor", mt[:], lt[:], cb_[:], ALU.mult,
                             reads=[lt.b, cb_.b], writes=[mt.b])
                        P.op("pe", "matmul", yps[:, j * 64:(j + 1) * 64], mt[:], xd_[:, j, :], start=True, stop=True,
                             reads=[mt.b, xd_.b], writes=[yps.b])
                P.op("pe", "matmul", yop[:], xcT[:, 5, cs], sb_cur[:], start=True, stop=True, reads=[xcT.b, sb_cur.b], writes=[yop.b])
                ym, y_ = ytm[i2], yt[i2]
                P.op("dve", "tensor_tensor", ym[:], yop[:].rearrange("p (j e) -> p j e", e=64),
                     ecum[:, c, dr, :].unsqueeze(2).to_broadcast([128, 8, 64]), ALU.mult, reads=[yop.b, ecum.b], writes=[ym.b])
                if dr == 1:
                    P.op("dve", "tensor_tensor", yb[:, c, :], ym[:].rearrange("p j e -> p (j e)"), yps[:], ALU.add,
                         reads=[ym.b, yps.b], writes=[yb.b])
                else:
                    P.op("dve", "tensor_tensor", y_[:].rearrange("p j e -> p (j e)"), ym[:].rearrange("p j e -> p (j e)"), yps[:],
                         ALU.add, reads=[ym.b, yps.b], writes=[y_.b])
                P.op("pe", "matmul", stp[:], bs_[:], xq_[:].rearrange("p j e -> p (j e)"), start=True, stop=True,
                     reads=[bs_.b, xq_.b], writes=[stp.b])
                P.op("dve", "tensor_tensor", Sst[:], Sst[:], etot[:, c, dr, :].unsqueeze(2).to_broadcast([128, 8, 64]), ALU.mult,
                     reads=[Sst.b, etot.b], writes=[Sst.b])
                P.op("dve", "tensor_tensor", Sst[:].rearrange("p j e -> p (j e)"), Sst[:].rearrange("p j e -> p (j e)"), stp[:],
                     ALU.add, reads=[Sst.b, stp.b], writes=[Sst.b])
                P.op("act", "copy", sb_nxt[:], Sst[:].rearrange("p j e -> p (j e)"), reads=[Sst.b], writes=[sb_nxt.b])
                if dr == 0:
                    z_ = zt[i2]
                    P.dma("sp", z_[:], utm[cs, g * 512:(g + 1) * 512], writes=[z_.b])
                    P.op("act", "activation", z_[:], z_[:], AF.Silu, reads=[z_.b], writes=[z_.b])
                    yf = y_[:].rearrange("p j e -> p (j e)")
                    P.op("dve", "tensor_tensor", yf, yf, yb[:, c, :], ALU.add, reads=[y_.b, yb.b], writes=[y_.b])
                    P.op("pool", "tensor_tensor", ym[:], xs_[:], dsk[:, gl, :].unsqueeze(2).to_broadcast([128, 8, 64]), ALU.mult,
                         reads=[xs_.b, dsk.b], writes=[ym.b])
                    P.op("dve", "tensor_tensor", y_[:], y_[:], ym[:], ALU.add, reads=[y_.b, ym.b], writes=[y_.b])
                    P.op("dve", "tensor_tensor", yf, yf, z_[:], ALU.mult, reads=[y_.b, z_.b], writes=[y_.b])
                    P.op("act", "activation", junk[:], yf, AF.Square, accum_out=ssq[:], reads=[y_.b], writes=[junk.b, ssq.b])
                    P.op("act", "activation", ssq[:], ssq[:], AF.Sqrt, bias=EPS, scale=1.0 / 512, reads=[ssq.b], writes=[ssq.b])
                    P.op("dve", "reciprocal", ssq[:], ssq[:], reads=[ssq.b], writes=[ssq.b])
                    P.op("dve", "scalar_tensor_tensor", z_[:], yf, ssq[:, 0:1], ngr[:, gl, :], ALU.mult, ALU.mult,
                         reads=[y_.b, ssq.b, ngr.b], writes=[z_.b])
                    P.dma("sp", o_d[cs, g * 512:(g + 1) * 512], z_[:], reads=[z_.b], writes=[o_b])


def ssd_consts():
    s = np.arange(128)[:, None]
    l = np.arange(128)[None, :]
    tm = np.stack([(s <= l), (s >= l)], 1).astype(np.float32)
    nm = np.stack([np.where(l < s, SSD_NEG, 0.0), np.where(l > s, SSD_NEG, 0.0)], 1).astype(np.float32)
    nm = np.ascontiguousarray(np.repeat(nm[:, :, None, :], 4, axis=2))
    return tm, nm


def ssd_small(hh, conv_w, conv_b, dt_bias, a_log, d_skip, norm_g):
    cwt = np.empty((128, 2, 6, 7), np.float32)
    cbt = np.empty((128, 2, 6), np.float32)
    dtb = np.empty((128, 2, 2, 8), np.float32)
    alog = np.empty((128, 2, 2, 8), np.float32)
    dsk = np.empty((128, 2, 8), np.float32)
    ng = np.empty((128, 2, 512), np.float32)
    for gl in range(2):
        g = 2 * hh + gl
        chans = [np.arange(g * 512 + k * 128, g * 512 + (k + 1) * 128) for k in range(4)]
        chans.append(np.arange(2048 + g * 128, 2048 + (g + 1) * 128))
        chans.append(np.arange(2048 + 512 + g * 128, 2048 + 512 + (g + 1) * 128))
        for ci, ch in enumerate(chans):
            cwt[:, gl, ci, :] = conv_w[:, ch].T
            cbt[:, gl, ci] = conv_b[ch]
        hs = slice(g * 8, (g + 1) * 8)
        dtb[:, gl] = dt_bias[:, hs][None]
        alog[:, gl] = a_log[:, hs][None]
        dsk[:, gl] = d_skip[hs][None]
        ng[:, gl] = norm_g[g * 512:(g + 1) * 512][None]
    return cwt, cbt, dtb, alog, dsk, ng


HG_LAYER = 1


def emit_hg(P, A, hh, io):
    A.reset()
    T = A.T
    NCH = S // 32
    ufm, utm, od_d, o_d = io["ufm"], io["utm"], io["od"], io["o"]
    od_b = Buf("od_d")
    o_b = Buf("o_d")

    def const(name, d_ap, shape, dtype=F32):
        t = T(name, shape, dtype)
        P.dma("sp", t[:], d_ap, writes=[t.b])
        return t
    ident = const("ident", io["ident"], [128, 128], BF16)
    smask = const("smask", io["smask"], [128, 1024])
    masks = const("masks", io["masks"], [32, 2, 32])
    ng = const("ng", io["hg_ng_rep"][hh], [128, 512])
    lbt = const("lbt", io["lbT"][hh], [128, 4, 4])
    lsum = T("lsum", [128, 4])
    lb = T("lb", [128, 4])
    oml = T("oml", [128, 4])
    P.op("act", "activation", lbt[:], lbt[:], AF.Exp, reads=[lbt.b], writes=[lbt.b])
    P.op("dve", "tensor_reduce", lsum[:], lbt[:].rearrange("p l h -> p h l"), AX.X, ALU.add, reads=[lbt.b], writes=[lsum.b])
    P.op("dve", "reciprocal", lsum[:], lsum[:], reads=[lsum.b], writes=[lsum.b])
    P.op("dve", "tensor_tensor", lb[:], lbt[:, HG_LAYER, :], lsum[:], ALU.mult, reads=[lbt.b, lsum.b], writes=[lb.b])
    P.op("dve", "tensor_scalar", oml[:], lb[:], -1.0, 1.0, ALU.mult, ALU.add, reads=[lb.b], writes=[oml.b])

    PW = 1024
    NP_ = S // PW
    qs = T("qs", [128, S])
    vstg = T("vstg", [32, 16, 128])
    v32 = T("v32b", [32, NCH, 128], BF16)
    qtl = T("qtl", [128, S], BF16)
    ktl = T("ktl", [128, S], BF16)
    qht = T("qht", [128, S], BF16)
    kh32 = T("kh32", [32, NCH, 128], BF16)
    ec = T("ec", [128, NCH])
    fr = [T("fr%d" % i, [128, PW]) for i in range(2)]
    tf2 = [T("tf0", [128, PW])] * 2
    tg2 = [T("tg0", [128, PW])] * 2
    tk2 = [T("tk%d" % i, [128, PW]) for i in range(2)]
    tX2 = [T("tX%d" % i, [128, PW]) for i in range(2)]
    tD2 = [T("tD%d" % i, [128, PW]) for i in range(2)]
    tE4 = [T("tE%d" % i, [128, PW]) for i in range(4)]
    khT2 = [T("khT%d" % i, [128, PW], BF16) for i in range(2)]
    tpk = [T("tpk%d" % i, [32, 4, 128], BF16, psum=True) for i in range(2)]
    aps = [T("aps%d" % i, [32, 32], F32, psum=True) for i in range(2)]
    ops_ = [T("ops%d" % i, [32, 128], F32, psum=True) for i in range(2)]
    ups = [T("ups%d" % i, [128, 128], F32, psum=True) for i in range(2)]
    asb = [T("asb%d" % i, [32, 32], BF16) for i in range(2)]
    st2 = [T("st%d" % i, [128, 128]) for i in range(2)]
    stb = [T("stb%d" % i, [128, 128], BF16) for i in range(2)]
    ost = [T("ost%d" % i, [32, 8, 128]) for i in range(2)]

    def v3(t):
        return t[:, 0:PW].rearrange("p (c j) -> p c j", j=32)

    fi = 0
    oi = 0
    for h in range(4):
        hg = 4 * hh + h
        for p in range(NP_):
            sl = slice(p * PW, (p + 1) * PW)
            P.dma("sp", qs[:, sl], ufm[hg * 128:(hg + 1) * 128, sl], writes=[qs.b])
        P.op("act", "activation", qs[:], qs[:], AF.Silu, reads=[qs.b], writes=[qs.b])
        for p in range(8):
            src = utm[p * 512:(p + 1) * 512, hg * 128:(hg + 1) * 128].rearrange("(c j) e -> j c e", j=32)
            P.dma("sp", vstg[:], src, writes=[vstg.b])
            P.op("pool", "tensor_copy", v32[:, p * 16:(p + 1) * 16, :], vstg[:], reads=[vstg.b], writes=[v32.b])
        for dr in range(2):
            frow = (1 + dr) * 1024 + hg * 128
            for p in range(NP_):
                sl = slice(p * PW, (p + 1) * PW)
                frt = fr[fi % 2]
                tf, tg, tk, tX, tD, khT = tf2[fi % 2], tg2[fi % 2], tk2[fi % 2], tX2[fi % 2], tD2[fi % 2], khT2[fi % 2]
                tE = tE4[(fi % 2) * 2:(fi % 2) * 2 + 2]
                fi += 1
                P.dma("sp", frt[:], ufm[frow:frow + 128, sl], writes=[frt.b])
                P.op("act", "activation", frt[:], frt[:], AF.Sigmoid, reads=[frt.b], writes=[frt.b])
                P.op("dve", "tensor_scalar", tf[:], frt[:], oml[:, h:h + 1], lb[:, h:h + 1], ALU.mult, ALU.add,
                     reads=[frt.b, oml.b, lb.b], writes=[tf.b])
                P.op("act", "activation", tg[:], tf[:], AF.Ln, reads=[tf.b], writes=[tg.b])
                P.op("pool", "tensor_scalar", tk[:], tf[:], -1.0, 1.0, ALU.mult, ALU.add, reads=[tf.b], writes=[tk.b])
                P.op("dve", "tensor_tensor_scan", tX[:], smask[:], tg[:], 0.0, ALU.mult, ALU.add,
                     reads=[smask.b, tg.b], writes=[tX.b])
                X3 = v3(tX)
                if dr == 1:
                    P.op("dve", "tensor_tensor", v3(tD), X3[:, :, 31:32].to_broadcast([128, 32, 32]), X3, ALU.subtract,
                         reads=[tX.b], writes=[tD.b])
                    P.op("dve", "tensor_tensor", tX[:], tD[:], tg[:], ALU.add, reads=[tD.b, tg.b], writes=[tX.b])
                edge = 31 if dr == 0 else 0
                P.op("act", "activation", ec[:, p * 32:(p + 1) * 32], X3[:, :, edge], AF.Exp, reads=[tX.b], writes=[ec.b])
                P.op("dve", "tensor_tensor", v3(tD), X3, X3[:, :, 16:17].to_broadcast([128, 32, 32]), ALU.subtract,
                     reads=[tX.b], writes=[tD.b])
                P.op("act", "activation", tE[0][:], tD[:], AF.Exp, reads=[tD.b], writes=[tE[0].b])
                P.op("act", "activation", tE[1][:], tD[:], AF.Exp, scale=-1.0, reads=[tD.b], writes=[tE[1].b])
                P.op("pool", "tensor_tensor", qtl[:, sl], qs[:, sl], tE[0][:], ALU.mult, reads=[qs.b, tE[0].b], writes=[qtl.b])
                P.op("dve", "tensor_tensor", ktl[:, sl], tk[:], tE[1][:], ALU.mult, reads=[tk.b, tE[1].b], writes=[ktl.b])
                P.op("act", "activation", tE[0][:], tX[:], AF.Exp, reads=[tX.b], writes=[tE[0].b])
                P.op("pool", "tensor_tensor", qht[:, sl], qs[:, sl], tE[0][:], ALU.mult, reads=[qs.b, tE[0].b], writes=[qht.b])
                P.op("dve", "tensor_tensor", v3(tD), X3[:, :, edge:edge + 1].to_broadcast([128, 32, 32]), X3, ALU.subtract,
                     reads=[tX.b], writes=[tD.b])
                P.op("act", "activation", tE[1][:], tD[:], AF.Exp, reads=[tD.b], writes=[tE[1].b])
                P.op("dve", "tensor_tensor", khT[:], tk[:], tE[1][:], ALU.mult, reads=[tk.b, tE[1].b], writes=[khT.b])
                for g8 in range(PW // 128):
                    tp = tpk[g8 % 2]
                    for q4 in range(4):
                        c = g8 * 4 + q4
                        P.op("pe", "transpose", tp[:, q4, :], khT[:, c * 32:(c + 1) * 32], ident[:],
                             reads=[khT.b, ident.b], writes=[tp.b])
                    c0 = p * 32 + g8 * 4
                    P.op("act", "copy", kh32[:, c0:c0 + 4, :], tp[:], reads=[tp.b], writes=[kh32.b])
            P.op("dve", "memset", st2[1][:], 0.0, writes=[st2[1].b])
            P.op("pool", "memset", stb[0][:], 0.0, writes=[stb[0].b])
            order = list(range(NCH)) if dr == 0 else list(range(NCH - 1, -1, -1))
            def front(n, c):
                cs = slice(c * 32, (c + 1) * 32)
                ap_, up_, as_ = aps[n % 2], ups[n % 2], asb[n % 2]
                P.op("pe", "matmul", ap_[:], ktl[:, cs], qtl[:, cs], start=True, stop=True, reads=[ktl.b, qtl.b], writes=[ap_.b])
                P.op("pe", "matmul", up_[:], kh32[:, c, :], v32[:, c, :], start=True, stop=True, reads=[kh32.b, v32.b], writes=[up_.b])
                P.op("dve", "tensor_tensor", as_[:], ap_[:], masks[:, dr, :], ALU.mult, reads=[ap_.b, masks.b], writes=[as_.b])
            front(0, order[0])
            for n, c in enumerate(order):
                cs = slice(c * 32, (c + 1) * 32)
                ap_, op_, up_, as_ = aps[n % 2], ops_[n % 2], ups[n % 2], asb[n % 2]
                sb_cur, sb_nxt = stb[n % 2], stb[(n + 1) % 2]
                if n + 1 < NCH:
                    front(n + 1, order[n + 1])
                P.op("pe", "matmul", op_[:], as_[:], v32[:, c, :], start=True, stop=False, reads=[as_.b, v32.b], writes=[op_.b])
                P.op("pe", "matmul", op_[:], qht[:, cs], sb_cur[:], start=False, stop=True, reads=[qht.b, sb_cur.b], writes=[op_.b])
                os_ = ost[oi % 2]
                slot = c % 8
                s_old, s_new = st2[(n + 1) % 2], st2[n % 2]
                P.op("dve", "scalar_tensor_tensor", s_new[:], s_old[:], ec[:, c:c + 1], up_[:], ALU.mult, ALU.add,
                     reads=[s_old.b, ec.b, up_.b], writes=[s_new.b])
                P.op("act", "copy", sb_nxt[:], s_new[:], reads=[s_new.b], writes=[sb_nxt.b])
                P.op("act", "copy", os_[:, slot, :], op_[:], reads=[op_.b], writes=[os_.b])
                if n % 8 == 7:
                    cb = (c // 8) * 8
                    dst = od_d[dr][cb * 32:(cb + 8) * 32, h * 128:(h + 1) * 128].rearrange("(c i) e -> i c e", i=32)
                    P.dma("sp", dst, os_[:], reads=[os_.b], writes=[od_b])
                    oi += 1
    fa = [T("fa%d" % i, [128, 512]) for i in range(2)]
    fb = [T("fb%d" % i, [128, 512]) for i in range(2)]
    fo = [T("fo%d" % i, [128, 512]) for i in range(2)]
    fss = [T("fss%d" % i, [128, 4]) for i in range(2)]
    fj = T("fj", [128, 128])
    for tt in range(NST):
        i = tt % 2
        sl = slice(tt * 128, (tt + 1) * 128)
        P.dma("sp", fa[i][:], od_d[0][sl], reads=[od_b], writes=[fa[i].b])
        P.dma("sp", fb[i][:], od_d[1][sl], reads=[od_b], writes=[fb[i].b])
        P.op("dve", "tensor_tensor", fa[i][:], fa[i][:], fb[i][:], ALU.add, reads=[fa[i].b, fb[i].b], writes=[fa[i].b])
        for h in range(4):
            P.op("act", "activation", fj[:], fa[i][:, h * 128:(h + 1) * 128], AF.Square, accum_out=fss[i][:, h:h + 1],
                 reads=[fa[i].b], writes=[fj.b, fss[i].b])
        P.op("act", "activation", fss[i][:], fss[i][:], AF.Sqrt, bias=EPS, scale=1.0 / 128, reads=[fss[i].b], writes=[fss[i].b])
        P.op("dve", "reciprocal", fss[i][:], fss[i][:], reads=[fss[i].b], writes=[fss[i].b])
        for h in range(4):
            hs = slice(h * 128, (h + 1) * 128)
            P.op("dve", "scalar_tensor_tensor", fo[i][:, hs], fa[i][:, hs], fss[i][:, h:h + 1], ng[:, hs], ALU.mult, ALU.mult,
                 reads=[fa[i].b, fss[i].b, ng.b], writes=[fo[i].b])
        P.dma("sp", o_d[sl, hh * 512:(hh + 1) * 512], fo[i][:], reads=[fo[i].b], writes=[o_b])


def hg_consts():
    sm = np.ones((128, 1024), np.float32)
    sm[:, ::32] = 0.0
    j = np.arange(32)[:, None]
    i = np.arange(32)[None, :]
    masks = np.stack([(j <= i), (j >= i)], 1).astype(np.float32)
    return sm, masks


def emit_at(P, A, hh, io):
    A.reset()
    T = A.T
    utm, oT_d = io["utm"], io["oT"]
    oT_b = Buf("oT_d")
    ident = T("ident", [128, 128], BF16)
    gain = T("gain", [128, 12, 128])
    ones = T("ones", [128, 128], BF16)
    P.dma("sp", ident[:], io["ident"], writes=[ident.b])
    P.dma("sp", gain[:], io["gain"], writes=[gain.b])
    P.op("pool", "memset", ones[:], 1.0, writes=[ones.b])
    qT = T("qT", [128, 8, S], BF16)
    kT = T("kT", [128, 4, S], BF16)
    vb = T("vb", [128, NST, 4, 128], BF16)
    qT_b = [Buf("qT%d" % i) for i in range(NST)]
    kT_b = [Buf("kT%d" % i) for i in range(NST)]
    vb_b = [Buf("vb%d" % i) for i in range(NST)]
    qk = [T("qk%d" % i, [128, 12, 128]) for i in range(2)]
    vt = [T("vt%d" % i, [128, 4, 128]) for i in range(2)]
    cs = [T("cs%d" % i, [128, 2, 2, 32]) for i in range(2)]
    junk = T("junk", [128, 128])
    ssq = [T("ssq%d" % i, [128, 12]) for i in range(2)]
    qn = [T("qn0", [128, 12, 128])] * 2
    ta = [T("ta0", [128, 12, 2, 32])] * 2
    tb = [T("tb0", [128, 12, 2, 32])] * 2
    tc = [T("tc0", [128, 12, 2, 32])] * 2
    td = [T("td0", [128, 12, 2, 32])] * 2
    qr = [T("qr%d" % i, [128, 12, 128], BF16) for i in range(2)]
    tpq = T("tpq", [128, 8, 128], BF16, psum=True)
    tpk = T("tpk", [128, 4, 128], BF16, psum=True)
    for tt in range(NST):
        i = tt % 2
        sl = slice(tt * 128, (tt + 1) * 128)
        qkt, vtt, cst, sst, qnt, qrt = qk[i], vt[i], cs[i], ssq[i], qn[i], qr[i]
        P.dma("sp", qkt[:, 0:8, :], utm[sl, hh * 1024:(hh + 1) * 1024].rearrange("t (h e) -> t h e", h=8), writes=[qkt.b])
        P.dma("sp", qkt[:, 8:12, :], utm[sl, 2048 + hh * 512:2048 + (hh + 1) * 512].rearrange("t (h e) -> t h e", h=4), writes=[qkt.b])
        P.dma("sp", vtt[:], utm[sl, 3072 + hh * 512:3072 + (hh + 1) * 512].rearrange("t (h e) -> t h e", h=4), writes=[vtt.b])
        P.dma("sp", cst[:, 0], io["cos"][sl], writes=[cst.b])
        P.dma("sp", cst[:, 1], io["sin"][sl], writes=[cst.b])
        P.op("pool", "tensor_copy", vb[:, tt], vtt[:], reads=[vtt.b], writes=[vb_b[tt]])
        for h in range(12):
            P.op("act", "activation", junk[:], qkt[:, h, :], AF.Square, accum_out=sst[:, h:h + 1], reads=[qkt.b], writes=[junk.b, sst.b])
        P.op("act", "activation", sst[:], sst[:], AF.Sqrt, bias=EPS, scale=1.0 / 128, reads=[sst.b], writes=[sst.b])
        P.op("dve", "reciprocal", sst[:], sst[:], reads=[sst.b], writes=[sst.b])
        for h in range(12):
            P.op("dve", "scalar_tensor_tensor", qnt[:, h, :], qkt[:, h, :], sst[:, h:h + 1], gain[:, h, :], ALU.mult, ALU.mult,
                 reads=[qkt.b, sst.b, gain.b], writes=[qnt.b])
        xv = qnt[:].rearrange("p h (a two f) -> p h a two f", a=2, two=2)
        ov = qrt[:].rearrange("p h (a two f) -> p h a two f", a=2, two=2)
        x1, x2 = xv[:, :, :, 0, :], xv[:, :, :, 1, :]
        cb = cst[:, 0].unsqueeze(1).to_broadcast([128, 12, 2, 32])
        sb = cst[:, 1].unsqueeze(1).to_broadcast([128, 12, 2, 32])
        a_, b_, c_, d_ = ta[i], tb[i], tc[i], td[i]
        P.op("dve", "tensor_tensor", a_[:], x1, cb, ALU.mult, reads=[qnt.b, cst.b], writes=[a_.b])
        P.op("pool", "tensor_tensor", b_[:], x2, sb, ALU.mult, reads=[qnt.b, cst.b], writes=[b_.b])
        P.op("dve", "tensor_tensor", c_[:], x2, cb, ALU.mult, reads=[qnt.b, cst.b], writes=[c_.b])
        P.op("pool", "tensor_tensor", d_[:], x1, sb, ALU.mult, reads=[qnt.b, cst.b], writes=[d_.b])
        P.op("dve", "tensor_tensor", ov[:, :, :, 0, :], a_[:], b_[:], ALU.subtract, reads=[a_.b, b_.b], writes=[qrt.b])
        P.op("pool", "tensor_tensor", ov[:, :, :, 1, :], c_[:], d_[:], ALU.add, reads=[c_.b, d_.b], writes=[qrt.b])
        for h in range(8):
            P.op("pe", "transpose", tpq[:, h, :], qrt[:, h, :], ident[:], reads=[qrt.b, ident.b], writes=[tpq.b])
        for h in range(4):
            P.op("pe", "transpose", tpk[:, h, :], qrt[:, 8 + h, :], ident[:], reads=[qrt.b, ident.b], writes=[tpk.b])
        P.op("act", "copy", qT[:, :, sl], tpq[:], reads=[tpq.b], writes=[qT_b[tt]])
        P.op("act", "copy", kT[:, :, sl], tpk[:], reads=[tpk.b], writes=[kT_b[tt]])
    sps = [T("sps%d" % i, [128, 512], F32, psum=True) for i in range(2)]
    for old_tv in (tpq, tpk):
        tv = TV(old_tv.bank[:, :], "sps_x", True)
        tv.b = old_tv.b
        sps.append(tv)
    NSP = len(sps)
    ops_ = [T("ops%d" % i, [128, 512], F32, psum=True) for i in range(2)]
    dps = [T("dps%d" % i, [128, 512], F32, psum=True) for i in range(2)]
    NPT = 6
    pT = [T("pT%d" % i, [128, 512], BF16) for i in range(NPT)]
    psum4 = [T("psum4_%d" % i, [128, 512], BF16) for i in range(2)]
    rec = [T("rec%d" % i, [128, 512]) for i in range(2)]
    osb = [T("osb%d" % i, [128, 512]) for i in range(2)]
    scale = 128.0 ** -0.5
    it = 0
    for h in range(8):
        kv = h // 2
        for qb in range(S // 512):
            qsl = slice(qb * 512, (qb + 1) * 512)
            q_reads = [qT_b[qb * 4 + j] for j in range(4)]
            op_, dp_ = ops_[it % 2], dps[it % 2]

            def mm1(kt):
                sp_ = sps[kt % NSP]
                P.op("pe", "matmul", sp_[:], kT[:, kv, kt * 128:(kt + 1) * 128], qT[:, h, qsl], start=True, stop=True,
                     reads=[kT_b[kt]] + q_reads, writes=[sp_.b])

            def rest(kt):
                sp_ = sps[kt % NSP]
                p_ = pT[kt % NPT]
                P.op("act", "activation", p_[:], sp_[:], AF.Exp, scale=scale, reads=[sp_.b], writes=[p_.b])
                P.op("pe", "matmul", op_[:], vb[:, kt, kv, :], p_[:], start=(kt == 0), stop=(kt == NST - 1),
                     reads=[vb_b[kt], p_.b], writes=[op_.b])
                if kt % 4 == 1:
                    ps_ = psum4[(kt // 4) % 2]
                    P.op("dve", "tensor_tensor", ps_[:], pT[(kt - 1) % NPT][:], p_[:], ALU.add,
                         reads=[pT[(kt - 1) % NPT].b, p_.b], writes=[ps_.b])
                elif kt % 4 in (2, 3):
                    ps_ = psum4[(kt // 4) % 2]
                    P.op("dve", "tensor_tensor", ps_[:], ps_[:], p_[:], ALU.add, reads=[ps_.b, p_.b], writes=[ps_.b])
                G = (kt - 5) // 4
                if kt >= 5 and (kt - 5) % 4 == 0:
                    pg = psum4[G % 2]
                    P.op("pe", "matmul", dp_[:], ones[:], pg[:], start=(G == 0), stop=False,
                         reads=[ones.b, pg.b], writes=[dp_.b])
                if kt == NST - 1:
                    pg = psum4[(NST // 4 - 1) % 2]
                    P.op("pe", "matmul", dp_[:], ones[:], pg[:], start=False, stop=True,
                         reads=[ones.b, pg.b], writes=[dp_.b])
            mm1(0)
            mm1(1)
            for kt in range(NST):
                if kt + 2 < NST:
                    mm1(kt + 2)
                rest(kt)
            r_, o_ = rec[it % 2], osb[it % 2]
            P.op("dve", "reciprocal", r_[:], dp_[:], reads=[dp_.b], writes=[r_.b])
            P.op("dve", "tensor_tensor", o_[:], op_[:], r_[:], ALU.mult, reads=[op_.b, r_.b], writes=[o_.b])
            hgl = hh * 8 + h
            P.dma("sp", oT_d[hgl * 128:(hgl + 1) * 128, qsl], o_[:], reads=[o_.b], writes=[oT_b])
            it += 1


def rope_tables():
    row = np.repeat(np.arange(S // 64), 64).astype(np.float32)
    col = (np.arange(S) % 64).astype(np.float32)
    inv = (np.float32(10000.0) ** (-np.arange(0, 64, 2, dtype=np.float32) / np.float32(64))).astype(np.float32)
    ang = np.stack([row[:, None] * inv, col[:, None] * inv], 1).astype(np.float32)
    return np.cos(ang).astype(np.float32), np.sin(ang).astype(np.float32)


DL_PAIRS = ((128, 1), (512, 4), (2048, 16))
NEG = -30000.0


def dl_geom(d):
    Ls = S // d
    nt = Ls // 128
    return Ls, nt, d * (Ls + 128), d * (nt + 1)


def emit_dl(P, A, hh, io):
    A.reset()
    T = A.T
    ufm, utm, nd_d, o_d = io["ufm"], io["utm"], io["nd"], io["o"]
    nd_b = Buf("nd_d")
    o_b = Buf("o_d")
    bias = T("bias", [128, 24, 256])
    P.dma("sp", bias[:], io["dl_bias"][hh], writes=[bias.b])
    KMAX = 6144
    VMAX = 48
    qst = T("qst", [64, S])
    kst = T("kst", [64, S])
    qb_ = [T("qb%d" % i, [64, S], BF16) for i in range(2)]
    kb_ = [T("kb%d" % i, [64, KMAX], BF16) for i in range(2)]
    vall = T("vall", [128, VMAX, 4, 65], BF16)
    vstage = [T("vstage%d" % i, [128, 12, 256]) for i in range(2)]
    nds = T("nds", [128, 32, 4, 65])
    NPB = 4
    sps = [T("sps%d" % i, [128, 2, 128], F32, psum=True) for i in range(NPB)]
    ops_ = [T("ops%d" % i, [128, 65], F32, psum=True) for i in range(NPB)]
    tmp = [T("tmp%d" % i, [128, 256]) for i in range(NPB)]
    pb = [T("pb%d" % i, [128, 2, 128], BF16) for i in range(NPB)]
    batches = [(g, d, hb) for g, (_, d) in enumerate(DL_PAIRS) for hb in range(2)]
    heads = [(bi, h4) for bi in range(len(batches)) for h4 in range(4)]
    vsi = [0]

    def prep_v(bi):
        g, d, hb = batches[bi]
        Ls, nt, klen, vt_n = dl_geom(d)
        c0 = g * 1024 + (8 * hh + 4 * hb) * 64
        vsrc = utm[:, c0:c0 + 256].rearrange("(m r) e -> r m e", r=d)
        v5 = vall[:, 0:vt_n].rearrange("p (r i) h c -> p r i h c", r=d)
        P.op("dve", "memset", v5[0:64, :, 0], 0.0, writes=[vall.b])
        P.op("dve", "memset", v5[64:128, :, nt], 0.0, writes=[vall.b])
        if nt > 1:
            P.op("dve", "memset", v5[:, :, 1:nt, :, 64:65], 1.0, writes=[vall.b])
        P.op("dve", "memset", v5[64:128, :, 0, :, 64:65], 1.0, writes=[vall.b])
        P.op("dve", "memset", v5[0:64, :, nt, :, 64:65], 1.0, writes=[vall.b])

        def piece(p0, p1, dst, src, ntile):
            st = vstage[vsi[0] % 2]
            vsi[0] += 1
            P.dma("sp", st[p0:p1, 0:ntile, :], src, writes=[st.b])
            P.op("act", "copy", dst, st[p0:p1, 0:ntile, :].rearrange("p n (h e) -> p n h e", h=4), reads=[st.b], writes=[vall.b])
        for r in range(d):
            base = r * (nt + 1)
            for i0 in range(1, nt, 12):
                i1 = min(nt, i0 + 12)
                src = vsrc[r][64 + 128 * (i0 - 1):64 + 128 * (i1 - 1), :].rearrange("(i k) e -> k i e", k=128)
                piece(0, 128, vall[:, base + i0:base + i1, :, 0:64], src, i1 - i0)
        for r0 in range(0, d, 8):
            r1 = min(d, r0 + 8)
            piece(64, 128, v5[64:128, r0:r1, 0, :, 0:64], vsrc[r0:r1, 0:64, :].rearrange("r k e -> k r e"), r1 - r0)
            piece(0, 64, v5[0:64, r0:r1, nt, :, 0:64], vsrc[r0:r1, Ls - 64:Ls, :].rearrange("r k e -> k r e"), r1 - r0)

    def prep_qk(hi):
        bi, h4 = heads[hi]
        g, d, hb = batches[bi]
        Ls, nt, klen, vt_n = dl_geom(d)
        i = hi % 2
        hgl = 8 * hh + 4 * hb + h4
        qrow = (g * 2) * 1024 + hgl * 64
        krow = (g * 2 + 1) * 1024 + hgl * 64
        P.dma("sp", qst[:], ufm[qrow:qrow + 64, :], writes=[qst.b])
        P.dma("sp", kst[:], ufm[krow:krow + 64, :], writes=[kst.b])
        P.op("act", "copy", qb_[i][:].rearrange("p (r m) -> p r m", r=d), qst[:].rearrange("p (m r) -> p r m", r=d),
             reads=[qst.b], writes=[qb_[i].b])
        kb3 = kb_[i][:, 0:klen].rearrange("p (r m) -> p r m", r=d)
        P.op("dve", "memset", kb3[:, :, 0:64], 0.0, writes=[kb_[i].b])
        P.op("dve", "memset", kb3[:, :, 64 + Ls:128 + Ls], 0.0, writes=[kb_[i].b])
        P.op("act", "copy", kb3[:, :, 64:64 + Ls], kst[:].rearrange("p (m r) -> p r m", r=d), reads=[kst.b], writes=[kb_[i].b])

    ucnt = [0]

    def compute(hi):
        bi, h4 = heads[hi]
        g, d, hb = batches[bi]
        Ls, nt, klen, vt_n = dl_geom(d)
        i = hi % 2
        hloc = 4 * hb + h4
        u0 = ucnt[0]
        ucnt[0] += 32

        def geo(j):
            r, jj = divmod(j, nt)
            return r * Ls + jj * 128, r * (Ls + 128) + jj * 128, r * (nt + 1) + jj

        def qk(j):
            n0, kbase, _ = geo(j)
            sp_ = sps[(u0 + j) % NPB]
            for c in range(2):
                P.op("pe", "matmul", sp_[:, c, :], kb_[i][:, kbase + c * 128:kbase + (c + 1) * 128],
                     qb_[i][:, n0:n0 + 128], start=True, stop=True, reads=[kb_[i].b, qb_[i].b], writes=[sp_.b])
            tm_, p_ = tmp[(u0 + j) % NPB], pb[(u0 + j) % NPB]
            P.op("dve", "scalar_tensor_tensor", tm_[:], sp_[:].rearrange("p c q -> p (c q)"), 0.125,
                 bias[:, g * 8 + hloc, :], ALU.mult, ALU.add, reads=[sp_.b, bias.b], writes=[tm_.b])
            P.op("act", "activation", p_[:].rearrange("p c q -> p (c q)"), tm_[:], AF.Exp, reads=[tm_.b], writes=[p_.b])

        def pv(j):
            _, _, vbase = geo(j)
            op_, p_ = ops_[(u0 + j) % NPB], pb[(u0 + j) % NPB]
            for c in range(2):
                P.op("pe", "matmul", op_[:], p_[:, c, :], vall[:, vbase + c, h4, :], start=(c == 0), stop=(c == 1),
                     reads=[p_.b, vall.b], writes=[op_.b])
            P.op("dve", "tensor_copy", nds[:, j, h4, :], op_[:], reads=[op_.b], writes=[nds.b])
        AHEAD = 2
        for j in range(AHEAD):
            qk(j)
        for j in range(32):
            if j + AHEAD < 32:
                qk(j + AHEAD)
            pv(j)

    def store(bi):
        g, d, hb = batches[bi]
        Ls, nt, klen, vt_n = dl_geom(d)
        ndv = nd_d.rearrange("(m r) g h c -> r m g h c", r=d)
        for r in range(d):
            dst = ndv[r][:, g, 4 * hb:4 * hb + 4, :].rearrange("(jj q) h c -> q jj h c", q=128)
            for j0 in range(0, nt, 8):
                j1 = min(nt, j0 + 8)
                P.dma("pool", dst[:, j0:j1], nds[:, r * nt + j0:r * nt + j1], reads=[nds.b], writes=[nd_b])

    prep_qk(0)
    for hi in range(len(heads)):
        bi, h4 = heads[hi]
        if h4 == 0:
            prep_v(bi)
        if hi + 1 < len(heads):
            prep_qk(hi + 1)
        compute(hi)
        if h4 == 3:
            store(bi)
    mt = [T("mt%d" % i, [128, 3, 8, 65]) for i in range(2)]
    ms = [T("ms%d" % i, [128, 8, 65]) for i in range(2)]
    mr = [T("mr%d" % i, [128, 8]) for i in range(2)]
    mo = [T("mo%d" % i, [128, 8, 64]) for i in range(2)]
    for tt in range(NST):
        i = tt % 2
        sl = slice(tt * 128, (tt + 1) * 128)
        P.dma("sp", mt[i][:], nd_d[sl], reads=[nd_b], writes=[mt[i].b])
        P.op("dve", "tensor_tensor", ms[i][:], mt[i][:, 0], mt[i][:, 1], ALU.add, reads=[mt[i].b], writes=[ms[i].b])
        P.op("dve", "tensor_tensor", ms[i][:], ms[i][:], mt[i][:, 2], ALU.add, reads=[mt[i].b, ms[i].b], writes=[ms[i].b])
        P.op("dve", "reciprocal", mr[i][:], ms[i][:, :, 64], reads=[ms[i].b], writes=[mr[i].b])
        P.op("dve", "tensor_tensor", mo[i][:], ms[i][:, :, 0:64], mr[i][:].unsqueeze(2).to_broadcast([128, 8, 64]),
             ALU.mult, reads=[ms[i].b, mr[i].b], writes=[mo[i].b])
        P.dma("sp", o_d[sl, hh * 512:(hh + 1) * 512].rearrange("t (h e) -> t h e", h=8), mo[i][:], reads=[mo[i].b], writes=[o_b])


def t5_bucket_np(rel):
    half, exact = 16, 8
    n = np.abs(rel)
    large = exact + (np.log(np.maximum(n, 1).astype(np.float32) / np.float32(exact))
                     / np.float32(np.log(1024 / exact)) * np.float32(half - exact)).astype(np.int32)
    large = np.minimum(large, half - 1)
    return np.where(rel > 0, half, 0) + np.where(n < exact, n, large)


def dl_bias_tables(rel_bias, heads):
    kk = np.arange(128)[:, None]
    qq = np.arange(128)[None, :]
    out = np.empty((128, 24, 256), np.float32)
    for g, (_, d) in enumerate(DL_PAIRS):
        bA = t5_bucket_np((kk - 64 - qq) * d)
        bB = t5_bucket_np((64 + kk - qq) * d)
        for hi, h in enumerate(heads):
            out[:, g * 8 + hi, 0:128] = np.where(kk >= qq, rel_bias[bA, h], NEG)
            out[:, g * 8 + hi, 128:256] = np.where(kk <= qq, rel_bias[bB, h], NEG)
    return out


def build_fused():
    P = Prog()
    nc = P.nc
    A = Arena(P, kib=207)
    dr = P.dram

    def scratch(name, shape):
        return nc.dram_tensor(name, list(shape), F32, kind="Internal").ap()

    def db(ap):
        return {"ap": ap, "b": Buf("d")}
    x_in = dr("x", [S, D])
    out_d = dr("out", [S, D], kind="ExternalOutput")
    ident = dr("ident", [128, 128], BF16)
    io = {
        "ident": ident, "identf": dr("identf", [128, 128]),
        "convw": dr("convw", [2, 128, 2, 6, 7]), "convb": dr("convb", [2, 128, 2, 6]),
        "dtb_rep": dr("dtb_rep", [2, 128, 2, 2, 8]), "alog_rep": dr("alog_rep", [2, 128, 2, 2, 8]),
        "d_rep": dr("d_rep", [2, 128, 2, 8]), "ssd_ng_rep": dr("ssd_ng_rep", [2, 128, 2, 512]),
        "tmat": dr("tmat", [128, 2, 128]), "negm": dr("negm", [128, 2, 4, 128]),
        "lbT": dr("lbT", [2, 128, 4, 4]), "hg_ng_rep": dr("hg_ng_rep", [2, 128, 512]),
        "smask": dr("smask", [128, 1024]), "masks": dr("masks", [32, 2, 32]),
        "cos": dr("cos", [S, 2, 32]), "sin": dr("sin", [S, 2, 32]), "gain": dr("gain", [128, 12, 128]),
        "dl_bias": dr("dl_bias", [2, 128, 24, 256]),
    }
    NIN = (5184, 5120, 6144, 10240)
    NFM = (3072, 3072, 2048, 6144)
    WOUT = (2048, 1024, 2048, 1024)
    g_rep = [dr("g_rep%d" % l, [128, D]) for l in range(4)]
    g_fin = dr("g_fin", [128, D])
    w_in = [dr("w_in%d" % l, [D, NIN[l]]) for l in range(4)]
    w_out = [dr("w_out%d" % l, [WOUT[l], D]) for l in range(4)]
    ufm = [scratch("ufm%d" % i, [6144, S]) for i in range(2)]
    utm = [scratch("utm%d" % i, [S, 4096]) for i in range(2)]
    o_tm = scratch("o_tm", [S, 2048])
    oT_fm = scratch("oT_fm", [2048, S])
    xs = [scratch("xs%d" % i, [S, D]) for i in range(2)]
    od = scratch("od", [2, S, 512])
    nd = scratch("nd", [S, 3, 8, 65])

    def lin(layer, x_src, c, x_dst, final=False):
        for t0 in (0, TT):
            a = None
            if layer < 4:
                a = {"g_rep": g_rep[layer], "w_in": w_in[layer], "n_fm": NFM[layer], "n_tm": NIN[layer] - NFM[layer],
                     "ufm": ufm[layer % 2], "utm": utm[layer % 2]}
            cc = None
            if c is not None:
                cc = dict(c)
                cc["o_b"] = Buf("o")
                cc["gate_b"] = Buf("g")
                cc["xout"] = db(x_dst) if x_dst is not None else None
            fin = {"g_rep": g_fin, "out": out_d} if final else None
            emit_lin(P, A, t0, ident, db(x_src), c=cc, a=a, fin=fin)

    lin(0, x_in, None, None)
    for hh in range(2):
        emit_ssd(P, A, hh, dict(io, ufm=ufm[0], utm=utm[0], o=o_tm))
    lin(1, x_in, {"mode": "tm", "W": 2048, "o": o_tm, "gate": None, "w_out": w_out[0]}, xs[0])
    for hh in range(2):
        emit_hg(P, A, hh, dict(io, ufm=ufm[1], utm=utm[1], od=od, o=o_tm))
    lin(2, xs[0], {"mode": "tm", "W": 1024, "o": o_tm, "gate": utm[1][:, 1024:2048], "w_out": w_out[1]}, xs[1])
    for hh in range(2):
        emit_at(P, A, hh, dict(io, utm=utm[0], oT=oT_fm))
    lin(3, xs[1], {"mode": "fm", "W": 2048, "o": oT_fm, "gate": ufm[0][0:2048, :], "w_out": w_out[2]}, xs[0])
    for hh in range(2):
        emit_dl(P, A, hh, dict(io, ufm=ufm[1], utm=utm[1], nd=nd, o=o_tm))
    lin(4, xs[0], {"mode": "tm", "W": 1024, "o": o_tm, "gate": utm[1][:, 3072:4096], "w_out": w_out[3]}, None, final=True)
    print("fused program ops:", P.n_ops, dict(P.ecnt))
    return P.emit()


def fused_inputs(x, norm_g, final_g, rel_bias, hgrn_lb,
                 ssd_w_in, ssd_conv_w, ssd_conv_b, ssd_dt_bias, ssd_a_log, ssd_d, ssd_norm_g, ssd_w_out,
                 hg_w_in, hg_norm_g, hg_w_out, at_w_in, at_q_norm_g, at_k_norm_g, at_w_out, dl_w_in, dl_w_out):
    f = lambda a: np.ascontiguousarray(np.asarray(a, dtype=np.float32))
    rep = lambda v: np.ascontiguousarray(np.broadcast_to(f(v)[None], (128,) + tuple(np.shape(v))))
    tm, nm = ssd_consts()
    sm, masks = hg_consts()
    cos, sin = rope_tables()
    w0 = f(ssd_w_in)[0]
    w2 = f(at_w_in)[0]
    w3 = f(dl_w_in)[0]
    fm3 = np.concatenate([np.arange(g * 3072 + s * 1024, g * 3072 + (s + 1) * 1024) for g in range(3) for s in range(2)])
    tm3 = np.concatenate([np.arange(g * 3072 + 2048, g * 3072 + 3072) for g in range(3)] + [np.arange(9216, 10240)])
    small = [ssd_small(hh, f(ssd_conv_w)[0], f(ssd_conv_b)[0], f(ssd_dt_bias)[0], f(ssd_a_log)[0], f(ssd_d)[0], f(ssd_norm_g)[0])
             for hh in range(2)]
    lb4 = f(hgrn_lb).reshape(4, 8, 128)
    gain = np.concatenate([np.broadcast_to(f(at_q_norm_g)[0][None, None], (128, 8, 128)),
                           np.broadcast_to(f(at_k_norm_g)[0][None, None], (128, 4, 128))], 1)
    common = {
        "ident": bf16_np(np.eye(128)), "identf": np.eye(128, dtype=np.float32),
        "convw": np.stack([s_[0] for s_ in small]), "convb": np.stack([s_[1] for s_ in small]),
        "dtb_rep": np.stack([s_[2] for s_ in small]), "alog_rep": np.stack([s_[3] for s_ in small]),
        "d_rep": np.stack([s_[4] for s_ in small]), "ssd_ng_rep": np.stack([s_[5] for s_ in small]),
        "tmat": tm, "negm": nm,
        "lbT": np.stack([np.ascontiguousarray(lb4[:, 4 * hh:4 * hh + 4].transpose(2, 0, 1)) for hh in range(2)]),
        "hg_ng_rep": np.stack([rep(f(hg_norm_g)[0][hh * 512:(hh + 1) * 512]) for hh in range(2)]),
        "smask": sm, "masks": masks, "cos": cos, "sin": sin, "gain": np.ascontiguousarray(gain, dtype=np.float32),
        "dl_bias": np.stack([dl_bias_tables(f(rel_bias), list(range(8 * hh, 8 * hh + 8))) for hh in range(2)]),
        "g_rep0": rep(f(norm_g)[0]), "g_rep1": rep(f(norm_g)[1]), "g_rep2": rep(f(norm_g)[2]), "g_rep3": rep(f(norm_g)[3]),
        "g_fin": rep(f(final_g)),
        "w_in0": np.ascontiguousarray(np.concatenate([w0[:, 2048:5120], w0[:, 0:2048], w0[:, 5120:5184]], 1)),
        "w_in1": f(hg_w_in)[0],
        "w_in2": np.ascontiguousarray(np.concatenate([w2[:, 4096:6144], w2[:, 0:4096]], 1)),
        "w_in3": np.ascontiguousarray(np.concatenate([w3[:, fm3], w3[:, tm3]], 1)),
        "w_out0": f(ssd_w_out)[0], "w_out1": f(hg_w_out)[0], "w_out2": f(at_w_out)[0], "w_out3": f(dl_w_out)[0],
    }
    x = f(x)
    return [dict(common, x=x[core % x.shape[0]]) for core in range(8)]


def kernel(**inputs):
    B = np.asarray(inputs["x"]).shape[0]
    maps = fused_inputs(**inputs)
    nc = build_fused()
    res = run_bass_kernel_spmd(nc, maps, core_ids=list(range(8)))
    return np.stack([res.results[b]["out"] for b in range(B)], 0)
```
